# Optimizing a Trainium2 kernel written in Bass

```python
import jax
import jax.numpy as jnp
from jax import lax
import numpy as np


D_MODEL = 2048
BATCH = 4
SEQ = 4096
DEPTH = 1

GRID_W = 64
CTX_LEN = 256
EPS = 1e-6

ATT_HEADS = 16
ATT_KV_HEADS = 4
ATT_GROUP = ATT_HEADS // ATT_KV_HEADS
ATT_HEAD_DIM = 128
ATT_WIDTH = ATT_HEADS * ATT_HEAD_DIM
ATT_KV_WIDTH = ATT_KV_HEADS * ATT_HEAD_DIM
WINDOW = 128
Q_BLOCK = 128
ROPE_BASE = 10000.0

HG_HEADS = 16
HG_EXPAND = 128
HG_HEAD_V = 128
HG_FDIM = HG_HEADS * HG_EXPAND
HG_VDIM = HG_HEADS * HG_HEAD_V
CHUNK = 64
LB_REST_INIT = 1.5

N_BRANCH = 2

OFF_ATT_Q = 0
OFF_ATT_K = OFF_ATT_Q + ATT_WIDTH
OFF_ATT_V = OFF_ATT_K + ATT_KV_WIDTH
OFF_ATT_G = OFF_ATT_V + ATT_KV_WIDTH
OFF_HG_Q = OFF_ATT_G + ATT_WIDTH
OFF_HG_FF = OFF_HG_Q + HG_FDIM
OFF_HG_FB = OFF_HG_FF + HG_FDIM
OFF_HG_I = OFF_HG_FB + HG_FDIM
OFF_HG_G = OFF_HG_I + HG_VDIM
OFF_MERGE = OFF_HG_G + HG_VDIM
IN_COLS = OFF_MERGE + N_BRANCH * D_MODEL

kernel_name = 'hybrid_hgrn2_swa_sink_block'


def rmsnorm(x, gain):
    xf = x.astype(jnp.float32)
    y = xf * lax.rsqrt(jnp.mean(xf * xf, axis=-1, keepdims=True) + EPS)
    return (y * gain.astype(jnp.float32)).astype(x.dtype)


def split_columns(p):
    return (p[..., OFF_ATT_Q:OFF_ATT_K], p[..., OFF_ATT_K:OFF_ATT_V], p[..., OFF_ATT_V:OFF_ATT_G],
            p[..., OFF_ATT_G:OFF_HG_Q], p[..., OFF_HG_Q:OFF_HG_FF], p[..., OFF_HG_FF:OFF_HG_FB],
            p[..., OFF_HG_FB:OFF_HG_I], p[..., OFF_HG_I:OFF_HG_G], p[..., OFF_HG_G:OFF_MERGE],
            p[..., OFF_MERGE:OFF_MERGE + D_MODEL], p[..., OFF_MERGE + D_MODEL:IN_COLS])


def axial_rope_tables(rows, dtype):
    row = jnp.repeat(jnp.arange(rows, dtype=jnp.float32), GRID_W)
    col = jnp.tile(jnp.arange(GRID_W, dtype=jnp.float32), rows)
    half = ATT_HEAD_DIM // 2
    inv_freq = 1.0 / (ROPE_BASE ** (jnp.arange(0, half, 2, dtype=jnp.float32) / half))
    ang_r = row[:, None] * inv_freq[None, :]
    ang_c = col[:, None] * inv_freq[None, :]
    ang = jnp.concatenate([ang_r, ang_r, ang_c, ang_c], axis=-1)
    return jnp.cos(ang).astype(dtype), jnp.sin(ang).astype(dtype)


def apply_axial_rope(x, cos, sin):
    x1, x2, x3, x4 = jnp.split(x, 4, axis=-1)
    rot = jnp.concatenate([-x2, x1, -x4, x3], axis=-1)
    return x * cos[:, None, :] + rot * sin[:, None, :]


def sink_softmax_av(s, v, sink):
    sk = sink.astype(jnp.float32).reshape(ATT_KV_HEADS, ATT_GROUP)[None, :, :, None]
    m = jnp.maximum(jnp.max(s, axis=-1), sk)
    p = jnp.exp(s - m[..., None])
    den = jnp.sum(p, axis=-1) + jnp.exp(sk - m)
    p = (p / den[..., None]).astype(v.dtype)
    return jnp.einsum('bkgqn,bnkd->bqkgd', p, v)


def window_attention(q, k, v, k_ctx, v_ctx, sink):
    B, n = q.shape[0], q.shape[1]
    nb = n // Q_BLOCK
    span = Q_BLOCK + 2 * WINDOW
    qb = q.reshape(B, nb, Q_BLOCK, ATT_KV_HEADS, ATT_GROUP, ATT_HEAD_DIM).transpose(1, 0, 2, 3, 4, 5)
    pad = ((0, 0), (WINDOW, WINDOW), (0, 0), (0, 0))
    kp = jnp.pad(k, pad)
    vp = jnp.pad(v, pad)

    def block(args):
        qn, start = args
        kb = lax.dynamic_slice_in_dim(kp, start, span, axis=1)
        vb = lax.dynamic_slice_in_dim(vp, start, span, axis=1)
        qi = start + jnp.arange(Q_BLOCK)
        kj = start - WINDOW + jnp.arange(span)
        valid = (jnp.abs(qi[:, None] - kj[None, :]) <= WINDOW) & (kj >= 0)[None, :] & (kj < n)[None, :]
        s_lat = jnp.einsum('bqkgd,bskd->bkgqs', qn, kb).astype(jnp.float32)
        s_lat = jnp.where(valid, s_lat, -jnp.inf)
        s_ctx = jnp.einsum('bqkgd,bskd->bkgqs', qn, k_ctx).astype(jnp.float32)
        s = jnp.concatenate([s_lat, s_ctx], axis=-1)
        return sink_softmax_av(s, jnp.concatenate([vb, v_ctx], axis=1), sink)

    out = lax.map(block, (qb, jnp.arange(nb) * Q_BLOCK))
    return out.transpose(1, 0, 2, 3, 4, 5).reshape(B, n, ATT_WIDTH)


def context_attention(q_c, k_c, v_c, sink):
    B, L = q_c.shape[0], q_c.shape[1]
    qg = q_c.reshape(B, L, ATT_KV_HEADS, ATT_GROUP, ATT_HEAD_DIM)
    s = jnp.einsum('bqkgd,bskd->bkgqs', qg, k_c).astype(jnp.float32)
    return sink_softmax_av(s, v_c, sink).reshape(B, L, ATT_WIDTH)


def to_heads(a):
    B, T, _ = a.shape
    return a.reshape(B, T, HG_HEADS, -1).transpose(0, 2, 1, 3)


def hgrn_forget(z, lb):
    lbf = lb[None, None, :]
    f = lbf + (1.0 - lbf) * jax.nn.sigmoid(z.astype(jnp.float32))
    return to_heads(jnp.log(f)), to_heads(1.0 - f)


def hgrn2_scan(q, k, v, logf, s0):
    B, H, T, dk = q.shape
    dv = v.shape[-1]
    nc = T // CHUNK

    def chunks(a):
        return jnp.moveaxis(a.astype(jnp.float32).reshape(B, H, nc, CHUNK, a.shape[-1]), 2, 0)

    causal = jnp.tril(jnp.ones((CHUNK, CHUNK), dtype=bool))

    def step(S, inp):
        qc, kc, vc, gc = inp
        G = jnp.cumsum(gc, axis=2)
        G_end = G[:, :, -1:, :]
        o_inter = jnp.einsum('bhtk,bhkv->bhtv', qc * jnp.exp(G), S)
        diff = G[:, :, :, None, :] - G[:, :, None, :, :]
        decay = jnp.exp(jnp.where(causal[:, :, None], diff, -jnp.inf))
        A = jnp.einsum('bhtk,bhsk,bhtsk->bhts', qc, kc, decay)
        o = o_inter + jnp.einsum('bhts,bhsv->bhtv', A, vc)
        S_new = jnp.exp(G_end[:, :, 0, :, None]) * S + jnp.einsum('bhsk,bhsv->bhkv', kc * jnp.exp(G_end - G), vc)
        return S_new, o

    s_end, o = lax.scan(step, s0.astype(jnp.float32), (chunks(q), chunks(k), chunks(v), chunks(logf)))
    o = jnp.moveaxis(o, 0, 2).reshape(B, H, T, dv)
    return o, s_end


def hgrn2_final_state(k, v, logf):
    G = jnp.cumsum(logf, axis=2)
    w = k.astype(jnp.float32) * jnp.exp(G[:, :, -1:, :] - G)
    return jnp.einsum('bhsk,bhsv->bhkv', w, v.astype(jnp.float32))


def flip_t(a):
    return jnp.flip(a, axis=2)


def hgrn2_bidir(q, v, k_f, g_f, k_b, g_b, s_f, s_b):
    o_f, s_f_end = hgrn2_scan(q, k_f, v, g_f, s_f)
    o_b, s_b_end = hgrn2_scan(flip_t(q), flip_t(k_b), flip_t(v), flip_t(g_b), s_b)
    return o_f + flip_t(o_b), s_f_end, s_b_end


def hgrn_readout(o, gain, gate):
    B, H, T, dv = o.shape
    on = rmsnorm(o.transpose(0, 2, 1, 3), gain.reshape(H, dv)).reshape(B, T, H * dv)
    return on.astype(gate.dtype) * jax.nn.silu(gate)


def merge_and_project(att_branch, hg_branch, m_a, m_b, b_m, w_o_hgrn_l, w_o_attn_l, w_out_l):
    y_a = hg_branch @ w_o_hgrn_l
    y_b = att_branch @ w_o_attn_l
    y = jax.nn.sigmoid(m_a + b_m[0]) * y_a + jax.nn.sigmoid(m_b + b_m[1]) * y_b
    return y @ w_out_l


def setup_inputs(seed: int = 0) -> dict:
    key = jax.random.key(seed)
    ks = jax.random.split(key, 17)
    f32 = jnp.float32
    nrm = lambda k, shape: jax.random.normal(k, shape, dtype=f32)
    return {
        'x': nrm(ks[0], (BATCH, SEQ, D_MODEL)),
        'c': nrm(ks[1], (BATCH, D_MODEL)),
        'ctx': nrm(ks[2], (BATCH, CTX_LEN, D_MODEL)),
        'c_ctx': nrm(ks[3], (D_MODEL,)),
        'w_ada': nrm(ks[4], (DEPTH, D_MODEL, 3 * D_MODEL)) * (0.5 * D_MODEL ** -0.5),
        'b_ada': nrm(ks[5], (DEPTH, 3 * D_MODEL)) * 0.02,
        'norm_gain': 1.0 + 0.02 * nrm(ks[6], (DEPTH, D_MODEL)),
        'w_in': nrm(ks[7], (DEPTH, D_MODEL, IN_COLS)) * D_MODEL ** -0.5,
        'b_merge': nrm(ks[8], (DEPTH, N_BRANCH, D_MODEL)) * 0.02,
        'lb_logits_fwd': (0.1 * nrm(ks[9], (DEPTH + 1, HG_FDIM))).at[-1].add(LB_REST_INIT),
        'lb_logits_bwd': (0.1 * nrm(ks[10], (DEPTH + 1, HG_FDIM))).at[-1].add(LB_REST_INIT),
        'hgrn_norm_gain': 1.0 + 0.02 * nrm(ks[11], (DEPTH, HG_VDIM)),
        'w_o_hgrn': nrm(ks[12], (DEPTH, HG_VDIM, D_MODEL)) * HG_VDIM ** -0.5,
        'sink_logits': nrm(ks[13], (DEPTH, ATT_HEADS)) * 0.5,
        'w_o_attn': nrm(ks[14], (DEPTH, ATT_WIDTH, D_MODEL)) * ATT_WIDTH ** -0.5,
        'w_out': nrm(ks[15], (DEPTH, D_MODEL, D_MODEL)) * D_MODEL ** -0.5,
        'final_norm_gain': 1.0 + 0.02 * nrm(ks[16], (D_MODEL,)),
    }


def reference(x, c, ctx, c_ctx, w_ada, b_ada, norm_gain, w_in, b_merge, lb_logits_fwd, lb_logits_bwd,
              hgrn_norm_gain, w_o_hgrn, sink_logits, w_o_attn, w_out, final_norm_gain):
    B, n, _ = x.shape
    L = ctx.shape[1]
    rows = n // GRID_W
    cos, sin = axial_rope_tables(rows, x.dtype)
    q_scale = ATT_HEAD_DIM ** -0.5
    lb_f_all = jnp.cumsum(jax.nn.softmax(lb_logits_fwd.astype(jnp.float32), axis=0), axis=0)
    lb_b_all = jnp.cumsum(jax.nn.softmax(lb_logits_bwd.astype(jnp.float32), axis=0), axis=0)
    h, hc = x, ctx
    for l in range(DEPTH):
        last = l == DEPTH - 1
        mod = jax.nn.silu(c) @ w_ada[l] + b_ada[l]
        shift, scale, gate = jnp.split(mod[:, None, :], 3, axis=-1)
        mod_c = jax.nn.silu(c_ctx) @ w_ada[l] + b_ada[l]
        shift_c, scale_c, gate_c = jnp.split(mod_c, 3, axis=-1)
        xn = rmsnorm(h, norm_gain[l]) * (1.0 + scale) + shift
        cn = rmsnorm(hc, norm_gain[l]) * (1.0 + scale_c) + shift_c
        aq, ak, av, ag, hq, hff, hfb, hi, hg, ma, mb = split_columns(xn @ w_in[l])
        caq, cak, cav, cag, chq, chff, chfb, chi, chg, cma, cmb = split_columns(cn @ w_in[l])

        q = apply_axial_rope(aq.reshape(B, n, ATT_HEADS, ATT_HEAD_DIM), cos, sin) * q_scale
        k = apply_axial_rope(ak.reshape(B, n, ATT_KV_HEADS, ATT_HEAD_DIM), cos, sin)
        v = av.reshape(B, n, ATT_KV_HEADS, ATT_HEAD_DIM)
        k_c = cak.reshape(B, L, ATT_KV_HEADS, ATT_HEAD_DIM)
        v_c = cav.reshape(B, L, ATT_KV_HEADS, ATT_HEAD_DIM)
        att_branch = window_attention(q, k, v, k_c, v_c, sink_logits[l]) * jax.nn.silu(ag)

        g_f, k_f = hgrn_forget(hff, lb_f_all[l])
        g_b, k_b = hgrn_forget(hfb, lb_b_all[l])
        cg_f, ck_f = hgrn_forget(chff, lb_f_all[l])
        cg_b, ck_b = hgrn_forget(chfb, lb_b_all[l])
        v_h = to_heads(hi)
        cv_h = to_heads(chi)
        q_h = to_heads(jax.nn.silu(hq))
        if last:
            s_f = hgrn2_final_state(ck_f, cv_h, cg_f)
            s_b = hgrn2_final_state(flip_t(ck_b), flip_t(cv_h), flip_t(cg_b))
        else:
            cq_h = to_heads(jax.nn.silu(chq))
            zero = jnp.zeros((B, HG_HEADS, HG_EXPAND, HG_HEAD_V), jnp.float32)
            co, s_f, s_b = hgrn2_bidir(cq_h, cv_h, ck_f, cg_f, ck_b, cg_b, zero, zero)
            c_att = context_attention(caq.reshape(B, L, ATT_HEADS, ATT_HEAD_DIM) * q_scale, k_c, v_c,
                                      sink_logits[l]) * jax.nn.silu(cag)
            c_hg = hgrn_readout(co, hgrn_norm_gain[l], chg)
            hc = hc + gate_c * merge_and_project(c_att, c_hg, cma, cmb, b_merge[l],
                                                 w_o_hgrn[l], w_o_attn[l], w_out[l])
        o, _, _ = hgrn2_bidir(q_h, v_h, k_f, g_f, k_b, g_b, s_f, s_b)
        hg_branch = hgrn_readout(o, hgrn_norm_gain[l], hg)

        h = h + gate * merge_and_project(att_branch, hg_branch, ma, mb, b_merge[l],
                                         w_o_hgrn[l], w_o_attn[l], w_out[l])
    return rmsnorm(h, final_norm_gain)
```

```python
import numpy as np
from contextlib import ExitStack
import concourse.bass as bass
import concourse.mybir as mybir
from concourse.bass_utils import run_bass_kernel_spmd

F32 = mybir.dt.float32
BF16 = mybir.dt.bfloat16
AF = mybir.ActivationFunctionType
ALU = mybir.AluOpType

D = 2048
KC = 16
NOWN = 2048
NOTH = 2048
LCTX = 256
EPS = 1e-6
DBG_HEADS = 2


class Prog:
    def __init__(self, nc):
        self.nc = nc
        self.ops = []

    def add(self, eng, fn, r=(), w=(), dma=None):
        self.ops.append(dict(eng=eng, fn=fn, r=tuple(r), w=tuple(w), dma=dma,
                             deps=[], signal=False, sig=None))

    def barrier(self):
        self.add('sp', lambda e: None, [], ['__bar'])
        self.ops[-1]['barrier'] = True
        for eng in ('pe', 'act', 'dve', 'pool'):
            self.add(eng, lambda e: None, ['__bar'], [])

    def capture(self):
        self._saved = self.ops
        self.ops = []

    def release(self):
        got = self.ops
        self.ops = self._saved
        return got

    def interleave(self, a, b):
        n = max(len(a), len(b))
        for i in range(n):
            if i < len(a):
                self.ops.append(a[i])
            if i < len(b):
                self.ops.append(b[i])

    def analyze(self):
        last_w = {}
        readers = {}
        ops = self.ops
        last_by_q = {}
        for i, op in enumerate(ops):
            deps = set()
            if op.get('barrier'):
                deps.update(last_by_q.values())
            last_by_q[('dma', op['dma']) if op['dma'] is not None else ('eng', op['eng'])] = i
            for res in op['r']:
                if res in last_w:
                    deps.add(last_w[res])
            for res in op['w']:
                if res in last_w:
                    deps.add(last_w[res])
                deps.update(readers.get(res, ()))
            final = []
            for d in deps:
                if d == i:
                    continue
                dop = ops[d]
                if dop['eng'] == op['eng'] and dop['dma'] is None and op['dma'] is None:
                    if op['eng'] == 'pe':
                        continue
                    if not (set(dop['w']) & set(op['r'])):
                        continue
                final.append(d)
            best = {}
            keep = []
            for d in final:
                if ops[d]['dma'] is None:
                    k = ops[d]['eng']
                    if k not in best or d > best[k]:
                        best[k] = d
                else:
                    keep.append(d)
            final = keep + list(best.values())
            op['deps'] = final
            for d in final:
                ops[d]['signal'] = True
            for res in op['w']:
                last_w[res] = i
                readers[res] = []
            for res in op['r']:
                if res not in op['w']:
                    readers.setdefault(res, []).append(i)
        cnt = {}
        for op in ops:
            if op['dma'] is not None:
                op['signal'] = True
            if op['signal']:
                key = ('dma', op['dma']) if op['dma'] is not None else ('eng', op['eng'])
                inc = 16 if op['dma'] is not None else 1
                cnt[key] = cnt.get(key, 0) + inc
                op['sig'] = (key, cnt[key])
        self.keys = list(cnt.keys())

    def emit(self, es):
        nc = self.nc
        self.analyze()
        sems = {}
        for n, k in enumerate(self.keys):
            sems[k] = es.enter_context(nc.semaphore("s%d" % n))
        block = es.enter_context(nc.Block())
        ops = self.ops

        def run(engname):
            def body(e):
                waited = {}
                for op in ops:
                    if op['eng'] != engname:
                        continue
                    need = {}
                    for d in op['deps']:
                        k, v = ops[d]['sig']
                        if v > need.get(k, 0):
                            need[k] = v
                    for k, v in need.items():
                        if v > waited.get(k, 0):
                            e.wait_ge(sems[k], v)
                            waited[k] = v
                    ins = op['fn'](e)
                    if op['signal']:
                        k, v = op['sig']
                        if ins is None:
                            ins = e.nop()
                        ins.then_inc(sems[k], 16 if op['dma'] is not None else 1)
            return body

        block.tensor(run('pe'))
        block.scalar(run('act'))
        block.vector(run('dve'))
        block.gpsimd(run('pool'))
        block.sync(run('sp'))


def build(stage=99):
    nc = bass.Bass("TRN2", target_bir_lowering=False)
    es = ExitStack()
    es.enter_context(nc.allow_low_precision("bf16 matmul operands, fp32 accumulation"))
    P = Prog(nc)

    def din(name, shape, dt=F32):
        return nc.dram_tensor(name, list(shape), dt, kind="ExternalInput").ap()

    def dout(name, shape, dt=F32):
        return nc.dram_tensor(name, list(shape), dt, kind="ExternalOutput").ap()

    def sb(name, shape, dt=F32):
        return es.enter_context(nc.sbuf_tensor(name, list(shape), dt))

    PE = lambda fn, r, w: P.add('pe', fn, r, w)
    ACT = lambda fn, r, w: P.add('act', fn, r, w)
    DVE = lambda fn, r, w: P.add('dve', fn, r, w)
    POOL = lambda fn, r, w: P.add('pool', fn, r, w)
    DMA = lambda q, fn, r, w, key: P.add(q, fn, r, w, dma=key)

    x_loc = din("x_loc", [NOWN + NOTH, D])
    ctx_loc = din("ctx_loc", [LCTX, D])
    cvec = din("cvec", [128, KC * 2])
    w_ada_l = din("w_ada_l", [128, KC, 3 * D])
    b_ada2 = din("b_ada2", [2, 3 * D])
    gain_fm = din("gain_fm", [128, KC])
    ident_d = din("ident", [128, 128])
    sel_d = din("sel", [2, 128])

    psb = [es.enter_context(nc.psum_tensor("ps%d" % i, [128, 512], F32)) for i in range(8)]

    ident = sb("ident_sb", [128, 128])
    sel = sb("sel_sb", [2, 128])
    DMA('sp', lambda e: e.dma_start(out=ident[:], in_=ident_d), [], ['ident'], 'c0')
    DMA('sp', lambda e: e.dma_start(out=sel[:], in_=sel_d), [], ['sel'], 'c1')

    w_h = din("w_h", [16, 5, 128, KC * 128])
    lbl_d = din("lbl", [128, 64])
    hgain_d = din("hgain_fm", [128, 16])
    mreset_d = din("mreset", [128, 512])
    maskab_d = din("maskab", [64, 128])
    gate_dram = nc.dram_tensor("gate_scr", [1, D], F32).ap()
    hg_spill = nc.dram_tensor("hg_spill", [D, NOWN], BF16).ap()

    ARENA_W = 29 * 1024
    arena = sb("arena", [128, ARENA_W])
    apos = [0]

    def a_reset():
        apos[0] = 0

    def a_f32(n):
        o = apos[0]
        apos[0] += n
        assert apos[0] <= ARENA_W, apos[0]
        return arena[:, o:o + n]

    def a_bf16(n):
        w = (n + 1) // 2
        return a_f32(w).bitcast(BF16)

    identb = sb("identb", [128, 128], BF16)
    onesb = sb("onesb", [128, 128], BF16)
    mreset = sb("mreset_sb", [128, 512])
    maskab = sb("maskab_sb", [64, 128])
    lbl = sb("lbl_sb", [128, 64])
    lb = sb("lb_sb", [128, 32])
    negoml = sb("negoml", [128, 32])
    oml = sb("oml", [128, 32])
    hgain = sb("hgain_sb", [128, 16])
    gfm = sb("gfm", [128, KC])
    modcol = sb("modcol", [128, 64])
    Avec = sb("Avec", [128, 2, KC])
    Bvec = sb("Bvec", [128, 2, KC])
    xnT = sb("xnT_main", [128, KC, NOWN], BF16)
    xnT_ctx = sb("xnT_ctx", [128, KC, LCTX], BF16)
    xnT_halo = sb("xnT_halo", [128, KC, 128], BF16)
    SinitB = sb("SinitB", [128, 16, 128])
    c_sb = sb("c_sb", [128, KC * 2])
    sc_sb = sb("sc_sb", [128, KC * 2])

    DMA('sp', lambda e: e.dma_start(out=mreset[:], in_=mreset_d), [], ['mreset'], 'c5')
    DMA('sp', lambda e: e.dma_start(out=maskab[:], in_=maskab_d), [], ['maskab'], 'c6')
    DMA('sp', lambda e: e.dma_start(out=lbl[:], in_=lbl_d), [], ['lbl'], 'c7')
    DMA('sp', lambda e: e.dma_start(out=hgain[:], in_=hgain_d), [], ['hgain'], 'c8')
    DMA('sp', lambda e: e.dma_start(out=c_sb[:], in_=cvec), [], ['c_sb'], 'c2')
    DMA('sp', lambda e: e.dma_start(out=gfm[:], in_=gain_fm), [], ['gfm'], 'c4')
    DVE(lambda e: e.tensor_copy(out=identb[:], in_=ident[:]), ['ident'], ['identb'])
    DVE(lambda e: e.memset(onesb[:], 1.0), [], ['onesb'])
    lbl4 = lbl[:].rearrange("p (d l h) -> p d l h", d=2, l=2)
    lb3 = lb[:].rearrange("p (d h) -> p d h", d=2)
    DVE(lambda e: e.tensor_tensor(out=lb3, in0=lbl4[:, :, 0, :], in1=lbl4[:, :, 1, :], op=ALU.subtract), ['lbl'], ['lb'])
    ACT(lambda e: e.activation(out=lb[:], in_=lb[:], func=AF.Sigmoid), ['lb'], ['lb'])
    DVE(lambda e: e.tensor_scalar(out=negoml[:], in0=lb[:], scalar1=-1.0, scalar2=None, op0=ALU.add), ['lb'], ['negoml'])
    DVE(lambda e: e.tensor_scalar(out=oml[:], in0=lb[:], scalar1=-1.0, scalar2=1.0, op0=ALU.mult, op1=ALU.add), ['lb'], ['oml'])

    ACT(lambda e: e.activation(out=sc_sb[:], in_=c_sb[:], func=AF.Silu), ['c_sb'], ['sc_sb'])
    a_reset()
    wada = [a_f32(KC * 512).rearrange("p (k c) -> p k c", k=KC) for _ in range(3)]
    badab = [a_f32(512) for _ in range(2)]
    mod2b = [a_f32(512) for _ in range(2)]
    for blk in range(12):
        s = blk % 2
        ws = blk % 3
        DMA('sp' if blk % 2 == 0 else 'act',
            lambda e, ws=ws, blk=blk: e.dma_start(out=wada[ws], in_=w_ada_l[:, :, blk * 512:(blk + 1) * 512]),
            [], [('wada', ws)], 'wada%d' % ws)
        DMA('sp', lambda e, s=s, blk=blk: e.dma_start(out=badab[s][0:2, :], in_=b_ada2[:, blk * 512:(blk + 1) * 512]),
            [], [('badab', s)], 'badab%d' % s)
        for kc in range(KC):
            PE(lambda e, s=s, ws=ws, kc=kc: e.matmul(psb[s][0:2, :], lhsT=sc_sb[:, 2 * kc:2 * kc + 2],
                                              rhs=wada[ws][:, kc, :], start=(kc == 0), stop=(kc == KC - 1)),
               ['sc_sb', ('wada', ws)], [('ps', s)])
        DVE(lambda e, s=s: e.tensor_tensor(out=mod2b[s][0:2, :], in0=psb[s][0:2, :], in1=badab[s][0:2, :], op=ALU.add),
            [('ps', s), ('badab', s)], [('mod2b', s)])
        if blk < 8:
            for jj in range(4):
                j = blk * 4 + jj
                PE(lambda e, s=s, j=j, jj=jj: e.matmul(psb[2][:, 2 * j:2 * j + 2], lhsT=mod2b[s][0:2, jj * 128:(jj + 1) * 128],
                                                       rhs=ident[0:2, 0:2], start=True, stop=True),
                   [('mod2b', s), 'ident'], [('ps', 2)])
        else:
            DMA('sp', lambda e, s=s, blk=blk: e.dma_start(out=gate_dram[0:1, (blk - 8) * 512:(blk - 7) * 512], in_=mod2b[s][0:1, :]),
                [('mod2b', s)], ['gate_dram'], 'gd%d' % s)
    DVE(lambda e: e.tensor_copy(out=modcol[:], in_=psb[2][:, 0:64]), [('ps', 2)], ['modcol'])
    mc3 = modcol[:].rearrange("p (j t) -> p j t", t=2)
    for t in range(2):
        DVE(lambda e, t=t: e.scalar_tensor_tensor(out=Avec[:, t, :], in0=mc3[:, 16:32, t], scalar=1.0, in1=gfm[:],
                                                  op0=ALU.add, op1=ALU.mult), ['modcol', 'gfm'], ['Avec'])
        DVE(lambda e, t=t: e.tensor_copy(out=Bvec[:, t, :], in_=mc3[:, 0:16, t]), ['modcol'], ['Bvec'])
    P.barrier()

    def alloc_xt():
        a_reset()
        d = dict(xt=[a_f32(D) for _ in range(3)], junk=a_bf16(D),
                 ssq=[a_f32(1) for _ in range(3)], rstd=[a_f32(1) for _ in range(3)])
        return d

    def xn_stats(X, n, src_rows):
        s = n % 3
        xt, junk, ssq, rstd = X['xt'], X['junk'], X['ssq'], X['rstd']
        DMA('sp', lambda e: e.dma_start(out=xt[s], in_=src_rows), [], [('xt', s)], 'xt%d' % s)
        DVE(lambda e: e.memset(ssq[s], 0.0), [], [('ssq', s)])
        ACT(lambda e: e.activation(out=junk, in_=xt[s], func=AF.Square, accum_out=ssq[s]),
            [('xt', s), ('ssq', s)], [('ssq', s), 'junk'])
        DVE(lambda e: e.tensor_scalar(out=rstd[s], in0=ssq[s], scalar1=1.0 / D, scalar2=EPS,
                                      op0=ALU.mult, op1=ALU.add), [('ssq', s)], [('rstd', s)])
        ACT(lambda e: e.sqrt(out=rstd[s], in_=rstd[s]), [('rstd', s)], [('rstd', s)])
        DVE(lambda e: e.reciprocal(out=rstd[s], in_=rstd[s]), [('rstd', s)], [('rstd', s)])
        DVE(lambda e: e.tensor_scalar(out=xt[s], in0=xt[s], scalar1=rstd[s][:, 0:1], scalar2=None, op0=ALU.mult),
            [('xt', s), ('rstd', s)], [('xt', s)])

    def xn_transpose(X, n, dst, dst_res, which):
        s = n % 3
        pb = n % 2
        xt = X['xt']
        for j in range(KC):
            bnk = j // 4 + 4 * pb
            PE(lambda e, j=j, bnk=bnk: e.transpose(out=psb[bnk][:, (j % 4) * 128:(j % 4 + 1) * 128],
                                                   in_=xt[s][:, j * 128:(j + 1) * 128], identity=ident[:]),
               [('xt', s), 'ident'], [('ps', bnk)])
        for j in range(KC):
            bnk = j // 4 + 4 * pb
            ACT(lambda e, j=j, bnk=bnk: e.activation(out=dst[:, j, :], in_=psb[bnk][:, (j % 4) * 128:(j % 4 + 1) * 128],
                                                     func=AF.Identity, scale=Avec[:, which, j:j + 1],
                                                     bias=Bvec[:, which, j:j + 1]),
                [('ps', bnk), 'Avec', 'Bvec'], [dst_res])

    def build_xnT_seq(X, tiles):
        for n, tl in enumerate(tiles):
            if n == 0:
                xn_stats(X, 0, tl[0])
            if n + 1 < len(tiles):
                xn_stats(X, n + 1, tiles[n + 1][0])
            xn_transpose(X, n, tl[1], tl[2], tl[3])

    X = alloc_xt()
    tiles = [(ctx_loc[t * 128:(t + 1) * 128, :], xnT_ctx[:, :, t * 128:(t + 1) * 128], 'xnT_ctx', 1) for t in range(2)]
    tiles += [(x_loc[NOWN + t * 128:NOWN + (t + 1) * 128, :], xnT[:, :, t * 128:(t + 1) * 128], 'xnT', 0)
              for t in range(NOTH // 128)]
    build_xnT_seq(X, tiles)
    POOL(lambda e: e.tensor_copy(out=xnT_halo[:], in_=xnT[:, :, 0:128]), ['xnT'], ['xnT_halo'])
    P.barrier()
    NWS = 8
    wctr = [0]
    onecol = sb("onecol", [128, 1])
    DVE(lambda e: e.memset(onecol[:], 1.0), [], ['onecol'])

    def alloc_hgrn():
        a_reset()
        H = {}
        H['wh'] = [a_bf16(KC * 128).rearrange("p (k c) -> p k c", k=KC) for _ in range(NWS)]
        H['T'] = [{nm: a_f32(512) for nm in ('ks', 'f', 'G', 'E', 'X', 'Eq', 'Ek')} for _ in range(2)]
        H['qf'] = a_f32(512)
        H['rs'] = H['T'][1]['ks']
        H['tt'] = H['T'][1]['f']
        H['qp'] = [a_bf16(NOWN) for _ in range(2)]
        H['kp'] = [a_bf16(NOWN) for _ in range(2)]
        H['gate'] = a_bf16(NOWN)
        H['vT'] = a_bf16(512)
        H['vtm'] = a_bf16(32 * 128).rearrange("p (c v) -> p c v", v=128)
        H['ktm'] = a_bf16(4 * 128).rearrange("p (c v) -> p c v", v=128)
        H['Sp'] = a_bf16(4 * 128).rearrange("p (c v) -> p c v", v=128)
        H['Am'] = a_bf16(256)
        H['tmpU'] = a_f32(4 * 128).rearrange("p (c v) -> p c v", v=128)
        H['oacc'] = a_f32(NOWN)
        H['SA'] = a_f32(128)
        H['abc'] = a_f32(3 * 2 * 32).rearrange("p (k d c) -> p k d c", k=3, d=2)
        H['sq'] = a_bf16(512)
        H['ktm2'] = [H['ktm'], a_bf16(4 * 128).rearrange("p (c v) -> p c v", v=128)]
        H['Sp2'] = [H['Sp'], a_bf16(4 * 128).rearrange("p (c v) -> p c v", v=128)]
        H['Am2'] = [H['Am'], a_bf16(256)]
        H['tmpU2'] = [H['tmpU'], a_f32(4 * 128).rearrange("p (c v) -> p c v", v=128)]
        return H

    def load_unit(H, h, u):
        slot = wctr[0] % NWS
        wctr[0] += 1
        DMA('pool', lambda e: e.dma_start(out=H['wh'][slot].rearrange("p k c -> p (k c)"), in_=w_h[h, u]),
            [], [('wh', slot)], 'wh%d' % slot)
        return slot

    pctr = [0]

    def proj(H, slot, xsrc, n):
        bnk = pctr[0] % 4
        pctr[0] += 1
        for kc in range(KC):
            PE(lambda e, kc=kc: e.matmul(psb[bnk][:, 0:n], lhsT=H['wh'][slot][:, kc, :], rhs=xsrc[:, kc, :],
                                         start=(kc == 0), stop=(kc == KC - 1)),
               [('wh', slot), 'xnT', 'xnT_ctx'], [('ps', bnk)])
        return bnk

    def c3(ap, n):
        return ap[:, 0:n].rearrange("p (c t) -> p c t", t=64)

    def hgrn_elem(H, h, d, bnk, col0, n):
        T = H['T'][d]
        nch = n // 64
        cb = col0 // 64
        ks, f, G, E, X_, Eq, Ek = (T[k] for k in ('ks', 'f', 'G', 'E', 'X', 'Eq', 'Ek'))
        qf = H['qf']
        R = lambda nm: (nm, d)
        hd = d * 16 + h
        ACT(lambda e: e.activation(out=ks[:, 0:n], in_=psb[bnk][:, 0:n], func=AF.Sigmoid, scale=-1.0),
            [('ps', bnk)], [R('ks')])
        DVE(lambda e: e.tensor_scalar(out=f[:, 0:n], in0=ks[:, 0:n], scalar1=negoml[:, hd:hd + 1], scalar2=1.0,
                                      op0=ALU.mult, op1=ALU.add), [R('ks'), 'negoml'], [R('f')])
        ACT(lambda e: e.activation(out=f[:, 0:n], in_=f[:, 0:n], func=AF.Ln), [R('f')], [R('f')])
        DVE(lambda e: e.tensor_tensor_scan(out=G[:, 0:n], data0=mreset[:, 0:n], data1=f[:, 0:n], initial=0.0,
                                           op0=ALU.mult, op1=ALU.add), [R('f'), 'mreset'], [R('G')])
        G3, E3, X3, Eq3 = c3(G, n), c3(E, n), c3(X_, n), c3(Eq, n)
        if d == 0:
            DVE(lambda e: e.tensor_tensor(out=X3, in0=G3, in1=G3[:, :, 31:32].to_broadcast([128, nch, 64]),
                                          op=ALU.subtract), [R('G')], [R('X')])
        else:
            DVE(lambda e: e.tensor_tensor(out=E[:, 0:n], in0=G[:, 0:n], in1=f[:, 0:n], op=ALU.subtract),
                [R('G'), R('f')], [R('E')])
            DVE(lambda e: e.tensor_tensor(out=X3, in0=E3[:, :, 32:33].to_broadcast([128, nch, 64]), in1=E3,
                                          op=ALU.subtract), [R('E')], [R('X')])
        ACT(lambda e: e.activation(out=Eq[:, 0:n], in_=X_[:, 0:n], func=AF.Exp), [R('X')], [R('Eq')])
        ACT(lambda e: e.activation(out=Ek[:, 0:n], in_=X_[:, 0:n], func=AF.Exp, scale=-1.0), [R('X')], [R('Ek')])
        DVE(lambda e: e.scalar_tensor_tensor(out=H['kp'][d][:, col0:col0 + n], in0=ks[:, 0:n], scalar=oml[:, hd:hd + 1],
                                             in1=Ek[:, 0:n], op0=ALU.mult, op1=ALU.mult),
            [R('ks'), R('Ek'), 'oml'], [('kp', d)])
        DVE(lambda e: e.tensor_tensor(out=H['qp'][d][:, col0:col0 + n], in0=qf[:, 0:n], in1=Eq[:, 0:n], op=ALU.mult),
            ['qf', R('Eq')], [('qp', d)])
        abc = H['abc']
        ACT(lambda e: e.activation(out=abc[:, 0, d, cb:cb + nch], in_=G3[:, :, 63], func=AF.Exp), [R('G')], [('abc', d)])
        if d == 0:
            DVE(lambda e: e.tensor_copy(out=abc[:, 1, d, cb:cb + nch], in_=Eq3[:, :, 63]), [R('Eq')], [('abc', d)])
            ACT(lambda e: e.activation(out=abc[:, 2, d, cb:cb + nch], in_=G3[:, :, 31], func=AF.Exp), [R('G')], [('abc', d)])
        else:
            DVE(lambda e: e.tensor_copy(out=abc[:, 1, d, cb:cb + nch], in_=Eq3[:, :, 0]), [R('Eq')], [('abc', d)])
            DVE(lambda e: e.tensor_tensor(out=abc[:, 2, d, cb:cb + nch], in0=G3[:, :, 63], in1=E3[:, :, 32],
                                          op=ALU.subtract), [R('G'), R('E')], [('abc', d)])
            ACT(lambda e: e.activation(out=abc[:, 2, d, cb:cb + nch], in_=abc[:, 2, d, cb:cb + nch], func=AF.Exp),
                [('abc', d)], [('abc', d)])

    def build_vtm(H, bnk, col0, n):
        nch = n // 64
        cb = col0 // 64
        if bnk is not None:
            ACT(lambda e: e.activation(out=H['vT'][:, 0:n], in_=psb[bnk][:, 0:n], func=AF.Copy), [('ps', bnk)], ['vT'])
        for g0 in range(0, nch, 4):
            bk = 4 + (g0 // 4) % 2
            for j in range(4):
                PE(lambda e, j=j, g0=g0, bk=bk: e.matmul(psb[bk][0:64, j * 128:(j + 1) * 128],
                                                         lhsT=H['vT'][:, (g0 + j) * 64:(g0 + j + 1) * 64], rhs=identb[:],
                                                         start=True, stop=True), ['vT', 'identb'], [('ps', bk)])
        for g0 in range(0, nch, 4):
            bk = 4 + (g0 // 4) % 2
            ACT(lambda e, g0=g0, bk=bk: e.activation(out=H['vtm'][0:64, cb + g0:cb + g0 + 4, :],
                                                     in_=psb[bk][0:64, :].rearrange("p (c v) -> p c v", v=128), func=AF.Copy),
                [('ps', bk)], ['vtm'])

    def state_block(H, h, d, bz, bv, n, S, sres, par=0):
        T = H['T'][par]
        ks, f, G, E, Ek = (T[k] for k in ('ks', 'f', 'G', 'E', 'Ek'))
        R = lambda nm: (nm, par)
        hd = d * 16 + h
        nt = n // 128
        kpb = H['kp'][par]
        vT = H['vT'] if par == 0 else H['sq']
        ktm = H['ktm'] if par == 0 else H['Sp']
        vtm = H['vtm'][:, 4 * par:4 * par + 4, :]
        bT = 4 if par == 0 else 7
        ab = H['abc'][:, 0, par, 0:1]
        nV = 'vT' if par == 0 else ('vT', 1)
        nK = 'ktm' if par == 0 else ('ktm', 1)
        nVt = 'vtm' if par == 0 else ('vtm', 1)
        nP5 = ('ps', 5) if par == 0 else ('ps', 7)
        bV = 6 if par == 0 else 7
        bU_ = 5 if par == 0 else 7
        ACT(lambda e: e.activation(out=ks[:, 0:n], in_=psb[bz][:, 0:n], func=AF.Sigmoid, scale=-1.0), [('ps', bz)], [R('ks')])
        ACT(lambda e: e.activation(out=vT[:, 0:n], in_=psb[bv][:, 0:n], func=AF.Copy), [('ps', bv)], [nV])
        DVE(lambda e: e.tensor_scalar(out=f[:, 0:n], in0=ks[:, 0:n], scalar1=negoml[:, hd:hd + 1], scalar2=1.0,
                                      op0=ALU.mult, op1=ALU.add), [R('ks'), 'negoml'], [R('f')])
        ACT(lambda e: e.activation(out=f[:, 0:n], in_=f[:, 0:n], func=AF.Ln), [R('f')], [R('f')])
        DVE(lambda e: e.tensor_tensor_scan(out=G[:, 0:n], data0=onecol[:, 0:1].to_broadcast([128, n]), data1=f[:, 0:n],
                                           initial=0.0, op0=ALU.mult, op1=ALU.add), [R('f'), 'onecol'], [R('G')])
        if d == 0:
            ACT(lambda e: e.activation(out=Ek[:, 0:n], in_=G[:, 0:n], func=AF.Exp, scale=-1.0, bias=G[:, n - 1:n]),
                [R('G')], [R('Ek')])
        else:
            DVE(lambda e: e.tensor_tensor(out=E[:, 0:n], in0=G[:, 0:n], in1=f[:, 0:n], op=ALU.subtract),
                [R('G'), R('f')], [R('E')])
            ACT(lambda e: e.activation(out=Ek[:, 0:n], in_=E[:, 0:n], func=AF.Exp), [R('E')], [R('Ek')])
        ACT(lambda e: e.activation(out=ab, in_=G[:, n - 1:n], func=AF.Exp), [R('G')], [('abc', par)])
        DVE(lambda e: e.scalar_tensor_tensor(out=kpb[:, 0:n], in0=ks[:, 0:n], scalar=oml[:, hd:hd + 1],
                                             in1=Ek[:, 0:n], op0=ALU.mult, op1=ALU.mult),
            [R('ks'), R('Ek'), 'oml'], [('kp', par)])
        for t in range(nt):
            PE(lambda e, t=t: e.matmul(psb[bT][:, t * 128:(t + 1) * 128], lhsT=kpb[:, t * 128:(t + 1) * 128],
                                       rhs=identb[:], start=True, stop=True), [('kp', par), 'identb'], [('ps', bT)])
        ACT(lambda e: e.activation(out=ktm[:, 0:nt, :], in_=psb[bT][:, 0:nt * 128].rearrange("p (c v) -> p c v", v=128),
                                   func=AF.Copy), [('ps', bT)], [nK])
        for t in range(nt):
            PE(lambda e, t=t: e.matmul(psb[bV][:, t * 128:(t + 1) * 128], lhsT=vT[:, t * 128:(t + 1) * 128],
                                       rhs=identb[:], start=True, stop=True), [nV, 'identb'], [('ps', bV)])
        ACT(lambda e: e.activation(out=vtm[:, 0:nt, :], in_=psb[bV][:, 0:nt * 128].rearrange("p (c v) -> p c v", v=128),
                                   func=AF.Copy), [('ps', bV)], [nVt])
        for t in range(nt):
            PE(lambda e, t=t: e.matmul(psb[bU_][:, 0:128], lhsT=ktm[:, t, :], rhs=vtm[:, t, :],
                                       start=(t == 0), stop=(t == nt - 1)), [nK, nVt], [nP5])
        DVE(lambda e: e.scalar_tensor_tensor(out=S, in0=S, scalar=ab, in1=psb[bU_][:, 0:128],
                                             op0=ALU.mult, op1=ALU.add), [sres, nP5, ('abc', par)], [sres])

    def hgrn_chain(H, d, S, sres, groups, bset):
        abc = H['abc']
        ktm, Sp, Am, tmpU = H['ktm2'][bset], H['Sp2'][bset], H['Am2'][bset], H['tmpU2'][bset]
        bT, bU, bA, bO = (4, 5, 6, 7) if bset == 0 else (0, 1, 2, 3)
        nK = 'ktm' if bset == 0 else ('ktm', 'b')
        nA = 'Am' if bset == 0 else ('Am', 'b')
        nS = (lambda j: ('Sp', j)) if bset == 0 else (lambda j: ('Sp', 'b', j))
        nU = (lambda j: ('tmpU', j)) if bset == 0 else (lambda j: ('tmpU', 'b', j))
        for g0 in groups:
            order = range(4) if d == 0 else range(3, -1, -1)
            for j in range(4):
                c = g0 + j
                PE(lambda e, j=j, c=c: e.matmul(psb[bT][0:64, j * 128:(j + 1) * 128],
                                                lhsT=H['kp'][d][:, c * 64:(c + 1) * 64], rhs=identb[:],
                                                start=True, stop=True), [('kp', d), 'identb'], [('ps', bT)])
            ACT(lambda e: e.activation(out=ktm[0:64, :, :], in_=psb[bT][0:64, :].rearrange("p (c v) -> p c v", v=128),
                                       func=AF.Copy), [('ps', bT)], [nK])
            for j in range(4):
                c = g0 + j
                PE(lambda e, j=j, c=c: e.matmul(psb[bU][:, j * 128:(j + 1) * 128], lhsT=ktm[0:64, j, :],
                                                rhs=H['vtm'][0:64, c, :], start=True, stop=True),
                   [nK, 'vtm'], [('ps', bU)])
            for j in range(4):
                c = g0 + j
                PE(lambda e, j=j, c=c: e.matmul(psb[bA][0:64, j * 64:(j + 1) * 64], lhsT=H['kp'][d][:, c * 64:(c + 1) * 64],
                                                rhs=H['qp'][d][:, c * 64:(c + 1) * 64], start=True, stop=True),
                   [('kp', d), ('qp', d)], [('ps', bA)])
            DVE(lambda e: e.tensor_tensor(out=Am[0:64, 0:256].rearrange("p (c t) -> p c t", t=64),
                                          in0=psb[bA][0:64, 0:256].rearrange("p (c t) -> p c t", t=64),
                                          in1=maskab[:, d * 64:(d + 1) * 64].unsqueeze(1).to_broadcast([64, 4, 64]),
                                          op=ALU.mult), [('ps', bA), 'maskab'], [nA])
            for j in order:
                c = g0 + j
                ACT(lambda e, j=j, c=c: e.activation(out=tmpU[:, j, :], in_=psb[bU][:, j * 128:(j + 1) * 128],
                                                     func=AF.Copy, scale=abc[:, 1, d, c:c + 1]),
                    [('ps', bU), ('abc', d)], [nU(j)])
            for j in order:
                c = g0 + j
                DVE(lambda e, j=j, c=c: e.tensor_scalar(out=Sp[:, j, :], in0=S, scalar1=abc[:, 2, d, c:c + 1],
                                                        scalar2=None, op0=ALU.mult), [sres, ('abc', d)], [nS(j)])
                DVE(lambda e, j=j, c=c: e.scalar_tensor_tensor(out=S, in0=S, scalar=abc[:, 0, d, c:c + 1],
                                                               in1=tmpU[:, j, :], op0=ALU.mult, op1=ALU.add),
                    [sres, nU(j), ('abc', d)], [sres])
            for j in range(4):
                c = g0 + j
                PE(lambda e, j=j, c=c: e.matmul(psb[bO][:, j * 64:(j + 1) * 64], lhsT=Sp[:, j, :],
                                                rhs=H['qp'][d][:, c * 64:(c + 1) * 64], start=True, stop=False),
                   [nS(j), ('qp', d)], [('ps', bO)])
                PE(lambda e, j=j, c=c: e.matmul(psb[bO][:, j * 64:(j + 1) * 64], lhsT=H['vtm'][0:64, c, :],
                                                rhs=Am[0:64, j * 64:(j + 1) * 64], start=False, stop=True),
                   ['vtm', nA], [('ps', bO)])
            oc = H['oacc'][:, g0 * 64:g0 * 64 + 256]
            first = (g0 < 64) if d == 0 else (g0 >= 64)
            first = (g0 < 16) if d == 0 else (g0 >= 16)
            if first:
                DVE(lambda e, oc=oc: e.tensor_copy(out=oc, in_=psb[bO][:, 0:256]), [('ps', bO)], [('oacc', g0)])
            else:
                DVE(lambda e, oc=oc: e.tensor_tensor(out=oc, in0=oc, in1=psb[bO][:, 0:256], op=ALU.add),
                    [('ps', bO), ('oacc', g0)], [('oacc', g0)])

    H = alloc_hgrn()
    for h in range({99: 16, 1: DBG_HEADS, 2: 0}[stage]):
        sB = load_unit(H, h, 2)
        sV = load_unit(H, h, 3)
        S = SinitB[:, h, :]
        DVE(lambda e, S=S: e.memset(S, 0.0), [], ['SB'])
        blocks = [(xnT_ctx[:], LCTX)] + [(xnT[:, :, tb * 512:(tb + 1) * 512], 512) for tb in range(3, -1, -1)]
        blocks.append(None)
        for i in range(0, 6, 2):
            lists = []
            for par in range(2):
                if blocks[i + par] is None:
                    lists.append([])
                    continue
                xs_, n_ = blocks[i + par]
                bz = proj(H, sB, xs_, n_)
                bv = proj(H, sV, xs_, n_)
                P.capture()
                state_block(H, h, 1, bz, bv, n_, S, 'SB', par)
                lists.append(P.release())
            P.interleave(lists[0], lists[1])
    P.barrier()
    X = alloc_xt()
    build_xnT_seq(X, [(x_loc[t * 128:(t + 1) * 128, :], xnT[:, :, t * 128:(t + 1) * 128], 'xnT', 0)
                      for t in range(NOWN // 128)])
    P.barrier()

    H = alloc_hgrn()
    NH = {99: 16, 1: DBG_HEADS, 2: 0}[stage]
    for h in range(NH):
        su = [load_unit(H, h, u) for u in range(5)]
        SA = H['SA']
        DVE(lambda e: e.memset(SA, 0.0), [], ['SA'])
        bz = proj(H, su[1], xnT_ctx[:], LCTX)
        bv = proj(H, su[3], xnT_ctx[:], LCTX)
        state_block(H, h, 0, bz, bv, LCTX, SA, 'SA')
        for tb in range(4):
            if True:
                xs = xnT[:, :, tb * 512:(tb + 1) * 512]
                bza = proj(H, su[1], xs, 512)
                bzb = proj(H, su[2], xs, 512)
                bv = proj(H, su[3], xs, 512)
                bq = proj(H, su[0], xs, 512)
                P.capture()
                hgrn_elem(H, h, 0, bza, tb * 512, 512)
                la = P.release()
                P.capture()
                hgrn_elem(H, h, 1, bzb, tb * 512, 512)
                lb_ = P.release()
                P.capture()
                ACT(lambda e, bv=bv: e.activation(out=H['vT'][:, 0:512], in_=psb[bv][:, 0:512], func=AF.Copy), [('ps', bv)], ['vT'])
                ACT(lambda e, bq=bq: e.activation(out=H['qf'][:, :], in_=psb[bq][:, :], func=AF.Silu), [('ps', bq)], ['qf'])
                lc = P.release()
                P.interleave(la[:3], lb_[:3])
                P.interleave(lc, [])
                bg = proj(H, su[4], xs, 512)
                P.interleave(la[3:], lb_[3:])
                build_vtm(H, None, tb * 512, 512)
                ACT(lambda e, bg=bg, tb=tb: e.activation(out=H['gate'][:, tb * 512:(tb + 1) * 512], in_=psb[bg][:, :], func=AF.Silu),
                    [('ps', bg)], ['gate'])
        P.capture()
        hgrn_chain(H, 0, SA, 'SA', list(range(0, 32, 4)), 0)
        lca = P.release()
        P.capture()
        hgrn_chain(H, 1, SinitB[:, h, :], 'SB', list(range(28, -1, -4)), 1)
        lcb = P.release()
        P.interleave(lca, lcb)
        for tb in range(4):
            oc = H['oacc'][:, tb * 512:(tb + 1) * 512]
            ocr = [('oacc', tb * 8), ('oacc', tb * 8 + 4)]
            ACT(lambda e, oc=oc: e.activation(out=H['sq'][:, :], in_=oc, func=AF.Square), ocr, ['sq'])
            PE(lambda e: e.matmul(psb[6][:, :], lhsT=onesb[:], rhs=H['sq'][:, :], start=True, stop=True),
               ['sq', 'onesb'], [('ps', 6)])
            DVE(lambda e: e.tensor_scalar(out=H['rs'][:, :], in0=psb[6][:, :], scalar1=1.0 / 128, scalar2=EPS,
                                          op0=ALU.mult, op1=ALU.add), [('ps', 6)], [('ks', 1)])
            ACT(lambda e: e.activation(out=H['rs'][:, :], in_=H['rs'][:, :], func=AF.Ln), [('ks', 1)], [('ks', 1)])
            ACT(lambda e: e.activation(out=H['rs'][:, :], in_=H['rs'][:, :], func=AF.Exp, scale=-0.5), [('ks', 1)], [('ks', 1)])
            DVE(lambda e, oc=oc: e.tensor_tensor(out=H['tt'][:, :], in0=oc, in1=H['rs'][:, :], op=ALU.mult),
                ocr + [('ks', 1)], [('f', 1)])
            DVE(lambda e, tb=tb, h=h: e.scalar_tensor_tensor(out=H['gate'][:, tb * 512:(tb + 1) * 512], in0=H['tt'][:, :],
                                                             scalar=hgain[:, h:h + 1], in1=H['gate'][:, tb * 512:(tb + 1) * 512],
                                                             op0=ALU.mult, op1=ALU.mult), [('f', 1), 'gate', 'hgain'], ['gate'])
        DMA('sp', lambda e, h=h: e.dma_start(out=hg_spill[h * 128:(h + 1) * 128, :], in_=H['gate'][:, :]),
            ['gate'], ['hg_spill'], 'hgsp')
    P.barrier()

    if stage == 1:
        dbg_hg = dout("dbg_hg", [128 * NH, NOWN], BF16)
        dbg_sb = dout("dbg_sb", [128, 16 * 128])
        DMA('sp', lambda e: e.dma_start(out=dbg_hg, in_=hg_spill[0:128 * NH, :]), ['hg_spill'], ['o1'], 'o1')
        DMA('sp', lambda e: e.dma_start(out=dbg_sb, in_=SinitB[:].rearrange("p h v -> p (h v)")), ['SB'], ['o2'], 'o2')
        P.add('sp', lambda e: None, ['o1', 'o2'], [])
        P.emit(es)
        es.close()
        return nc
    w_att = din("w_att", [4, 9, 128, KC * 128])
    w_v = din("w_v", [128, KC * 512])
    cos_d = din("cosT", [128, NOWN + 128])
    sin_d = din("sinT", [128, NOWN + 128])
    rotT_d = din("rotT", [128, 128])
    sink_d = din("sink", [1, 16])
    bmask_d = din("bmask", [128, 2 * 512])
    att_spill = nc.dram_tensor("att_spill", [D, NOWN], BF16).ap()
    QSCALE = 128.0 ** -0.5
    NKT = NOWN // 128 + 1

    a_reset()
    A = {}
    A['wv_flat'] = a_bf16(KC * 512)
    A['wv'] = A['wv_flat'].rearrange("p (k c) -> p k c", k=KC)
    A['qT_all'] = A['wv_flat']
    A['wh'] = [a_bf16(KC * 128).rearrange("p (k c) -> p k c", k=KC) for _ in range(4)]
    A['V'] = a_bf16(NKT * 512).rearrange("p (t c) -> p t c", c=512)
    A['Vc'] = a_bf16(2 * 512).rearrange("p (t c) -> p t c", c=512)
    A['kT'] = a_bf16(NOWN + 128)
    A['kcT'] = a_bf16(LCTX)
    A['cos'] = a_f32(NOWN + 128)
    A['sin'] = a_f32(NOWN + 128)
    A['rotT'] = a_f32(128)
    A['qf'] = a_f32(512)
    A['t1'] = a_f32(512)
    A['t2'] = a_f32(512)
    A['gT_all'] = a_bf16(4 * 2048)
    A['ao'] = a_bf16(4 * 512)
    A['pT'] = [a_bf16(512) for _ in range(2)]
    A['den'] = a_f32(512)
    A['o1'] = a_f32(512)
    A['sinkrow'] = a_f32(512)
    A['sinkexp'] = a_f32(16)
    A['bmf'] = a_f32(1024)
    A['bm'] = a_bf16(1024)
    DMA('sp', lambda e: e.dma_start(out=A['cos'], in_=cos_d), [], ['cos'], 'p3a')
    DMA('sp', lambda e: e.dma_start(out=A['sin'], in_=sin_d), [], ['sin'], 'p3b')
    DMA('sp', lambda e: e.dma_start(out=A['rotT'], in_=rotT_d), [], ['rotT'], 'p3c')
    DMA('sp', lambda e: e.dma_start(out=A['bmf'], in_=bmask_d), [], ['bmf'], 'p3d')
    DMA('sp', lambda e: e.dma_start(out=A['sinkexp'], in_=sink_d.partition_broadcast(128)), [], ['sinkexp'], 'p3e')
    DMA('pool', lambda e: e.dma_start(out=A['wv'].rearrange("p k c -> p (k c)"), in_=w_v), [], ['wv'], 'p3f')
    ACT(lambda e: e.activation(out=A['sinkexp'], in_=A['sinkexp'], func=AF.Exp), ['sinkexp'], ['sinkexp'])
    DVE(lambda e: e.tensor_copy(out=A['bm'], in_=A['bmf']), ['bmf'], ['bm'])

    awctr = [0]

    def load_att(g, u):
        slot = awctr[0] % 4
        awctr[0] += 1
        DMA('pool', lambda e: e.dma_start(out=A['wh'][slot].rearrange("p k c -> p (k c)"), in_=w_att[g, u]),
            [], [('wh', slot)], 'wh%d' % slot)
        return slot

    apctr = [0]

    def aproj(slot, xsrc, n):
        bnk = apctr[0] % 3
        apctr[0] += 1
        for kc in range(KC):
            PE(lambda e, kc=kc: e.matmul(psb[bnk][:, 0:n], lhsT=A['wh'][slot][:, kc, :], rhs=xsrc[:, kc, :],
                                         start=(kc == 0), stop=(kc == KC - 1)),
               [('wh', slot), 'xnT', 'xnT_ctx', 'xnT_halo'], [('ps', bnk)])
        return bnk

    def vproj(xsrc, dst):
        bnk = apctr[0] % 3
        apctr[0] += 1
        for kc in range(KC):
            PE(lambda e, kc=kc: e.matmul(psb[bnk][:, :], lhsT=xsrc[:, kc, :], rhs=A['wv'][:, kc, :],
                                         start=(kc == 0), stop=(kc == KC - 1)),
               ['wv', 'xnT', 'xnT_ctx', 'xnT_halo'], [('ps', bnk)])
        ACT(lambda e: e.activation(out=dst, in_=psb[bnk][:, :], func=AF.Copy), [('ps', bnk)], ['V'])

    for t in range(NKT):
        src = xnT[:, :, t * 128:(t + 1) * 128] if t < NKT - 1 else xnT_halo[:]
        vproj(src, A['V'][:, t, :])
    for t in range(2):
        vproj(xnT_ctx[:, :, t * 128:(t + 1) * 128], A['Vc'][:, t, :])

    def rope(bnk, n, col0, dst):
        ACT(lambda e: e.activation(out=A['qf'][:, 0:n], in_=psb[bnk][:, 0:n], func=AF.Copy), [('ps', bnk)], ['qf'])
        PE(lambda e: e.matmul(psb[4][:, 0:n], lhsT=A['rotT'], rhs=A['qf'][:, 0:n], start=True, stop=True),
           ['qf', 'rotT'], [('ps', 4)])
        DVE(lambda e: e.tensor_tensor(out=A['t1'][:, 0:n], in0=A['qf'][:, 0:n], in1=A['cos'][:, col0:col0 + n], op=ALU.mult),
            ['qf', 'cos'], ['t1'])
        DVE(lambda e: e.tensor_tensor(out=A['t2'][:, 0:n], in0=psb[4][:, 0:n], in1=A['sin'][:, col0:col0 + n], op=ALU.mult),
            [('ps', 4), 'sin'], ['t2'])
        return ['t1', 't2'], dst

    for g in range(4 if stage != 2 else 1):
        DVE(lambda e, g=g: e.tensor_copy(out=A['sinkrow'].rearrange("p (i t) -> p i t", t=128),
                                         in_=A['sinkexp'][:, 4 * g:4 * g + 4].unsqueeze(2).to_broadcast([128, 4, 128])),
            ['sinkexp'], ['sinkrow'])
        sk = load_att(g, 4)
        for tb in range(5):
            n = 512 if tb < 4 else 128
            src = xnT[:, :, tb * 512:(tb + 1) * 512] if tb < 4 else xnT_halo[:]
            bk = aproj(sk, src, n)
            rope(bk, n, tb * 512, None)
            DVE(lambda e, tb=tb, n=n: e.tensor_tensor(out=A['kT'][:, tb * 512:tb * 512 + n], in0=A['t1'][:, 0:n],
                                                      in1=A['t2'][:, 0:n], op=ALU.add), ['t1', 't2'], ['kT'])
        bk = aproj(sk, xnT_ctx[:], LCTX)
        ACT(lambda e, bk=bk: e.activation(out=A['kcT'], in_=psb[bk][:, 0:LCTX], func=AF.Copy), [('ps', bk)], ['kcT'])
        for i in range(4):
            sq_ = load_att(g, i)
            for tb in range(4):
                xs = xnT[:, :, tb * 512:(tb + 1) * 512]
                qT4 = A['qT_all'][:, tb * 2048:(tb + 1) * 2048].rearrange("p (q i t) -> p q i t", q=4, i=4)
                bq = aproj(sq_, xs, 512)
                rope(bq, 512, tb * 512, None)
                DVE(lambda e, i=i, qT4=qT4: e.tensor_tensor(out=qT4[:, :, i, :], in0=A['t1'].rearrange("p (q t) -> p q t", t=128),
                                                            in1=A['t2'].rearrange("p (q t) -> p q t", t=128), op=ALU.add),
                    ['t1', 't2'], [('qT', tb), 'wv'])
        for i in range(4):
            sg = load_att(g, 5 + i)
            for tb in range(4):
                xs = xnT[:, :, tb * 512:(tb + 1) * 512]
                gT4 = A['gT_all'][:, tb * 2048:(tb + 1) * 2048].rearrange("p (q i t) -> p q i t", q=4, i=4)
                bg = aproj(sg, xs, 512)
                ACT(lambda e, i=i, bg=bg, gT4=gT4: e.activation(out=gT4[:, :, i, :], in_=psb[bg][:, :].rearrange("p (q t) -> p q t", t=128),
                                                                func=AF.Silu), [('ps', bg)], [('gT', tb)])

        def attn(tb):
            ao4 = A['ao'].rearrange("p (q i t) -> p q i t", q=4, i=4)
            for qb in range(4):
                Q = tb * 4 + qb
                kbs = []
                if Q >= 1:
                    kbs.append((A['kT'][:, (Q - 1) * 128:Q * 128], A['V'][:, Q - 1, g * 128:(g + 1) * 128], 0))
                kbs.append((A['kT'][:, Q * 128:(Q + 1) * 128], A['V'][:, Q, g * 128:(g + 1) * 128], None))
                kbs.append((A['kT'][:, (Q + 1) * 128:(Q + 2) * 128], A['V'][:, Q + 1, g * 128:(g + 1) * 128], 1))
                for t in range(2):
                    kbs.append((A['kcT'][:, t * 128:(t + 1) * 128], A['Vc'][:, t, g * 128:(g + 1) * 128], None))
                qrhs = A['qT_all'][:, tb * 2048 + qb * 512:tb * 2048 + (qb + 1) * 512]
                bO = 7 if Q % 2 == 0 else 0
                bD = 3 if Q % 2 == 0 else 1
                nk = len(kbs)

                def emit_S(ki):
                    kap, vap, mk = kbs[ki]
                    sl = (Q * 5 + ki) % 2
                    PE(lambda e, kap=kap, sl=sl, mk=mk, qrhs=qrhs: e.matmul(psb[5 + sl][:, :], lhsT=kap, rhs=qrhs, start=True, stop=(mk is None)),
                       ['kT', 'kcT', ('qT', tb)], [('ps', 5 + sl)])
                    if mk is not None:
                        PE(lambda e, sl=sl, mk=mk: e.matmul(psb[5 + sl][:, :], lhsT=identb[:], rhs=A['bm'][:, mk * 512:(mk + 1) * 512],
                                                            start=False, stop=True), ['identb', 'bm'], [('ps', 5 + sl)])

                emit_S(0)
                for ki in range(nk):
                    kap, vap, mk = kbs[ki]
                    sl = (Q * 5 + ki) % 2
                    if ki + 1 < nk:
                        emit_S(ki + 1)
                    ACT(lambda e, sl=sl: e.activation(out=A['pT'][sl], in_=psb[5 + sl][:, :], func=AF.Exp, scale=QSCALE),
                        [('ps', 5 + sl)], [('pT', sl)])
                    first, last = ki == 0, ki == nk - 1
                    PE(lambda e, vap=vap, sl=sl, first=first, last=last, bO=bO: e.matmul(psb[bO][:, :], lhsT=vap, rhs=A['pT'][sl],
                                                                                   start=first, stop=last),
                       ['V', ('pT', sl)], [('ps', bO)])
                    PE(lambda e, sl=sl, first=first, last=last, bD=bD: e.matmul(psb[bD][:, :], lhsT=onesb[:], rhs=A['pT'][sl],
                                                                          start=first, stop=last),
                       ['onesb', ('pT', sl)], [('ps', bD)])
                DVE(lambda e, bD=bD: e.tensor_tensor(out=A['den'], in0=psb[bD][:, :], in1=A['sinkrow'], op=ALU.add),
                    [('ps', bD), 'sinkrow'], ['den'])
                ACT(lambda e: e.activation(out=A['den'], in_=A['den'], func=AF.Ln), ['den'], ['den'])
                ACT(lambda e: e.activation(out=A['den'], in_=A['den'], func=AF.Exp, scale=-1.0), ['den'], ['den'])
                DVE(lambda e, bO=bO: e.tensor_tensor(out=A['o1'], in0=psb[bO][:, :], in1=A['den'], op=ALU.mult),
                    [('ps', bO), 'den'], ['o1'])
                DVE(lambda e, qb=qb: e.tensor_tensor(out=A['ao'][:, qb * 512:(qb + 1) * 512], in0=A['o1'],
                                                     in1=A['gT_all'][:, tb * 2048 + qb * 512:tb * 2048 + (qb + 1) * 512], op=ALU.mult),
                    ['o1', ('gT', tb)], ['ao'])
            for i in range(4):
                hh = 4 * g + i
                DMA('sp', lambda e, hh=hh, i=i, tb=tb: e.dma_start(
                    out=att_spill[hh * 128:(hh + 1) * 128, tb * 512:(tb + 1) * 512].rearrange("p (q t) -> p q t", t=128),
                    in_=ao4[:, :, i, :]), ['ao'], ['att_spill'], 'atsp')

        for tb in range(4):
            attn(tb)
    P.barrier()

    if stage == 2:
        dbg_at = dout("dbg_at", [512, NOWN], BF16)
        DMA('sp', lambda e: e.dma_start(out=dbg_at, in_=att_spill[0:512, :]), ['att_spill'], ['o1'], 'o1')
        P.add('sp', lambda e: None, ['o1'], [])
        P.emit(es)
        es.close()
        return nc

    w_fm = din("w_fm", [16, 4, 128, KC * 128])
    w_outd = din("w_out_l", [8, 128, KC * 256])
    bm_d = din("bmerge_fm", [128, 32])
    fg_d = din("fgain", [1, D])
    y_out = dout("y", [NOWN, D])
    TB4 = 512
    NB4 = NOWN // TB4
    NT4 = TB4 // 128

    a_reset()
    F = {}
    F['wh'] = [a_bf16(KC * 128).rearrange("p (k c) -> p k c", k=KC) for _ in range(3)]
    F['wo'] = [a_bf16(KC * 256).rearrange("p (k c) -> p k c", k=KC) for _ in range(2)]
    F['hgT'] = a_bf16(KC * TB4).rearrange("p (k t) -> p k t", k=KC)
    F['atT'] = a_bf16(KC * TB4).rearrange("p (k t) -> p k t", k=KC)
    F['yT'] = a_bf16(KC * TB4).rearrange("p (k t) -> p k t", k=KC)
    F['hrow'] = [a_f32(D) for _ in range(2)]
    F['gate'] = a_f32(D)
    F['fg'] = a_f32(D)
    F['sa'] = a_f32(TB4)
    F['sb'] = a_f32(TB4)
    F['tmp'] = a_f32(512)
    F['junk'] = F['hgT'].rearrange("p k t -> p (k t)")[:, 0:D]
    F['ssq'] = [a_f32(1) for _ in range(2)]
    F['rstd'] = [a_f32(1) for _ in range(2)]
    F['bm'] = a_f32(32)
    DMA('sp', lambda e: e.dma_start(out=F['gate'], in_=gate_dram.partition_broadcast(128)), ['gate_dram'], ['gate_bc'], 'p4a')
    DMA('sp', lambda e: e.dma_start(out=F['fg'], in_=fg_d.partition_broadcast(128)), [], ['fg'], 'p4b')
    DMA('sp', lambda e: e.dma_start(out=F['bm'], in_=bm_d), [], ['bmf4'], 'p4c')

    fwctr = [0]
    fpctr = [0]
    owctr = [0]
    for tb in range(NB4):
        t0 = tb * TB4
        DMA('sp', lambda e, t0=t0: e.dma_start(out=F['hgT'], in_=hg_spill[:, t0:t0 + TB4].rearrange("(k p) t -> p k t", p=128)),
            ['hg_spill'], ['hgT'], 'p4h')
        DMA('act', lambda e, t0=t0: e.dma_start(out=F['atT'], in_=att_spill[:, t0:t0 + TB4].rearrange("(k p) t -> p k t", p=128)),
            ['att_spill'], ['atT'], 'p4t')
        xs = xnT[:, :, t0:t0 + TB4]
        for c in range(16):
            bnks = []
            for u, src, sres in ((0, F['hgT'], 'hgT'), (1, xs, 'xnT'), (2, F['atT'], 'atT'), (3, xs, 'xnT')):
                slot = fwctr[0] % 3
                fwctr[0] += 1
                DMA('pool', lambda e, slot=slot, c=c, u=u: e.dma_start(out=F['wh'][slot].rearrange("p k c -> p (k c)"),
                                                                       in_=w_fm[c, u]), [], [('wh', slot)], 'wh%d' % slot)
                bnk = fpctr[0] % 4
                fpctr[0] += 1
                for kc in range(KC):
                    PE(lambda e, kc=kc, slot=slot, bnk=bnk, src=src: e.matmul(psb[bnk][:, 0:TB4], lhsT=F['wh'][slot][:, kc, :],
                                                                              rhs=src[:, kc, :], start=(kc == 0), stop=(kc == KC - 1)),
                       [('wh', slot), sres], [('ps', bnk)])
                bnks.append(bnk)
            ACT(lambda e, b=bnks[1], c=c: e.activation(out=F['sa'], in_=psb[b][:, 0:TB4], func=AF.Sigmoid, bias=F['bm'][:, c:c + 1]),
                [('ps', bnks[1]), 'bmf4'], ['sa'])
            ACT(lambda e, b=bnks[3], c=c: e.activation(out=F['sb'], in_=psb[b][:, 0:TB4], func=AF.Sigmoid, bias=F['bm'][:, 16 + c:17 + c]),
                [('ps', bnks[3]), 'bmf4'], ['sb'])
            DVE(lambda e, b=bnks[0]: e.tensor_tensor(out=F['sa'], in0=psb[b][:, 0:TB4], in1=F['sa'], op=ALU.mult),
                [('ps', bnks[0]), 'sa'], ['sa'])
            DVE(lambda e, b=bnks[2]: e.tensor_tensor(out=F['sb'], in0=psb[b][:, 0:TB4], in1=F['sb'], op=ALU.mult),
                [('ps', bnks[2]), 'sb'], ['sb'])
            DVE(lambda e, c=c: e.tensor_tensor(out=F['yT'][:, c, :], in0=F['sa'], in1=F['sb'], op=ALU.add), ['sa', 'sb'], ['yT'])
        for th in range(NT4 // 2):
            for t2 in range(2):
                tt = th * 2 + t2
                DMA('sp' if t2 == 0 else 'act',
                    lambda e, t2=t2, tt=tt, t0=t0: e.dma_start(out=F['hrow'][t2], in_=x_loc[t0 + tt * 128:t0 + (tt + 1) * 128, :]),
                    [], [('hrow', t2)], 'p4x%d' % t2)
            for cb in range(8):
                os_ = owctr[0] % 2
                owctr[0] += 1
                DMA('pool', lambda e, os_=os_, cb=cb: e.dma_start(out=F['wo'][os_].rearrange("p k c -> p (k c)"), in_=w_outd[cb]),
                    [], [('wo', os_)], 'wo%d' % os_)
                for t2 in range(2):
                    tt = th * 2 + t2
                    bnk = 4 + (cb * 2 + t2) % 4
                    for kc in range(KC):
                        PE(lambda e, kc=kc, tt=tt, os_=os_, bnk=bnk: e.matmul(psb[bnk][:, 0:256], lhsT=F['yT'][:, kc, tt * 128:(tt + 1) * 128],
                                                                              rhs=F['wo'][os_][:, kc, :], start=(kc == 0), stop=(kc == KC - 1)),
                           ['yT', ('wo', os_)], [('ps', bnk)])
                    DVE(lambda e, bnk=bnk, cb=cb: e.tensor_tensor(out=F['tmp'][:, 0:256], in0=psb[bnk][:, 0:256], in1=F['gate'][:, cb * 256:(cb + 1) * 256],
                                                                  op=ALU.mult), [('ps', bnk), 'gate_bc'], ['tmp4'])
                    DVE(lambda e, t2=t2, cb=cb: e.tensor_tensor(out=F['hrow'][t2][:, cb * 256:(cb + 1) * 256],
                                                                in0=F['hrow'][t2][:, cb * 256:(cb + 1) * 256], in1=F['tmp'][:, 0:256], op=ALU.add),
                        ['tmp4', ('hrow', t2)], [('hrow', t2)])
            for t2 in range(2):
                tt = th * 2 + t2
                hr = F['hrow'][t2]
                DVE(lambda e, t2=t2: e.memset(F['ssq'][t2], 0.0), [], [('ssq4', t2)])
                ACT(lambda e, t2=t2, hr=hr: e.activation(out=F['junk'], in_=hr, func=AF.Square, accum_out=F['ssq'][t2]),
                    [('hrow', t2), ('ssq4', t2)], [('ssq4', t2), 'hgT'])
                DVE(lambda e, t2=t2: e.tensor_scalar(out=F['rstd'][t2], in0=F['ssq'][t2], scalar1=1.0 / D, scalar2=EPS,
                                                     op0=ALU.mult, op1=ALU.add), [('ssq4', t2)], [('rstd4', t2)])
                ACT(lambda e, t2=t2: e.sqrt(out=F['rstd'][t2], in_=F['rstd'][t2]), [('rstd4', t2)], [('rstd4', t2)])
                DVE(lambda e, t2=t2: e.reciprocal(out=F['rstd'][t2], in_=F['rstd'][t2]), [('rstd4', t2)], [('rstd4', t2)])
                DVE(lambda e, t2=t2, hr=hr: e.scalar_tensor_tensor(out=hr, in0=hr, scalar=F['rstd'][t2][:, 0:1], in1=F['fg'],
                                                                   op0=ALU.mult, op1=ALU.mult),
                    [('hrow', t2), ('rstd4', t2), 'fg'], [('hrow', t2)])
                DMA('sp', lambda e, tt=tt, hr=hr, t0=t0: e.dma_start(out=y_out[t0 + tt * 128:t0 + (tt + 1) * 128, :], in_=hr),
                    [('hrow', t2)], [('y_out', t2)], 'p4y%d' % t2)
    P.add('sp', lambda e: None, [('y_out', t2) for t2 in range(2)], [])

    P.emit(es)
    es.close()
    return nc


def prep_inputs(inp):
    f = np.float32
    x = np.asarray(inp['x'], f)
    ctx = np.asarray(inp['ctx'], f)
    c = np.asarray(inp['c'], f)
    c_ctx = np.asarray(inp['c_ctx'], f)
    w_ada = np.asarray(inp['w_ada'], f)[0]
    b_ada = np.asarray(inp['b_ada'], f)[0]
    w_ada_l = np.ascontiguousarray(w_ada.reshape(KC, 128, 3 * D).transpose(1, 0, 2))
    b_ada2 = np.ascontiguousarray(np.stack([b_ada, b_ada], 0))
    gain_fm = np.ascontiguousarray(np.asarray(inp['norm_gain'], f)[0].reshape(KC, 128).T)
    ident = np.eye(128, dtype=f)
    sel = np.zeros((2, 128), f)
    sel[0, :] = 1.0
    w_in = np.asarray(inp['w_in'], f)[0]
    OFF_HG_Q, OFF_HG_FF, OFF_HG_FB, OFF_HG_I, OFF_HG_G = 5120, 7168, 9216, 11264, 13312

    def unit(col0, ncol=128):
        return w_in[:, col0:col0 + ncol].reshape(KC, 128, ncol).transpose(1, 0, 2).reshape(128, KC * ncol)

    def fm(v):
        return np.ascontiguousarray(v.reshape(16, 128).T)

    w_h_s = []
    lbl_s = []
    lf = np.asarray(inp['lb_logits_fwd'], f)
    lbk = np.asarray(inp['lb_logits_bwd'], f)
    for s in range(2):
        offA, offB = (OFF_HG_FF, OFF_HG_FB) if s == 0 else (OFF_HG_FB, OFF_HG_FF)
        wh = np.empty((16, 5, 128, KC * 128), f)
        for h in range(16):
            for u, off in enumerate((OFF_HG_Q, offA, offB, OFF_HG_I, OFF_HG_G)):
                wh[h, u] = unit(off + h * 128)
        w_h_s.append(wh)
        la, lb_ = (lf, lbk) if s == 0 else (lbk, lf)
        arr = np.stack([np.stack([fm(la[0]), fm(la[1])], 1), np.stack([fm(lb_[0]), fm(lb_[1])], 1)], 1)
        lbl_s.append(np.ascontiguousarray(arr.reshape(128, 64)))
    hgain_fm = fm(np.asarray(inp['hgrn_norm_gain'], f)[0])
    mreset = np.ones((128, 512), f)
    mreset[:, ::64] = 0.0
    ss, tt = np.meshgrid(np.arange(64), np.arange(64), indexing='ij')
    maskab = np.concatenate([(ss <= tt).astype(f), (ss >= tt).astype(f)], 1)
    OFF_ATT_Q, OFF_ATT_K, OFF_ATT_V, OFF_ATT_G, OFF_MERGE = 0, 2048, 2560, 3072, 15360
    w_att = np.empty((4, 9, 128, KC * 128), f)
    for g in range(4):
        for u in range(4):
            w_att[g, u] = unit(OFF_ATT_Q + (4 * g + u) * 128)
            w_att[g, 5 + u] = unit(OFF_ATT_G + (4 * g + u) * 128)
        w_att[g, 4] = unit(OFF_ATT_K + g * 128)
    w_v = np.ascontiguousarray(unit(OFF_ATT_V, 512))
    w_o_hgrn = np.asarray(inp['w_o_hgrn'], f)[0]
    w_o_attn = np.asarray(inp['w_o_attn'], f)[0]
    w_out = np.asarray(inp['w_out'], f)[0]

    def unit_of(w, col0, ncol=128):
        return w[:, col0:col0 + ncol].reshape(KC, 128, ncol).transpose(1, 0, 2).reshape(128, KC * ncol)

    w_fm = np.empty((16, 4, 128, KC * 128), f)
    for cc in range(16):
        w_fm[cc, 0] = unit_of(w_o_hgrn, cc * 128)
        w_fm[cc, 1] = unit(OFF_MERGE + cc * 128)
        w_fm[cc, 2] = unit_of(w_o_attn, cc * 128)
        w_fm[cc, 3] = unit(OFF_MERGE + D + cc * 128)
    w_out_l = np.stack([unit_of(w_out, cb * 256, 256) for cb in range(8)], 0)
    bmg = np.asarray(inp['b_merge'], f)[0]
    bmerge_fm = np.ascontiguousarray(np.concatenate([fm(bmg[0]), fm(bmg[1])], 1))
    fgain = np.ascontiguousarray(np.asarray(inp['final_norm_gain'], f).reshape(1, D))
    sink = np.ascontiguousarray(np.asarray(inp['sink_logits'], f).reshape(1, 16))
    rotT = np.zeros((128, 128), f)
    for dq in range(128):
        if (dq // 32) % 2 == 0:
            rotT[dq + 32, dq] = -1.0
        else:
            rotT[dq - 32, dq] = 1.0
    aa, bb = np.meshgrid(np.arange(128), np.arange(128), indexing='ij')
    bm0 = np.tile(np.where(bb <= aa, 0.0, -30000.0).astype(f), (1, 4))
    bm1 = np.tile(np.where(aa <= bb, 0.0, -30000.0).astype(f), (1, 4))
    bmask = np.ascontiguousarray(np.concatenate([bm0, bm1], 1))
    inv_freq = (1.0 / (np.float32(10000.0) ** (np.arange(0, 64, 2, dtype=f) / np.float32(64)))).astype(f)
    rope_tabs = []
    for s in range(2):
        ii = np.arange(NOWN + 128)
        jj = ii if s == 0 else 4095 - ii
        row = (jj // 64).astype(f)
        colp = (jj % 64).astype(f)
        ang_r = row[:, None] * inv_freq[None, :]
        ang_c = colp[:, None] * inv_freq[None, :]
        ang = np.concatenate([ang_r, ang_r, ang_c, ang_c], -1).astype(f)
        rope_tabs.append((np.ascontiguousarray(np.cos(ang).T.astype(f)), np.ascontiguousarray(np.sin(ang).T.astype(f))))
    maps = []
    for core in range(8):
        b, s = core // 2, core % 2
        xb = x[b]
        cb = ctx[b]
        if s == 1:
            xb = xb[::-1]
            cb = cb[::-1]
        cv = np.stack([c[b].reshape(KC, 128).T, c_ctx.reshape(KC, 128).T], -1).reshape(128, KC * 2)
        maps.append(dict(
            x_loc=np.ascontiguousarray(xb), ctx_loc=np.ascontiguousarray(cb),
            cvec=np.ascontiguousarray(cv), w_ada_l=w_ada_l, b_ada2=b_ada2, gain_fm=gain_fm,
            ident=ident, sel=sel, w_h=w_h_s[s], lbl=lbl_s[s], hgain_fm=hgain_fm, mreset=mreset, maskab=maskab,
            w_att=w_att, w_v=w_v, cosT=rope_tabs[s][0], sinT=rope_tabs[s][1], rotT=rotT, sink=sink, bmask=bmask,
            w_fm=w_fm, w_out_l=w_out_l, bmerge_fm=bmerge_fm, fgain=fgain))
    return maps


def kernel(**inputs):
    maps = prep_inputs(inputs)
    nc = build()
    res = run_bass_kernel_spmd(nc, maps, core_ids=list(range(8)))
    out = np.zeros((4, 4096, D), np.float32)
    for core in range(8):
        b, s = core // 2, core % 2
        y = res.results[core]["y"]
        if s == 0:
            out[b, :NOWN] = y
        else:
            out[b, NOWN:] = y[::-1]
    return out
```

```python
import numpy as np
from contextlib import ExitStack
import concourse.bass as bass
import concourse.mybir as mybir
from concourse.bass_utils import run_bass_kernel_spmd

F32 = mybir.dt.float32
BF16 = mybir.dt.bfloat16
AF = mybir.ActivationFunctionType
ALU = mybir.AluOpType

D = 2048
KC = 16
NOWN = 2048
NOTH = 2048
LCTX = 256
EPS = 1e-6
DBG_HEADS = 2


class Prog:
    def __init__(self, nc):
        self.nc = nc
        self.ops = []

    def add(self, eng, fn, r=(), w=(), dma=None):
        self.ops.append(dict(eng=eng, fn=fn, r=tuple(r), w=tuple(w), dma=dma,
                             deps=[], signal=False, sig=None))

    def barrier(self):
        self.add('sp', lambda e: None, [], ['__bar'])
        self.ops[-1]['barrier'] = True
        for eng in ('pe', 'act', 'dve', 'pool'):
            self.add(eng, lambda e: None, ['__bar'], [])

    def capture(self):
        self._saved = self.ops
        self.ops = []

    def release(self):
        got = self.ops
        self.ops = self._saved
        return got

    def interleave(self, a, b):
        n = max(len(a), len(b))
        for i in range(n):
            if i < len(a):
                self.ops.append(a[i])
            if i < len(b):
                self.ops.append(b[i])

    def analyze(self):
        last_w = {}
        readers = {}
        ops = self.ops
        last_by_q = {}
        for i, op in enumerate(ops):
            deps = set()
            if op.get('barrier'):
                deps.update(last_by_q.values())
            last_by_q[('dma', op['dma']) if op['dma'] is not None else ('eng', op['eng'])] = i
            for res in op['r']:
                if res in last_w:
                    deps.add(last_w[res])
            for res in op['w']:
                if res in last_w:
                    deps.add(last_w[res])
                deps.update(readers.get(res, ()))
            final = []
            for d in deps:
                if d == i:
                    continue
                dop = ops[d]
                if dop['eng'] == op['eng'] and dop['dma'] is None and op['dma'] is None:
                    if op['eng'] == 'pe':
                        continue
                    if not (set(dop['w']) & set(op['r'])):
                        continue
                final.append(d)
            best = {}
            keep = []
            for d in final:
                if ops[d]['dma'] is None:
                    k = ops[d]['eng']
                    if k not in best or d > best[k]:
                        best[k] = d
                else:
                    keep.append(d)
            final = keep + list(best.values())
            op['deps'] = final
            for d in final:
                ops[d]['signal'] = True
            for res in op['w']:
                last_w[res] = i
                readers[res] = []
            for res in op['r']:
                if res not in op['w']:
                    readers.setdefault(res, []).append(i)
        cnt = {}
        for op in ops:
            if op['dma'] is not None:
                op['signal'] = True
            if op['signal']:
                key = ('dma', op['dma']) if op['dma'] is not None else ('eng', op['eng'])
                inc = 16 if op['dma'] is not None else 1
                cnt[key] = cnt.get(key, 0) + inc
                op['sig'] = (key, cnt[key])
        self.keys = list(cnt.keys())

    def emit(self, es):
        nc = self.nc
        self.analyze()
        sems = {}
        for n, k in enumerate(self.keys):
            sems[k] = es.enter_context(nc.semaphore("s%d" % n))
        block = es.enter_context(nc.Block())
        ops = self.ops

        def run(engname):
            def body(e):
                waited = {}
                for op in ops:
                    if op['eng'] != engname:
                        continue
                    need = {}
                    for d in op['deps']:
                        k, v = ops[d]['sig']
                        if v > need.get(k, 0):
                            need[k] = v
                    for k, v in need.items():
                        if v > waited.get(k, 0):
                            e.wait_ge(sems[k], v)
                            waited[k] = v
                    ins = op['fn'](e)
                    if op['signal']:
                        k, v = op['sig']
                        if ins is None:
                            ins = e.nop()
                        ins.then_inc(sems[k], 16 if op['dma'] is not None else 1)
            return body

        block.tensor(run('pe'))
        block.scalar(run('act'))
        block.vector(run('dve'))
        block.gpsimd(run('pool'))
        block.sync(run('sp'))


def build(stage=99):
    nc = bass.Bass("TRN2", target_bir_lowering=False)
    es = ExitStack()
    es.enter_context(nc.allow_low_precision("bf16 matmul operands, fp32 accumulation"))
    P = Prog(nc)

    def din(name, shape, dt=F32):
        return nc.dram_tensor(name, list(shape), dt, kind="ExternalInput").ap()

    def dout(name, shape, dt=F32):
        return nc.dram_tensor(name, list(shape), dt, kind="ExternalOutput").ap()

    def sb(name, shape, dt=F32):
        return es.enter_context(nc.sbuf_tensor(name, list(shape), dt))

    PE = lambda fn, r, w: P.add('pe', fn, r, w)
    ACT = lambda fn, r, w: P.add('act', fn, r, w)
    DVE = lambda fn, r, w: P.add('dve', fn, r, w)
    POOL = lambda fn, r, w: P.add('pool', fn, r, w)
    DMA = lambda q, fn, r, w, key: P.add(q, fn, r, w, dma=key)

    x_loc = din("x_loc", [NOWN + NOTH, D])
    ctx_loc = din("ctx_loc", [LCTX, D])
    cvec = din("cvec", [128, KC * 2])
    w_ada_l = din("w_ada_l", [128, KC, 3 * D])
    b_ada2 = din("b_ada2", [2, 3 * D])
    gain_fm = din("gain_fm", [128, KC])
    ident_d = din("ident", [128, 128])
    sel_d = din("sel", [2, 128])

    psb = [es.enter_context(nc.psum_tensor("ps%d" % i, [128, 512], F32)) for i in range(8)]

    ident = sb("ident_sb", [128, 128])
    sel = sb("sel_sb", [2, 128])
    DMA('sp', lambda e: e.dma_start(out=ident[:], in_=ident_d), [], ['ident'], 'c0')
    DMA('sp', lambda e: e.dma_start(out=sel[:], in_=sel_d), [], ['sel'], 'c1')

    w_h = din("w_h", [16, 5, 128, KC * 128])
    lbl_d = din("lbl", [128, 64])
    hgain_d = din("hgain_fm", [128, 16])
    mreset_d = din("mreset", [128, 512])
    maskab_d = din("maskab", [64, 128])
    gate_dram = nc.dram_tensor("gate_scr", [1, D], F32).ap()
    hg_spill = nc.dram_tensor("hg_spill", [D, NOWN], BF16).ap()

    ARENA_W = 29 * 1024
    arena = sb("arena", [128, ARENA_W])
    apos = [0]

    def a_reset():
        apos[0] = 0

    def a_f32(n):
        o = apos[0]
        apos[0] += n
        assert apos[0] <= ARENA_W, apos[0]
        return arena[:, o:o + n]

    def a_bf16(n):
        w = (n + 1) // 2
        return a_f32(w).bitcast(BF16)

    identb = sb("identb", [128, 128], BF16)
    onesb = sb("onesb", [128, 128], BF16)
    mreset = sb("mreset_sb", [128, 512])
    maskab = sb("maskab_sb", [64, 128])
    lbl = sb("lbl_sb", [128, 64])
    lb = sb("lb_sb", [128, 32])
    negoml = sb("negoml", [128, 32])
    oml = sb("oml", [128, 32])
    hgain = sb("hgain_sb", [128, 16])
    gfm = sb("gfm", [128, KC])
    modcol = sb("modcol", [128, 64])
    Avec = sb("Avec", [128, 2, KC])
    Bvec = sb("Bvec", [128, 2, KC])
    xnT = sb("xnT_main", [128, KC, NOWN], BF16)
    xnT_ctx = sb("xnT_ctx", [128, KC, LCTX], BF16)
    xnT_halo = sb("xnT_halo", [128, KC, 128], BF16)
    SinitB = sb("SinitB", [128, 16, 128])
    c_sb = sb("c_sb", [128, KC * 2])
    sc_sb = sb("sc_sb", [128, KC * 2])

    DMA('sp', lambda e: e.dma_start(out=mreset[:], in_=mreset_d), [], ['mreset'], 'c5')
    DMA('sp', lambda e: e.dma_start(out=maskab[:], in_=maskab_d), [], ['maskab'], 'c6')
    DMA('sp', lambda e: e.dma_start(out=lbl[:], in_=lbl_d), [], ['lbl'], 'c7')
    DMA('sp', lambda e: e.dma_start(out=hgain[:], in_=hgain_d), [], ['hgain'], 'c8')
    DMA('sp', lambda e: e.dma_start(out=c_sb[:], in_=cvec), [], ['c_sb'], 'c2')
    DMA('sp', lambda e: e.dma_start(out=gfm[:], in_=gain_fm), [], ['gfm'], 'c4')
    DVE(lambda e: e.tensor_copy(out=identb[:], in_=ident[:]), ['ident'], ['identb'])
    DVE(lambda e: e.memset(onesb[:], 1.0), [], ['onesb'])
    lbl4 = lbl[:].rearrange("p (d l h) -> p d l h", d=2, l=2)
    lb3 = lb[:].rearrange("p (d h) -> p d h", d=2)
    DVE(lambda e: e.tensor_tensor(out=lb3, in0=lbl4[:, :, 0, :], in1=lbl4[:, :, 1, :], op=ALU.subtract), ['lbl'], ['lb'])
    ACT(lambda e: e.activation(out=lb[:], in_=lb[:], func=AF.Sigmoid), ['lb'], ['lb'])
    DVE(lambda e: e.tensor_scalar(out=negoml[:], in0=lb[:], scalar1=-1.0, scalar2=None, op0=ALU.add), ['lb'], ['negoml'])
    DVE(lambda e: e.tensor_scalar(out=oml[:], in0=lb[:], scalar1=-1.0, scalar2=1.0, op0=ALU.mult, op1=ALU.add), ['lb'], ['oml'])

    ACT(lambda e: e.activation(out=sc_sb[:], in_=c_sb[:], func=AF.Silu), ['c_sb'], ['sc_sb'])
    a_reset()
    wada = [a_f32(KC * 512).rearrange("p (k c) -> p k c", k=KC) for _ in range(3)]
    badab = [a_f32(512) for _ in range(2)]
    mod2b = [a_f32(512) for _ in range(2)]
    for blk in range(12):
        s = blk % 2
        ws = blk % 3
        DMA('sp' if blk % 2 == 0 else 'act',
            lambda e, ws=ws, blk=blk: e.dma_start(out=wada[ws], in_=w_ada_l[:, :, blk * 512:(blk + 1) * 512]),
            [], [('wada', ws)], 'wada%d' % ws)
        DMA('sp', lambda e, s=s, blk=blk: e.dma_start(out=badab[s][0:2, :], in_=b_ada2[:, blk * 512:(blk + 1) * 512]),
            [], [('badab', s)], 'badab%d' % s)
        for kc in range(KC):
            PE(lambda e, s=s, ws=ws, kc=kc: e.matmul(psb[s][0:2, :], lhsT=sc_sb[:, 2 * kc:2 * kc + 2],
                                              rhs=wada[ws][:, kc, :], start=(kc == 0), stop=(kc == KC - 1)),
               ['sc_sb', ('wada', ws)], [('ps', s)])
        DVE(lambda e, s=s: e.tensor_tensor(out=mod2b[s][0:2, :], in0=psb[s][0:2, :], in1=badab[s][0:2, :], op=ALU.add),
            [('ps', s), ('badab', s)], [('mod2b', s)])
        if blk < 8:
            for jj in range(4):
                j = blk * 4 + jj
                PE(lambda e, s=s, j=j, jj=jj: e.matmul(psb[2][:, 2 * j:2 * j + 2], lhsT=mod2b[s][0:2, jj * 128:(jj + 1) * 128],
                                                       rhs=ident[0:2, 0:2], start=True, stop=True),
                   [('mod2b', s), 'ident'], [('ps', 2)])
        else:
            DMA('sp', lambda e, s=s, blk=blk: e.dma_start(out=gate_dram[0:1, (blk - 8) * 512:(blk - 7) * 512], in_=mod2b[s][0:1, :]),
                [('mod2b', s)], ['gate_dram'], 'gd%d' % s)
    DVE(lambda e: e.tensor_copy(out=modcol[:], in_=psb[2][:, 0:64]), [('ps', 2)], ['modcol'])
    mc3 = modcol[:].rearrange("p (j t) -> p j t", t=2)
    for t in range(2):
        DVE(lambda e, t=t: e.scalar_tensor_tensor(out=Avec[:, t, :], in0=mc3[:, 16:32, t], scalar=1.0, in1=gfm[:],
                                                  op0=ALU.add, op1=ALU.mult), ['modcol', 'gfm'], ['Avec'])
        DVE(lambda e, t=t: e.tensor_copy(out=Bvec[:, t, :], in_=mc3[:, 0:16, t]), ['modcol'], ['Bvec'])
    P.barrier()

    def alloc_xt():
        a_reset()
        d = dict(xt=[a_f32(D) for _ in range(3)], junk=a_bf16(D),
                 ssq=[a_f32(1) for _ in range(3)], rstd=[a_f32(1) for _ in range(3)])
        return d

    def xn_stats(X, n, src_rows):
        s = n % 3
        xt, junk, ssq, rstd = X['xt'], X['junk'], X['ssq'], X['rstd']
        DMA('sp', lambda e: e.dma_start(out=xt[s], in_=src_rows), [], [('xt', s)], 'xt%d' % s)
        DVE(lambda e: e.memset(ssq[s], 0.0), [], [('ssq', s)])
        ACT(lambda e: e.activation(out=junk, in_=xt[s], func=AF.Square, accum_out=ssq[s]),
            [('xt', s), ('ssq', s)], [('ssq', s), 'junk'])
        DVE(lambda e: e.tensor_scalar(out=rstd[s], in0=ssq[s], scalar1=1.0 / D, scalar2=EPS,
                                      op0=ALU.mult, op1=ALU.add), [('ssq', s)], [('rstd', s)])
        ACT(lambda e: e.sqrt(out=rstd[s], in_=rstd[s]), [('rstd', s)], [('rstd', s)])
        DVE(lambda e: e.reciprocal(out=rstd[s], in_=rstd[s]), [('rstd', s)], [('rstd', s)])
        DVE(lambda e: e.tensor_scalar(out=xt[s], in0=xt[s], scalar1=rstd[s][:, 0:1], scalar2=None, op0=ALU.mult),
            [('xt', s), ('rstd', s)], [('xt', s)])

    def xn_transpose(X, n, dst, dst_res, which):
        s = n % 3
        pb = n % 2
        xt = X['xt']
        for j in range(KC):
            bnk = j // 4 + 4 * pb
            PE(lambda e, j=j, bnk=bnk: e.transpose(out=psb[bnk][:, (j % 4) * 128:(j % 4 + 1) * 128],
                                                   in_=xt[s][:, j * 128:(j + 1) * 128], identity=ident[:]),
               [('xt', s), 'ident'], [('ps', bnk)])
        for j in range(KC):
            bnk = j // 4 + 4 * pb
            ACT(lambda e, j=j, bnk=bnk: e.activation(out=dst[:, j, :], in_=psb[bnk][:, (j % 4) * 128:(j % 4 + 1) * 128],
                                                     func=AF.Identity, scale=Avec[:, which, j:j + 1],
                                                     bias=Bvec[:, which, j:j + 1]),
                [('ps', bnk), 'Avec', 'Bvec'], [dst_res])

    def build_xnT_seq(X, tiles):
        for n, tl in enumerate(tiles):
            if n == 0:
                xn_stats(X, 0, tl[0])
            if n + 1 < len(tiles):
                xn_stats(X, n + 1, tiles[n + 1][0])
            xn_transpose(X, n, tl[1], tl[2], tl[3])

    X = alloc_xt()
    tiles = [(ctx_loc[t * 128:(t + 1) * 128, :], xnT_ctx[:, :, t * 128:(t + 1) * 128], 'xnT_ctx', 1) for t in range(2)]
    tiles += [(x_loc[NOWN + t * 128:NOWN + (t + 1) * 128, :], xnT[:, :, t * 128:(t + 1) * 128], 'xnT', 0)
              for t in range(NOTH // 128)]
    build_xnT_seq(X, tiles)
    POOL(lambda e: e.tensor_copy(out=xnT_halo[:], in_=xnT[:, :, 0:128]), ['xnT'], ['xnT_halo'])
    P.barrier()
    NWS = 8
    wctr = [0]
    onecol = sb("onecol", [128, 1])
    DVE(lambda e: e.memset(onecol[:], 1.0), [], ['onecol'])

    def alloc_hgrn():
        a_reset()
        H = {}
        H['wh'] = [a_bf16(KC * 128).rearrange("p (k c) -> p k c", k=KC) for _ in range(NWS)]
        H['T'] = [{nm: a_f32(512) for nm in ('ks', 'f', 'G', 'E', 'X', 'Eq', 'Ek')} for _ in range(2)]
        H['qf'] = a_f32(512)
        H['rs'] = H['T'][1]['ks']
        H['tt'] = H['T'][1]['f']
        H['qp'] = [a_bf16(NOWN) for _ in range(2)]
        H['kp'] = [a_bf16(NOWN) for _ in range(2)]
        H['gate'] = a_bf16(NOWN)
        H['vT'] = a_bf16(512)
        H['vtm'] = a_bf16(32 * 128).rearrange("p (c v) -> p c v", v=128)
        H['ktm'] = a_bf16(4 * 128).rearrange("p (c v) -> p c v", v=128)
        H['Sp'] = a_bf16(4 * 128).rearrange("p (c v) -> p c v", v=128)
        H['Am'] = a_bf16(256)
        H['tmpU'] = a_f32(4 * 128).rearrange("p (c v) -> p c v", v=128)
        H['oacc'] = a_f32(NOWN)
        H['SA'] = a_f32(128)
        H['abc'] = a_f32(3 * 2 * 32).rearrange("p (k d c) -> p k d c", k=3, d=2)
        H['sq'] = a_bf16(512)
        H['ktm2'] = [H['ktm'], a_bf16(4 * 128).rearrange("p (c v) -> p c v", v=128)]
        H['Sp2'] = [H['Sp'], a_bf16(4 * 128).rearrange("p (c v) -> p c v", v=128)]
        H['Am2'] = [H['Am'], a_bf16(256)]
        H['tmpU2'] = [H['tmpU'], a_f32(4 * 128).rearrange("p (c v) -> p c v", v=128)]
        return H

    def load_unit(H, h, u):
        slot = wctr[0] % NWS
        wctr[0] += 1
        DMA('pool', lambda e: e.dma_start(out=H['wh'][slot].rearrange("p k c -> p (k c)"), in_=w_h[h, u]),
            [], [('wh', slot)], 'wh%d' % slot)
        return slot

    pctr = [0]

    def proj(H, slot, xsrc, n):
        bnk = pctr[0] % 4
        pctr[0] += 1
        for kc in range(KC):
            PE(lambda e, kc=kc: e.matmul(psb[bnk][:, 0:n], lhsT=H['wh'][slot][:, kc, :], rhs=xsrc[:, kc, :],
                                         start=(kc == 0), stop=(kc == KC - 1)),
               [('wh', slot), 'xnT', 'xnT_ctx'], [('ps', bnk)])
        return bnk

    def c3(ap, n):
        return ap[:, 0:n].rearrange("p (c t) -> p c t", t=64)

    def hgrn_elem(H, h, d, bnk, col0, n):
        T = H['T'][d]
        nch = n // 64
        cb = col0 // 64
        ks, f, G, E, X_, Eq, Ek = (T[k] for k in ('ks', 'f', 'G', 'E', 'X', 'Eq', 'Ek'))
        qf = H['qf']
        R = lambda nm: (nm, d)
        hd = d * 16 + h
        ACT(lambda e: e.activation(out=ks[:, 0:n], in_=psb[bnk][:, 0:n], func=AF.Sigmoid, scale=-1.0),
            [('ps', bnk)], [R('ks')])
        DVE(lambda e: e.tensor_scalar(out=f[:, 0:n], in0=ks[:, 0:n], scalar1=negoml[:, hd:hd + 1], scalar2=1.0,
                                      op0=ALU.mult, op1=ALU.add), [R('ks'), 'negoml'], [R('f')])
        ACT(lambda e: e.activation(out=f[:, 0:n], in_=f[:, 0:n], func=AF.Ln), [R('f')], [R('f')])
        DVE(lambda e: e.tensor_tensor_scan(out=G[:, 0:n], data0=mreset[:, 0:n], data1=f[:, 0:n], initial=0.0,
                                           op0=ALU.mult, op1=ALU.add), [R('f'), 'mreset'], [R('G')])
        G3, E3, X3, Eq3 = c3(G, n), c3(E, n), c3(X_, n), c3(Eq, n)
        if d == 0:
            DVE(lambda e: e.tensor_tensor(out=X3, in0=G3, in1=G3[:, :, 31:32].to_broadcast([128, nch, 64]),
                                          op=ALU.subtract), [R('G')], [R('X')])
        else:
            DVE(lambda e: e.tensor_tensor(out=E[:, 0:n], in0=G[:, 0:n], in1=f[:, 0:n], op=ALU.subtract),
                [R('G'), R('f')], [R('E')])
            DVE(lambda e: e.tensor_tensor(out=X3, in0=E3[:, :, 32:33].to_broadcast([128, nch, 64]), in1=E3,
                                          op=ALU.subtract), [R('E')], [R('X')])
        ACT(lambda e: e.activation(out=Eq[:, 0:n], in_=X_[:, 0:n], func=AF.Exp), [R('X')], [R('Eq')])
        ACT(lambda e: e.activation(out=Ek[:, 0:n], in_=X_[:, 0:n], func=AF.Exp, scale=-1.0), [R('X')], [R('Ek')])
        DVE(lambda e: e.scalar_tensor_tensor(out=H['kp'][d][:, col0:col0 + n], in0=ks[:, 0:n], scalar=oml[:, hd:hd + 1],
                                             in1=Ek[:, 0:n], op0=ALU.mult, op1=ALU.mult),
            [R('ks'), R('Ek'), 'oml'], [('kp', d)])
        DVE(lambda e: e.tensor_tensor(out=H['qp'][d][:, col0:col0 + n], in0=qf[:, 0:n], in1=Eq[:, 0:n], op=ALU.mult),
            ['qf', R('Eq')], [('qp', d)])
        abc = H['abc']
        ACT(lambda e: e.activation(out=abc[:, 0, d, cb:cb + nch], in_=G3[:, :, 63], func=AF.Exp), [R('G')], [('abc', d)])
        if d == 0:
            DVE(lambda e: e.tensor_copy(out=abc[:, 1, d, cb:cb + nch], in_=Eq3[:, :, 63]), [R('Eq')], [('abc', d)])
            ACT(lambda e: e.activation(out=abc[:, 2, d, cb:cb + nch], in_=G3[:, :, 31], func=AF.Exp), [R('G')], [('abc', d)])
        else:
            DVE(lambda e: e.tensor_copy(out=abc[:, 1, d, cb:cb + nch], in_=Eq3[:, :, 0]), [R('Eq')], [('abc', d)])
            DVE(lambda e: e.tensor_tensor(out=abc[:, 2, d, cb:cb + nch], in0=G3[:, :, 63], in1=E3[:, :, 32],
                                          op=ALU.subtract), [R('G'), R('E')], [('abc', d)])
            ACT(lambda e: e.activation(out=abc[:, 2, d, cb:cb + nch], in_=abc[:, 2, d, cb:cb + nch], func=AF.Exp),
                [('abc', d)], [('abc', d)])

    def build_vtm(H, bnk, col0, n):
        nch = n // 64
        cb = col0 // 64
        if bnk is not None:
            ACT(lambda e: e.activation(out=H['vT'][:, 0:n], in_=psb[bnk][:, 0:n], func=AF.Copy), [('ps', bnk)], ['vT'])
        for g0 in range(0, nch, 4):
            bk = 4 + (g0 // 4) % 2
            for j in range(4):
                PE(lambda e, j=j, g0=g0, bk=bk: e.matmul(psb[bk][0:64, j * 128:(j + 1) * 128],
                                                         lhsT=H['vT'][:, (g0 + j) * 64:(g0 + j + 1) * 64], rhs=identb[:],
                                                         start=True, stop=True), ['vT', 'identb'], [('ps', bk)])
        for g0 in range(0, nch, 4):
            bk = 4 + (g0 // 4) % 2
            ACT(lambda e, g0=g0, bk=bk: e.activation(out=H['vtm'][0:64, cb + g0:cb + g0 + 4, :],
                                                     in_=psb[bk][0:64, :].rearrange("p (c v) -> p c v", v=128), func=AF.Copy),
                [('ps', bk)], ['vtm'])

    def state_block(H, h, d, bz, bv, n, S, sres, par=0):
        T = H['T'][par]
        ks, f, G, E, Ek = (T[k] for k in ('ks', 'f', 'G', 'E', 'Ek'))
        R = lambda nm: (nm, par)
        hd = d * 16 + h
        nt = n // 128
        kpb = H['kp'][par]
        vT = H['vT'] if par == 0 else H['sq']
        ktm = H['ktm'] if par == 0 else H['Sp']
        vtm = H['vtm'][:, 4 * par:4 * par + 4, :]
        bT = 4 if par == 0 else 7
        ab = H['abc'][:, 0, par, 0:1]
        nV = 'vT' if par == 0 else ('vT', 1)
        nK = 'ktm' if par == 0 else ('ktm', 1)
        nVt = 'vtm' if par == 0 else ('vtm', 1)
        nP5 = ('ps', 5) if par == 0 else ('ps', 7)
        bV = 6 if par == 0 else 7
        bU_ = 5 if par == 0 else 7
        ACT(lambda e: e.activation(out=ks[:, 0:n], in_=psb[bz][:, 0:n], func=AF.Sigmoid, scale=-1.0), [('ps', bz)], [R('ks')])
        ACT(lambda e: e.activation(out=vT[:, 0:n], in_=psb[bv][:, 0:n], func=AF.Copy), [('ps', bv)], [nV])
        DVE(lambda e: e.tensor_scalar(out=f[:, 0:n], in0=ks[:, 0:n], scalar1=negoml[:, hd:hd + 1], scalar2=1.0,
                                      op0=ALU.mult, op1=ALU.add), [R('ks'), 'negoml'], [R('f')])
        ACT(lambda e: e.activation(out=f[:, 0:n], in_=f[:, 0:n], func=AF.Ln), [R('f')], [R('f')])
        DVE(lambda e: e.tensor_tensor_scan(out=G[:, 0:n], data0=onecol[:, 0:1].to_broadcast([128, n]), data1=f[:, 0:n],
                                           initial=0.0, op0=ALU.mult, op1=ALU.add), [R('f'), 'onecol'], [R('G')])
        if d == 0:
            ACT(lambda e: e.activation(out=Ek[:, 0:n], in_=G[:, 0:n], func=AF.Exp, scale=-1.0, bias=G[:, n - 1:n]),
                [R('G')], [R('Ek')])
        else:
            DVE(lambda e: e.tensor_tensor(out=E[:, 0:n], in0=G[:, 0:n], in1=f[:, 0:n], op=ALU.subtract),
                [R('G'), R('f')], [R('E')])
            ACT(lambda e: e.activation(out=Ek[:, 0:n], in_=E[:, 0:n], func=AF.Exp), [R('E')], [R('Ek')])
        ACT(lambda e: e.activation(out=ab, in_=G[:, n - 1:n], func=AF.Exp), [R('G')], [('abc', par)])
        DVE(lambda e: e.scalar_tensor_tensor(out=kpb[:, 0:n], in0=ks[:, 0:n], scalar=oml[:, hd:hd + 1],
                                             in1=Ek[:, 0:n], op0=ALU.mult, op1=ALU.mult),
            [R('ks'), R('Ek'), 'oml'], [('kp', par)])
        for t in range(nt):
            PE(lambda e, t=t: e.matmul(psb[bT][:, t * 128:(t + 1) * 128], lhsT=kpb[:, t * 128:(t + 1) * 128],
                                       rhs=identb[:], start=True, stop=True), [('kp', par), 'identb'], [('ps', bT)])
        ACT(lambda e: e.activation(out=ktm[:, 0:nt, :], in_=psb[bT][:, 0:nt * 128].rearrange("p (c v) -> p c v", v=128),
                                   func=AF.Copy), [('ps', bT)], [nK])
        for t in range(nt):
            PE(lambda e, t=t: e.matmul(psb[bV][:, t * 128:(t + 1) * 128], lhsT=vT[:, t * 128:(t + 1) * 128],
                                       rhs=identb[:], start=True, stop=True), [nV, 'identb'], [('ps', bV)])
        ACT(lambda e: e.activation(out=vtm[:, 0:nt, :], in_=psb[bV][:, 0:nt * 128].rearrange("p (c v) -> p c v", v=128),
                                   func=AF.Copy), [('ps', bV)], [nVt])
        for t in range(nt):
            PE(lambda e, t=t: e.matmul(psb[bU_][:, 0:128], lhsT=ktm[:, t, :], rhs=vtm[:, t, :],
                                       start=(t == 0), stop=(t == nt - 1)), [nK, nVt], [nP5])
        DVE(lambda e: e.scalar_tensor_tensor(out=S, in0=S, scalar=ab, in1=psb[bU_][:, 0:128],
                                             op0=ALU.mult, op1=ALU.add), [sres, nP5, ('abc', par)], [sres])

    def hgrn_chain(H, d, S, sres, groups, bset):
        abc = H['abc']
        ktm, Sp, Am, tmpU = H['ktm2'][bset], H['Sp2'][bset], H['Am2'][bset], H['tmpU2'][bset]
        bT, bU, bA, bO = (4, 5, 6, 7) if bset == 0 else (0, 1, 2, 3)
        nK = 'ktm' if bset == 0 else ('ktm', 'b')
        nA = 'Am' if bset == 0 else ('Am', 'b')
        nS = (lambda j: ('Sp', j)) if bset == 0 else (lambda j: ('Sp', 'b', j))
        nU = (lambda j: ('tmpU', j)) if bset == 0 else (lambda j: ('tmpU', 'b', j))
        for g0 in groups:
            order = range(4) if d == 0 else range(3, -1, -1)
            for j in range(4):
                c = g0 + j
                PE(lambda e, j=j, c=c: e.matmul(psb[bT][0:64, j * 128:(j + 1) * 128],
                                                lhsT=H['kp'][d][:, c * 64:(c + 1) * 64], rhs=identb[:],
                                                start=True, stop=True), [('kp', d), 'identb'], [('ps', bT)])
            ACT(lambda e: e.activation(out=ktm[0:64, :, :], in_=psb[bT][0:64, :].rearrange("p (c v) -> p c v", v=128),
                                       func=AF.Copy), [('ps', bT)], [nK])
            for j in range(4):
                c = g0 + j
                PE(lambda e, j=j, c=c: e.matmul(psb[bU][:, j * 128:(j + 1) * 128], lhsT=ktm[0:64, j, :],
                                                rhs=H['vtm'][0:64, c, :], start=True, stop=True),
                   [nK, 'vtm'], [('ps', bU)])
            for j in range(4):
                c = g0 + j
                PE(lambda e, j=j, c=c: e.matmul(psb[bA][0:64, j * 64:(j + 1) * 64], lhsT=H['kp'][d][:, c * 64:(c + 1) * 64],
                                                rhs=H['qp'][d][:, c * 64:(c + 1) * 64], start=True, stop=True),
                   [('kp', d), ('qp', d)], [('ps', bA)])
            DVE(lambda e: e.tensor_tensor(out=Am[0:64, 0:256].rearrange("p (c t) -> p c t", t=64),
                                          in0=psb[bA][0:64, 0:256].rearrange("p (c t) -> p c t", t=64),
                                          in1=maskab[:, d * 64:(d + 1) * 64].unsqueeze(1).to_broadcast([64, 4, 64]),
                                          op=ALU.mult), [('ps', bA), 'maskab'], [nA])
            for j in order:
                c = g0 + j
                ACT(lambda e, j=j, c=c: e.activation(out=tmpU[:, j, :], in_=psb[bU][:, j * 128:(j + 1) * 128],
                                                     func=AF.Copy, scale=abc[:, 1, d, c:c + 1]),
                    [('ps', bU), ('abc', d)], [nU(j)])
            for j in order:
                c = g0 + j
                DVE(lambda e, j=j, c=c: e.tensor_scalar(out=Sp[:, j, :], in0=S, scalar1=abc[:, 2, d, c:c + 1],
                                                        scalar2=None, op0=ALU.mult), [sres, ('abc', d)], [nS(j)])
                DVE(lambda e, j=j, c=c: e.scalar_tensor_tensor(out=S, in0=S, scalar=abc[:, 0, d, c:c + 1],
                                                               in1=tmpU[:, j, :], op0=ALU.mult, op1=ALU.add),
                    [sres, nU(j), ('abc', d)], [sres])
            for j in range(4):
                c = g0 + j
                PE(lambda e, j=j, c=c: e.matmul(psb[bO][:, j * 64:(j + 1) * 64], lhsT=Sp[:, j, :],
                                                rhs=H['qp'][d][:, c * 64:(c + 1) * 64], start=True, stop=False),
                   [nS(j), ('qp', d)], [('ps', bO)])
                PE(lambda e, j=j, c=c: e.matmul(psb[bO][:, j * 64:(j + 1) * 64], lhsT=H['vtm'][0:64, c, :],
                                                rhs=Am[0:64, j * 64:(j + 1) * 64], start=False, stop=True),
                   ['vtm', nA], [('ps', bO)])
            oc = H['oacc'][:, g0 * 64:g0 * 64 + 256]
            first = (g0 < 64) if d == 0 else (g0 >= 64)
            first = (g0 < 16) if d == 0 else (g0 >= 16)
            if first:
                DVE(lambda e, oc=oc: e.tensor_copy(out=oc, in_=psb[bO][:, 0:256]), [('ps', bO)], [('oacc', g0)])
            else:
                DVE(lambda e, oc=oc: e.tensor_tensor(out=oc, in0=oc, in1=psb[bO][:, 0:256], op=ALU.add),
                    [('ps', bO), ('oacc', g0)], [('oacc', g0)])

    H = alloc_hgrn()
    for h in range({99: 16, 1: DBG_HEADS, 2: 0}[stage]):
        sB = load_unit(H, h, 2)
        sV = load_unit(H, h, 3)
        S = SinitB[:, h, :]
        DVE(lambda e, S=S: e.memset(S, 0.0), [], ['SB'])
        blocks = [(xnT_ctx[:], LCTX)] + [(xnT[:, :, tb * 512:(tb + 1) * 512], 512) for tb in range(3, -1, -1)]
        blocks.append(None)
        for i in range(0, 6, 2):
            lists = []
            for par in range(2):
                if blocks[i + par] is None:
                    lists.append([])
                    continue
                xs_, n_ = blocks[i + par]
                bz = proj(H, sB, xs_, n_)
                bv = proj(H, sV, xs_, n_)
                P.capture()
                state_block(H, h, 1, bz, bv, n_, S, 'SB', par)
                lists.append(P.release())
            P.interleave(lists[0], lists[1])
    P.barrier()
    X = alloc_xt()
    build_xnT_seq(X, [(x_loc[t * 128:(t + 1) * 128, :], xnT[:, :, t * 128:(t + 1) * 128], 'xnT', 0)
                      for t in range(NOWN // 128)])
    P.barrier()

    H = alloc_hgrn()
    NH = {99: 16, 1: DBG_HEADS, 2: 0}[stage]
    for h in range(NH):
        su = [load_unit(H, h, u) for u in range(5)]
        SA = H['SA']
        DVE(lambda e: e.memset(SA, 0.0), [], ['SA'])
        bz = proj(H, su[1], xnT_ctx[:], LCTX)
        bv = proj(H, su[3], xnT_ctx[:], LCTX)
        state_block(H, h, 0, bz, bv, LCTX, SA, 'SA')
        for tb in range(4):
            if True:
                xs = xnT[:, :, tb * 512:(tb + 1) * 512]
                bza = proj(H, su[1], xs, 512)
                bzb = proj(H, su[2], xs, 512)
                bv = proj(H, su[3], xs, 512)
                bq = proj(H, su[0], xs, 512)
                P.capture()
                hgrn_elem(H, h, 0, bza, tb * 512, 512)
                la = P.release()
                P.capture()
                hgrn_elem(H, h, 1, bzb, tb * 512, 512)
                lb_ = P.release()
                P.capture()
                ACT(lambda e, bv=bv: e.activation(out=H['vT'][:, 0:512], in_=psb[bv][:, 0:512], func=AF.Copy), [('ps', bv)], ['vT'])
                ACT(lambda e, bq=bq: e.activation(out=H['qf'][:, :], in_=psb[bq][:, :], func=AF.Silu), [('ps', bq)], ['qf'])
                lc = P.release()
                P.interleave(la[:3], lb_[:3])
                P.interleave(lc, [])
                bg = proj(H, su[4], xs, 512)
                P.interleave(la[3:], lb_[3:])
                build_vtm(H, None, tb * 512, 512)
                ACT(lambda e, bg=bg, tb=tb: e.activation(out=H['gate'][:, tb * 512:(tb + 1) * 512], in_=psb[bg][:, :], func=AF.Silu),
                    [('ps', bg)], ['gate'])
        P.capture()
        hgrn_chain(H, 0, SA, 'SA', list(range(0, 32, 4)), 0)
        lca = P.release()
        P.capture()
        hgrn_chain(H, 1, SinitB[:, h, :], 'SB', list(range(28, -1, -4)), 1)
        lcb = P.release()
        P.interleave(lca, lcb)
        for tb in range(4):
            oc = H['oacc'][:, tb * 512:(tb + 1) * 512]
            ocr = [('oacc', tb * 8), ('oacc', tb * 8 + 4)]
            ACT(lambda e, oc=oc: e.activation(out=H['sq'][:, :], in_=oc, func=AF.Square), ocr, ['sq'])
            PE(lambda e: e.matmul(psb[6][:, :], lhsT=onesb[:], rhs=H['sq'][:, :], start=True, stop=True),
               ['sq', 'onesb'], [('ps', 6)])
            DVE(lambda e: e.tensor_scalar(out=H['rs'][:, :], in0=psb[6][:, :], scalar1=1.0 / 128, scalar2=EPS,
                                          op0=ALU.mult, op1=ALU.add), [('ps', 6)], [('ks', 1)])
            ACT(lambda e: e.activation(out=H['rs'][:, :], in_=H['rs'][:, :], func=AF.Ln), [('ks', 1)], [('ks', 1)])
            ACT(lambda e: e.activation(out=H['rs'][:, :], in_=H['rs'][:, :], func=AF.Exp, scale=-0.5), [('ks', 1)], [('ks', 1)])
            DVE(lambda e, oc=oc: e.tensor_tensor(out=H['tt'][:, :], in0=oc, in1=H['rs'][:, :], op=ALU.mult),
                ocr + [('ks', 1)], [('f', 1)])
            DVE(lambda e, tb=tb, h=h: e.scalar_tensor_tensor(out=H['gate'][:, tb * 512:(tb + 1) * 512], in0=H['tt'][:, :],
                                                             scalar=hgain[:, h:h + 1], in1=H['gate'][:, tb * 512:(tb + 1) * 512],
                                                             op0=ALU.mult, op1=ALU.mult), [('f', 1), 'gate', 'hgain'], ['gate'])
        DMA('sp', lambda e, h=h: e.dma_start(out=hg_spill[h * 128:(h + 1) * 128, :], in_=H['gate'][:, :]),
            ['gate'], ['hg_spill'], 'hgsp')
    P.barrier()

    if stage == 1:
        dbg_hg = dout("dbg_hg", [128 * NH, NOWN], BF16)
        dbg_sb = dout("dbg_sb", [128, 16 * 128])
        DMA('sp', lambda e: e.dma_start(out=dbg_hg, in_=hg_spill[0:128 * NH, :]), ['hg_spill'], ['o1'], 'o1')
        DMA('sp', lambda e: e.dma_start(out=dbg_sb, in_=SinitB[:].rearrange("p h v -> p (h v)")), ['SB'], ['o2'], 'o2')
        P.add('sp', lambda e: None, ['o1', 'o2'], [])
        P.emit(es)
        es.close()
        return nc
    w_att = din("w_att", [4, 9, 128, KC * 128])
    w_v = din("w_v", [128, KC * 512])
    cos_d = din("cosT", [128, NOWN + 128])
    sin_d = din("sinT", [128, NOWN + 128])
    rotT_d = din("rotT", [128, 128])
    sink_d = din("sink", [1, 16])
    bmask_d = din("bmask", [128, 2 * 512])
    att_spill = nc.dram_tensor("att_spill", [D, NOWN], BF16).ap()
    QSCALE = 128.0 ** -0.5
    NKT = NOWN // 128 + 1

    a_reset()
    A = {}
    A['wv_flat'] = a_bf16(KC * 512)
    A['wv'] = A['wv_flat'].rearrange("p (k c) -> p k c", k=KC)
    A['qT_all'] = A['wv_flat']
    A['wh'] = [a_bf16(KC * 128).rearrange("p (k c) -> p k c", k=KC) for _ in range(4)]
    A['V'] = a_bf16(NKT * 512).rearrange("p (t c) -> p t c", c=512)
    A['Vc'] = a_bf16(2 * 512).rearrange("p (t c) -> p t c", c=512)
    A['kT'] = a_bf16(NOWN + 128)
    A['kcT'] = a_bf16(LCTX)
    A['cos'] = a_f32(NOWN + 128)
    A['sin'] = a_f32(NOWN + 128)
    A['rotT'] = a_f32(128)
    A['qf'] = a_f32(512)
    A['t1'] = a_f32(512)
    A['t2'] = a_f32(512)
    A['gT_all'] = a_bf16(4 * 2048)
    A['ao'] = a_bf16(4 * 512)
    A['pT'] = [a_bf16(512) for _ in range(2)]
    A['den'] = a_f32(512)
    A['o1'] = a_f32(512)
    A['sinkrow'] = a_f32(512)
    A['sinkexp'] = a_f32(16)
    A['bmf'] = a_f32(1024)
    A['bm'] = a_bf16(1024)
    DMA('sp', lambda e: e.dma_start(out=A['cos'], in_=cos_d), [], ['cos'], 'p3a')
    DMA('sp', lambda e: e.dma_start(out=A['sin'], in_=sin_d), [], ['sin'], 'p3b')
    DMA('sp', lambda e: e.dma_start(out=A['rotT'], in_=rotT_d), [], ['rotT'], 'p3c')
    DMA('sp', lambda e: e.dma_start(out=A['bmf'], in_=bmask_d), [], ['bmf'], 'p3d')
    DMA('sp', lambda e: e.dma_start(out=A['sinkexp'], in_=sink_d.partition_broadcast(128)), [], ['sinkexp'], 'p3e')
    DMA('pool', lambda e: e.dma_start(out=A['wv'].rearrange("p k c -> p (k c)"), in_=w_v), [], ['wv'], 'p3f')
    ACT(lambda e: e.activation(out=A['sinkexp'], in_=A['sinkexp'], func=AF.Exp), ['sinkexp'], ['sinkexp'])
    DVE(lambda e: e.tensor_copy(out=A['bm'], in_=A['bmf']), ['bmf'], ['bm'])

    awctr = [0]

    def load_att(g, u):
        slot = awctr[0] % 4
        awctr[0] += 1
        DMA('pool', lambda e: e.dma_start(out=A['wh'][slot].rearrange("p k c -> p (k c)"), in_=w_att[g, u]),
            [], [('wh', slot)], 'wh%d' % slot)
        return slot

    apctr = [0]

    def aproj(slot, xsrc, n):
        bnk = apctr[0] % 3
        apctr[0] += 1
        for kc in range(KC):
            PE(lambda e, kc=kc: e.matmul(psb[bnk][:, 0:n], lhsT=A['wh'][slot][:, kc, :], rhs=xsrc[:, kc, :],
                                         start=(kc == 0), stop=(kc == KC - 1)),
               [('wh', slot), 'xnT', 'xnT_ctx', 'xnT_halo'], [('ps', bnk)])
        return bnk

    def vproj(xsrc, dst):
        bnk = apctr[0] % 3
        apctr[0] += 1
        for kc in range(KC):
            PE(lambda e, kc=kc: e.matmul(psb[bnk][:, :], lhsT=xsrc[:, kc, :], rhs=A['wv'][:, kc, :],
                                         start=(kc == 0), stop=(kc == KC - 1)),
               ['wv', 'xnT', 'xnT_ctx', 'xnT_halo'], [('ps', bnk)])
        ACT(lambda e: e.activation(out=dst, in_=psb[bnk][:, :], func=AF.Copy), [('ps', bnk)], ['V'])

    for t in range(NKT):
        src = xnT[:, :, t * 128:(t + 1) * 128] if t < NKT - 1 else xnT_halo[:]
        vproj(src, A['V'][:, t, :])
    for t in range(2):
        vproj(xnT_ctx[:, :, t * 128:(t + 1) * 128], A['Vc'][:, t, :])

    def rope(bnk, n, col0, dst):
        ACT(lambda e: e.activation(out=A['qf'][:, 0:n], in_=psb[bnk][:, 0:n], func=AF.Copy), [('ps', bnk)], ['qf'])
        PE(lambda e: e.matmul(psb[4][:, 0:n], lhsT=A['rotT'], rhs=A['qf'][:, 0:n], start=True, stop=True),
           ['qf', 'rotT'], [('ps', 4)])
        DVE(lambda e: e.tensor_tensor(out=A['t1'][:, 0:n], in0=A['qf'][:, 0:n], in1=A['cos'][:, col0:col0 + n], op=ALU.mult),
            ['qf', 'cos'], ['t1'])
        DVE(lambda e: e.tensor_tensor(out=A['t2'][:, 0:n], in0=psb[4][:, 0:n], in1=A['sin'][:, col0:col0 + n], op=ALU.mult),
            [('ps', 4), 'sin'], ['t2'])
        return ['t1', 't2'], dst

    for g in range(4 if stage != 2 else 1):
        DVE(lambda e, g=g: e.tensor_copy(out=A['sinkrow'].rearrange("p (i t) -> p i t", t=128),
                                         in_=A['sinkexp'][:, 4 * g:4 * g + 4].unsqueeze(2).to_broadcast([128, 4, 128])),
            ['sinkexp'], ['sinkrow'])
        sk = load_att(g, 4)
        for tb in range(5):
            n = 512 if tb < 4 else 128
            src = xnT[:, :, tb * 512:(tb + 1) * 512] if tb < 4 else xnT_halo[:]
            bk = aproj(sk, src, n)
            rope(bk, n, tb * 512, None)
            DVE(lambda e, tb=tb, n=n: e.tensor_tensor(out=A['kT'][:, tb * 512:tb * 512 + n], in0=A['t1'][:, 0:n],
                                                      in1=A['t2'][:, 0:n], op=ALU.add), ['t1', 't2'], ['kT'])
        bk = aproj(sk, xnT_ctx[:], LCTX)
        ACT(lambda e, bk=bk: e.activation(out=A['kcT'], in_=psb[bk][:, 0:LCTX], func=AF.Copy), [('ps', bk)], ['kcT'])
        for i in range(4):
            sq_ = load_att(g, i)
            for tb in range(4):
                xs = xnT[:, :, tb * 512:(tb + 1) * 512]
                qT4 = A['qT_all'][:, tb * 2048:(tb + 1) * 2048].rearrange("p (q i t) -> p q i t", q=4, i=4)
                bq = aproj(sq_, xs, 512)
                rope(bq, 512, tb * 512, None)
                DVE(lambda e, i=i, qT4=qT4: e.tensor_tensor(out=qT4[:, :, i, :], in0=A['t1'].rearrange("p (q t) -> p q t", t=128),
                                                            in1=A['t2'].rearrange("p (q t) -> p q t", t=128), op=ALU.add),
                    ['t1', 't2'], [('qT', tb), 'wv'])
        for i in range(4):
            sg = load_att(g, 5 + i)
            for tb in range(4):
                xs = xnT[:, :, tb * 512:(tb + 1) * 512]
                gT4 = A['gT_all'][:, tb * 2048:(tb + 1) * 2048].rearrange("p (q i t) -> p q i t", q=4, i=4)
                bg = aproj(sg, xs, 512)
                ACT(lambda e, i=i, bg=bg, gT4=gT4: e.activation(out=gT4[:, :, i, :], in_=psb[bg][:, :].rearrange("p (q t) -> p q t", t=128),
                                                                func=AF.Silu), [('ps', bg)], [('gT', tb)])

        def attn(tb):
            ao4 = A['ao'].rearrange("p (q i t) -> p q i t", q=4, i=4)
            for qb in range(4):
                Q = tb * 4 + qb
                kbs = []
                if Q >= 1:
                    kbs.append((A['kT'][:, (Q - 1) * 128:Q * 128], A['V'][:, Q - 1, g * 128:(g + 1) * 128], 0))
                kbs.append((A['kT'][:, Q * 128:(Q + 1) * 128], A['V'][:, Q, g * 128:(g + 1) * 128], None))
                kbs.append((A['kT'][:, (Q + 1) * 128:(Q + 2) * 128], A['V'][:, Q + 1, g * 128:(g + 1) * 128], 1))
                for t in range(2):
                    kbs.append((A['kcT'][:, t * 128:(t + 1) * 128], A['Vc'][:, t, g * 128:(g + 1) * 128], None))
                qrhs = A['qT_all'][:, tb * 2048 + qb * 512:tb * 2048 + (qb + 1) * 512]
                bO = 7 if Q % 2 == 0 else 0
                bD = 3 if Q % 2 == 0 else 1
                nk = len(kbs)

                def emit_S(ki):
                    kap, vap, mk = kbs[ki]
                    sl = (Q * 5 + ki) % 2
                    PE(lambda e, kap=kap, sl=sl, mk=mk, qrhs=qrhs: e.matmul(psb[5 + sl][:, :], lhsT=kap, rhs=qrhs, start=True, stop=(mk is None)),
                       ['kT', 'kcT', ('qT', tb)], [('ps', 5 + sl)])
                    if mk is not None:
                        PE(lambda e, sl=sl, mk=mk: e.matmul(psb[5 + sl][:, :], lhsT=identb[:], rhs=A['bm'][:, mk * 512:(mk + 1) * 512],
                                                            start=False, stop=True), ['identb', 'bm'], [('ps', 5 + sl)])

                emit_S(0)
                for ki in range(nk):
                    kap, vap, mk = kbs[ki]
                    sl = (Q * 5 + ki) % 2
                    if ki + 1 < nk:
                        emit_S(ki + 1)
                    ACT(lambda e, sl=sl: e.activation(out=A['pT'][sl], in_=psb[5 + sl][:, :], func=AF.Exp, scale=QSCALE),
                        [('ps', 5 + sl)], [('pT', sl)])
                    first, last = ki == 0, ki == nk - 1
                    PE(lambda e, vap=vap, sl=sl, first=first, last=last, bO=bO: e.matmul(psb[bO][:, :], lhsT=vap, rhs=A['pT'][sl],
                                                                                   start=first, stop=last),
                       ['V', ('pT', sl)], [('ps', bO)])
                    PE(lambda e, sl=sl, first=first, last=last, bD=bD: e.matmul(psb[bD][:, :], lhsT=onesb[:], rhs=A['pT'][sl],
                                                                          start=first, stop=last),
                       ['onesb', ('pT', sl)], [('ps', bD)])
                DVE(lambda e, bD=bD: e.tensor_tensor(out=A['den'], in0=psb[bD][:, :], in1=A['sinkrow'], op=ALU.add),
                    [('ps', bD), 'sinkrow'], ['den'])
                ACT(lambda e: e.activation(out=A['den'], in_=A['den'], func=AF.Ln), ['den'], ['den'])
                ACT(lambda e: e.activation(out=A['den'], in_=A['den'], func=AF.Exp, scale=-1.0), ['den'], ['den'])
                DVE(lambda e, bO=bO: e.tensor_tensor(out=A['o1'], in0=psb[bO][:, :], in1=A['den'], op=ALU.mult),
                    [('ps', bO), 'den'], ['o1'])
                DVE(lambda e, qb=qb: e.tensor_tensor(out=A['ao'][:, qb * 512:(qb + 1) * 512], in0=A['o1'],
                                                     in1=A['gT_all'][:, tb * 2048 + qb * 512:tb * 2048 + (qb + 1) * 512], op=ALU.mult),
                    ['o1', ('gT', tb)], ['ao'])
            for i in range(4):
                hh = 4 * g + i
                DMA('sp', lambda e, hh=hh, i=i, tb=tb: e.dma_start(
                    out=att_spill[hh * 128:(hh + 1) * 128, tb * 512:(tb + 1) * 512].rearrange("p (q t) -> p q t", t=128),
                    in_=ao4[:, :, i, :]), ['ao'], ['att_spill'], 'atsp')

        for tb in range(4):
            attn(tb)
    P.barrier()

    if stage == 2:
        dbg_at = dout("dbg_at", [512, NOWN], BF16)
        DMA('sp', lambda e: e.dma_start(out=dbg_at, in_=att_spill[0:512, :]), ['att_spill'], ['o1'], 'o1')
        P.add('sp', lambda e: None, ['o1'], [])
        P.emit(es)
        es.close()
        return nc

    w_fm = din("w_fm", [16, 4, 128, KC * 128])
    w_outd = din("w_out_l", [8, 128, KC * 256])
    bm_d = din("bmerge_fm", [128, 32])
    fg_d = din("fgain", [1, D])
    y_out = dout("y", [NOWN, D])
    TB4 = 512
    NB4 = NOWN // TB4
    NT4 = TB4 // 128

    a_reset()
    F = {}
    F['wh'] = [a_bf16(KC * 128).rearrange("p (k c) -> p k c", k=KC) for _ in range(3)]
    F['wh'].append(xnT_halo[:])
    F['wo'] = [a_bf16(KC * 256).rearrange("p (k c) -> p k c", k=KC) for _ in range(2)]
    F['hgT'] = a_bf16(KC * TB4).rearrange("p (k t) -> p k t", k=KC)
    F['atT'] = a_bf16(KC * TB4).rearrange("p (k t) -> p k t", k=KC)
    F['yT'] = a_bf16(KC * TB4).rearrange("p (k t) -> p k t", k=KC)
    F['hrow'] = [a_f32(D), a_f32(D), SinitB[:].rearrange("p h v -> p (h v)"),
                 xnT_ctx[:].rearrange("p k t -> p (k t)").bitcast(F32)]
    F['gate'] = a_f32(D)
    F['fg'] = a_f32(D)
    F['sa'] = a_f32(TB4)
    F['sb'] = a_f32(TB4)
    F['tmp'] = a_f32(512)
    F['junk'] = F['hgT'].rearrange("p k t -> p (k t)")[:, 0:D]
    F['ssq'] = [a_f32(1) for _ in range(4)]
    F['rstd'] = [a_f32(1) for _ in range(4)]
    F['bm'] = a_f32(32)
    DMA('sp', lambda e: e.dma_start(out=F['gate'], in_=gate_dram.partition_broadcast(128)), ['gate_dram'], ['gate_bc'], 'p4a')
    DMA('sp', lambda e: e.dma_start(out=F['fg'], in_=fg_d.partition_broadcast(128)), [], ['fg'], 'p4b')
    DMA('sp', lambda e: e.dma_start(out=F['bm'], in_=bm_d), [], ['bmf4'], 'p4c')

    fwctr = [0]
    fpctr = [0]
    owctr = [0]
    for tb in range(NB4):
        t0 = tb * TB4
        DMA('sp', lambda e, t0=t0: e.dma_start(out=F['hgT'], in_=hg_spill[:, t0:t0 + TB4].rearrange("(k p) t -> p k t", p=128)),
            ['hg_spill'], ['hgT'], 'p4h')
        DMA('act', lambda e, t0=t0: e.dma_start(out=F['atT'], in_=att_spill[:, t0:t0 + TB4].rearrange("(k p) t -> p k t", p=128)),
            ['att_spill'], ['atT'], 'p4t')
        xs = xnT[:, :, t0:t0 + TB4]
        for c in range(16):
            bnks = []
            for u, src, sres in ((0, F['hgT'], 'hgT'), (1, xs, 'xnT'), (2, F['atT'], 'atT'), (3, xs, 'xnT')):
                slot = fwctr[0] % 4
                fwctr[0] += 1
                DMA('pool', lambda e, slot=slot, c=c, u=u: e.dma_start(out=F['wh'][slot].rearrange("p k c -> p (k c)"),
                                                                       in_=w_fm[c, u]), [], [('wh', slot)], 'wh%d' % slot)
                bnk = fpctr[0] % 4
                fpctr[0] += 1
                for kc in range(KC):
                    PE(lambda e, kc=kc, slot=slot, bnk=bnk, src=src: e.matmul(psb[bnk][:, 0:TB4], lhsT=F['wh'][slot][:, kc, :],
                                                                              rhs=src[:, kc, :], start=(kc == 0), stop=(kc == KC - 1)),
                       [('wh', slot), sres], [('ps', bnk)])
                bnks.append(bnk)
            ACT(lambda e, b=bnks[1], c=c: e.activation(out=F['sa'], in_=psb[b][:, 0:TB4], func=AF.Sigmoid, bias=F['bm'][:, c:c + 1]),
                [('ps', bnks[1]), 'bmf4'], ['sa'])
            ACT(lambda e, b=bnks[3], c=c: e.activation(out=F['sb'], in_=psb[b][:, 0:TB4], func=AF.Sigmoid, bias=F['bm'][:, 16 + c:17 + c]),
                [('ps', bnks[3]), 'bmf4'], ['sb'])
            DVE(lambda e, b=bnks[0]: e.tensor_tensor(out=F['sa'], in0=psb[b][:, 0:TB4], in1=F['sa'], op=ALU.mult),
                [('ps', bnks[0]), 'sa'], ['sa'])
            DVE(lambda e, b=bnks[2]: e.tensor_tensor(out=F['sb'], in0=psb[b][:, 0:TB4], in1=F['sb'], op=ALU.mult),
                [('ps', bnks[2]), 'sb'], ['sb'])
            DVE(lambda e, c=c: e.tensor_tensor(out=F['yT'][:, c, :], in0=F['sa'], in1=F['sb'], op=ALU.add), ['sa', 'sb'], ['yT'])
        for tt in range(4):
            DMA('sp' if tt % 2 == 0 else 'act',
                lambda e, tt=tt, t0=t0: e.dma_start(out=F['hrow'][tt], in_=x_loc[t0 + tt * 128:t0 + (tt + 1) * 128, :]),
                [], [('hrow', tt)], 'p4x%d' % tt)
        for cb in range(8):
            os_ = owctr[0] % 2
            owctr[0] += 1
            DMA('pool', lambda e, os_=os_, cb=cb: e.dma_start(out=F['wo'][os_].rearrange("p k c -> p (k c)"), in_=w_outd[cb]),
                [], [('wo', os_)], 'wo%d' % os_)
            for tt in range(4):
                bnk = 4 + (cb * 4 + tt) % 4
                for kc in range(KC):
                    PE(lambda e, kc=kc, tt=tt, os_=os_, bnk=bnk: e.matmul(psb[bnk][:, 0:256], lhsT=F['yT'][:, kc, tt * 128:(tt + 1) * 128],
                                                                          rhs=F['wo'][os_][:, kc, :], start=(kc == 0), stop=(kc == KC - 1)),
                       ['yT', ('wo', os_)], [('ps', bnk)])
                DVE(lambda e, bnk=bnk, cb=cb: e.tensor_tensor(out=F['tmp'][:, 0:256], in0=psb[bnk][:, 0:256], in1=F['gate'][:, cb * 256:(cb + 1) * 256],
                                                              op=ALU.mult), [('ps', bnk), 'gate_bc'], ['tmp4'])
                DVE(lambda e, tt=tt, cb=cb: e.tensor_tensor(out=F['hrow'][tt][:, cb * 256:(cb + 1) * 256],
                                                            in0=F['hrow'][tt][:, cb * 256:(cb + 1) * 256], in1=F['tmp'][:, 0:256], op=ALU.add),
                    ['tmp4', ('hrow', tt)], [('hrow', tt)])
        for tt in range(4):
            hr = F['hrow'][tt]
            DVE(lambda e, tt=tt: e.memset(F['ssq'][tt], 0.0), [], [('ssq4', tt)])
            ACT(lambda e, tt=tt, hr=hr: e.activation(out=F['junk'], in_=hr, func=AF.Square, accum_out=F['ssq'][tt]),
                [('hrow', tt), ('ssq4', tt)], [('ssq4', tt), 'hgT'])
            DVE(lambda e, tt=tt: e.tensor_scalar(out=F['rstd'][tt], in0=F['ssq'][tt], scalar1=1.0 / D, scalar2=EPS,
                                                 op0=ALU.mult, op1=ALU.add), [('ssq4', tt)], [('rstd4', tt)])
            ACT(lambda e, tt=tt: e.sqrt(out=F['rstd'][tt], in_=F['rstd'][tt]), [('rstd4', tt)], [('rstd4', tt)])
            DVE(lambda e, tt=tt: e.reciprocal(out=F['rstd'][tt], in_=F['rstd'][tt]), [('rstd4', tt)], [('rstd4', tt)])
            DVE(lambda e, tt=tt, hr=hr: e.scalar_tensor_tensor(out=hr, in0=hr, scalar=F['rstd'][tt][:, 0:1], in1=F['fg'],
                                                               op0=ALU.mult, op1=ALU.mult),
                [('hrow', tt), ('rstd4', tt), 'fg'], [('hrow', tt)])
            DMA('sp', lambda e, tt=tt, hr=hr, t0=t0: e.dma_start(out=y_out[t0 + tt * 128:t0 + (tt + 1) * 128, :], in_=hr),
                [('hrow', tt)], [('y_out', tt)], 'p4y%d' % tt)
    P.add('sp', lambda e: None, [('y_out', tt) for tt in range(4)], [])

    P.emit(es)
    es.close()
    return nc


def prep_inputs(inp):
    f = np.float32
    x = np.asarray(inp['x'], f)
    ctx = np.asarray(inp['ctx'], f)
    c = np.asarray(inp['c'], f)
    c_ctx = np.asarray(inp['c_ctx'], f)
    w_ada = np.asarray(inp['w_ada'], f)[0]
    b_ada = np.asarray(inp['b_ada'], f)[0]
    w_ada_l = np.ascontiguousarray(w_ada.reshape(KC, 128, 3 * D).transpose(1, 0, 2))
    b_ada2 = np.ascontiguousarray(np.stack([b_ada, b_ada], 0))
    gain_fm = np.ascontiguousarray(np.asarray(inp['norm_gain'], f)[0].reshape(KC, 128).T)
    ident = np.eye(128, dtype=f)
    sel = np.zeros((2, 128), f)
    sel[0, :] = 1.0
    w_in = np.asarray(inp['w_in'], f)[0]
    OFF_HG_Q, OFF_HG_FF, OFF_HG_FB, OFF_HG_I, OFF_HG_G = 5120, 7168, 9216, 11264, 13312

    def unit(col0, ncol=128):
        return w_in[:, col0:col0 + ncol].reshape(KC, 128, ncol).transpose(1, 0, 2).reshape(128, KC * ncol)

    def fm(v):
        return np.ascontiguousarray(v.reshape(16, 128).T)

    w_h_s = []
    lbl_s = []
    lf = np.asarray(inp['lb_logits_fwd'], f)
    lbk = np.asarray(inp['lb_logits_bwd'], f)
    for s in range(2):
        offA, offB = (OFF_HG_FF, OFF_HG_FB) if s == 0 else (OFF_HG_FB, OFF_HG_FF)
        wh = np.empty((16, 5, 128, KC * 128), f)
        for h in range(16):
            for u, off in enumerate((OFF_HG_Q, offA, offB, OFF_HG_I, OFF_HG_G)):
                wh[h, u] = unit(off + h * 128)
        w_h_s.append(wh)
        la, lb_ = (lf, lbk) if s == 0 else (lbk, lf)
        arr = np.stack([np.stack([fm(la[0]), fm(la[1])], 1), np.stack([fm(lb_[0]), fm(lb_[1])], 1)], 1)
        lbl_s.append(np.ascontiguousarray(arr.reshape(128, 64)))
    hgain_fm = fm(np.asarray(inp['hgrn_norm_gain'], f)[0])
    mreset = np.ones((128, 512), f)
    mreset[:, ::64] = 0.0
    ss, tt = np.meshgrid(np.arange(64), np.arange(64), indexing='ij')
    maskab = np.concatenate([(ss <= tt).astype(f), (ss >= tt).astype(f)], 1)
    OFF_ATT_Q, OFF_ATT_K, OFF_ATT_V, OFF_ATT_G, OFF_MERGE = 0, 2048, 2560, 3072, 15360
    w_att = np.empty((4, 9, 128, KC * 128), f)
    for g in range(4):
        for u in range(4):
            w_att[g, u] = unit(OFF_ATT_Q + (4 * g + u) * 128)
            w_att[g, 5 + u] = unit(OFF_ATT_G + (4 * g + u) * 128)
        w_att[g, 4] = unit(OFF_ATT_K + g * 128)
    w_v = np.ascontiguousarray(unit(OFF_ATT_V, 512))
    w_o_hgrn = np.asarray(inp['w_o_hgrn'], f)[0]
    w_o_attn = np.asarray(inp['w_o_attn'], f)[0]
    w_out = np.asarray(inp['w_out'], f)[0]

    def unit_of(w, col0, ncol=128):
        return w[:, col0:col0 + ncol].reshape(KC, 128, ncol).transpose(1, 0, 2).reshape(128, KC * ncol)

    w_fm = np.empty((16, 4, 128, KC * 128), f)
    for cc in range(16):
        w_fm[cc, 0] = unit_of(w_o_hgrn, cc * 128)
        w_fm[cc, 1] = unit(OFF_MERGE + cc * 128)
        w_fm[cc, 2] = unit_of(w_o_attn, cc * 128)
        w_fm[cc, 3] = unit(OFF_MERGE + D + cc * 128)
    w_out_l = np.stack([unit_of(w_out, cb * 256, 256) for cb in range(8)], 0)
    bmg = np.asarray(inp['b_merge'], f)[0]
    bmerge_fm = np.ascontiguousarray(np.concatenate([fm(bmg[0]), fm(bmg[1])], 1))
    fgain = np.ascontiguousarray(np.asarray(inp['final_norm_gain'], f).reshape(1, D))
    sink = np.ascontiguousarray(np.asarray(inp['sink_logits'], f).reshape(1, 16))
    rotT = np.zeros((128, 128), f)
    for dq in range(128):
        if (dq // 32) % 2 == 0:
            rotT[dq + 32, dq] = -1.0
        else:
            rotT[dq - 32, dq] = 1.0
    aa, bb = np.meshgrid(np.arange(128), np.arange(128), indexing='ij')
    bm0 = np.tile(np.where(bb <= aa, 0.0, -30000.0).astype(f), (1, 4))
    bm1 = np.tile(np.where(aa <= bb, 0.0, -30000.0).astype(f), (1, 4))
    bmask = np.ascontiguousarray(np.concatenate([bm0, bm1], 1))
    inv_freq = (1.0 / (np.float32(10000.0) ** (np.arange(0, 64, 2, dtype=f) / np.float32(64)))).astype(f)
    rope_tabs = []
    for s in range(2):
        ii = np.arange(NOWN + 128)
        jj = ii if s == 0 else 4095 - ii
        row = (jj // 64).astype(f)
        colp = (jj % 64).astype(f)
        ang_r = row[:, None] * inv_freq[None, :]
        ang_c = colp[:, None] * inv_freq[None, :]
        ang = np.concatenate([ang_r, ang_r, ang_c, ang_c], -1).astype(f)
        rope_tabs.append((np.ascontiguousarray(np.cos(ang).T.astype(f)), np.ascontiguousarray(np.sin(ang).T.astype(f))))
    maps = []
    for core in range(8):
        b, s = core // 2, core % 2
        xb = x[b]
        cb = ctx[b]
        if s == 1:
            xb = xb[::-1]
            cb = cb[::-1]
        cv = np.stack([c[b].reshape(KC, 128).T, c_ctx.reshape(KC, 128).T], -1).reshape(128, KC * 2)
        maps.append(dict(
            x_loc=np.ascontiguousarray(xb), ctx_loc=np.ascontiguousarray(cb),
            cvec=np.ascontiguousarray(cv), w_ada_l=w_ada_l, b_ada2=b_ada2, gain_fm=gain_fm,
            ident=ident, sel=sel, w_h=w_h_s[s], lbl=lbl_s[s], hgain_fm=hgain_fm, mreset=mreset, maskab=maskab,
            w_att=w_att, w_v=w_v, cosT=rope_tabs[s][0], sinT=rope_tabs[s][1], rotT=rotT, sink=sink, bmask=bmask,
            w_fm=w_fm, w_out_l=w_out_l, bmerge_fm=bmerge_fm, fgain=fgain))
    return maps


def kernel(**inputs):
    maps = prep_inputs(inputs)
    nc = build()
    res = run_bass_kernel_spmd(nc, maps, core_ids=list(range(8)))
    out = np.zeros((4, 4096, D), np.float32)
    for core in range(8):
        b, s = core // 2, core % 2
        y = res.results[core]["y"]
        if s == 0:
            out[b, :NOWN] = y
        else:
            out[b, NOWN:] = y[::-1]
    return out
```

```python
import numpy as np
from contextlib import ExitStack
import concourse.bass as bass
import concourse.mybir as mybir
from concourse.bass_utils import run_bass_kernel_spmd

F32 = mybir.dt.float32
BF16 = mybir.dt.bfloat16
AF = mybir.ActivationFunctionType
ALU = mybir.AluOpType

D = 2048
KC = 16
NOWN = 2048
NOTH = 2048
LCTX = 256
EPS = 1e-6
DBG_HEADS = 2


class Prog:
    def __init__(self, nc):
        self.nc = nc
        self.ops = []

    def add(self, eng, fn, r=(), w=(), dma=None):
        self.ops.append(dict(eng=eng, fn=fn, r=tuple(r), w=tuple(w), dma=dma,
                             deps=[], signal=False, sig=None))

    def barrier(self):
        self.add('sp', lambda e: None, [], ['__bar'])
        self.ops[-1]['barrier'] = True
        for eng in ('pe', 'act', 'dve', 'pool'):
            self.add(eng, lambda e: None, ['__bar'], [])

    def capture(self):
        self._saved = self.ops
        self.ops = []

    def release(self):
        got = self.ops
        self.ops = self._saved
        return got

    def interleave(self, a, b):
        n = max(len(a), len(b))
        for i in range(n):
            if i < len(a):
                self.ops.append(a[i])
            if i < len(b):
                self.ops.append(b[i])

    def analyze(self):
        last_w = {}
        readers = {}
        ops = self.ops
        last_by_q = {}
        for i, op in enumerate(ops):
            deps = set()
            if op.get('barrier'):
                deps.update(last_by_q.values())
            last_by_q[('dma', op['dma']) if op['dma'] is not None else ('eng', op['eng'])] = i
            for res in op['r']:
                if res in last_w:
                    deps.add(last_w[res])
            for res in op['w']:
                if res in last_w:
                    deps.add(last_w[res])
                deps.update(readers.get(res, ()))
            final = []
            for d in deps:
                if d == i:
                    continue
                dop = ops[d]
                if dop['eng'] == op['eng'] and dop['dma'] is None and op['dma'] is None:
                    if op['eng'] == 'pe':
                        continue
                    if not (set(dop['w']) & set(op['r'])):
                        continue
                final.append(d)
            best = {}
            keep = []
            for d in final:
                if ops[d]['dma'] is None:
                    k = ops[d]['eng']
                    if k not in best or d > best[k]:
                        best[k] = d
                else:
                    keep.append(d)
            final = keep + list(best.values())
            op['deps'] = final
            for d in final:
                ops[d]['signal'] = True
            for res in op['w']:
                last_w[res] = i
                readers[res] = []
            for res in op['r']:
                if res not in op['w']:
                    readers.setdefault(res, []).append(i)
        cnt = {}
        for op in ops:
            if op['dma'] is not None:
                op['signal'] = True
            if op['signal']:
                key = ('dma', op['dma']) if op['dma'] is not None else ('eng', op['eng'])
                inc = 16 if op['dma'] is not None else 1
                cnt[key] = cnt.get(key, 0) + inc
                op['sig'] = (key, cnt[key])
        self.keys = list(cnt.keys())

    def emit(self, es):
        nc = self.nc
        self.analyze()
        sems = {}
        for n, k in enumerate(self.keys):
            sems[k] = es.enter_context(nc.semaphore("s%d" % n))
        block = es.enter_context(nc.Block())
        ops = self.ops

        def run(engname):
            def body(e):
                waited = {}
                for op in ops:
                    if op['eng'] != engname:
                        continue
                    need = {}
                    for d in op['deps']:
                        k, v = ops[d]['sig']
                        if v > need.get(k, 0):
                            need[k] = v
                    for k, v in need.items():
                        if v > waited.get(k, 0):
                            e.wait_ge(sems[k], v)
                            waited[k] = v
                    ins = op['fn'](e)
                    if op['signal']:
                        k, v = op['sig']
                        if ins is None:
                            ins = e.nop()
                        ins.then_inc(sems[k], 16 if op['dma'] is not None else 1)
            return body

        block.tensor(run('pe'))
        block.scalar(run('act'))
        block.vector(run('dve'))
        block.gpsimd(run('pool'))
        block.sync(run('sp'))


def build(stage=99):
    nc = bass.Bass("TRN2", target_bir_lowering=False)
    es = ExitStack()
    es.enter_context(nc.allow_low_precision("bf16 matmul operands, fp32 accumulation"))
    P = Prog(nc)

    def din(name, shape, dt=F32):
        return nc.dram_tensor(name, list(shape), dt, kind="ExternalInput").ap()

    def dout(name, shape, dt=F32):
        return nc.dram_tensor(name, list(shape), dt, kind="ExternalOutput").ap()

    def sb(name, shape, dt=F32):
        return es.enter_context(nc.sbuf_tensor(name, list(shape), dt))

    PE = lambda fn, r, w: P.add('pe', fn, r, w)
    ACT = lambda fn, r, w: P.add('act', fn, r, w)
    DVE = lambda fn, r, w: P.add('dve', fn, r, w)
    POOL = lambda fn, r, w: P.add('pool', fn, r, w)
    DMA = lambda q, fn, r, w, key: P.add(q, fn, r, w, dma=key)

    x_loc = din("x_loc", [NOWN + NOTH, D])
    ctx_loc = din("ctx_loc", [LCTX, D])
    cvec = din("cvec", [128, KC * 2])
    w_ada_l = din("w_ada_l", [128, KC, 3 * D])
    b_ada2 = din("b_ada2", [2, 3 * D])
    gain_fm = din("gain_fm", [128, KC])
    ident_d = din("ident", [128, 128])
    sel_d = din("sel", [2, 128])

    psb = [es.enter_context(nc.psum_tensor("ps%d" % i, [128, 512], F32)) for i in range(8)]

    ident = sb("ident_sb", [128, 128])
    sel = sb("sel_sb", [2, 128])
    DMA('sp', lambda e: e.dma_start(out=ident[:], in_=ident_d), [], ['ident'], 'c0')
    DMA('sp', lambda e: e.dma_start(out=sel[:], in_=sel_d), [], ['sel'], 'c1')

    w_h = din("w_h", [16, 5, 128, KC * 128])
    lbl_d = din("lbl", [128, 64])
    hgain_d = din("hgain_fm", [128, 16])
    mreset_d = din("mreset", [128, 512])
    maskab_d = din("maskab", [64, 128])
    gate_dram = nc.dram_tensor("gate_scr", [1, D], F32).ap()
    hg_spill = nc.dram_tensor("hg_spill", [D, NOWN], BF16).ap()

    ARENA_W = 29 * 1024
    arena = sb("arena", [128, ARENA_W])
    apos = [0]

    def a_reset():
        apos[0] = 0

    def a_f32(n):
        o = apos[0]
        apos[0] += n
        assert apos[0] <= ARENA_W, apos[0]
        return arena[:, o:o + n]

    def a_bf16(n):
        w = (n + 1) // 2
        return a_f32(w).bitcast(BF16)

    identb = sb("identb", [128, 128], BF16)
    onesb = sb("onesb", [128, 128], BF16)
    mreset = sb("mreset_sb", [128, 512])
    maskab = sb("maskab_sb", [64, 128])
    lbl = sb("lbl_sb", [128, 64])
    lb = sb("lb_sb", [128, 32])
    negoml = sb("negoml", [128, 32])
    oml = sb("oml", [128, 32])
    hgain = sb("hgain_sb", [128, 16])
    gfm = sb("gfm", [128, KC])
    modcol = sb("modcol", [128, 64])
    Avec = sb("Avec", [128, 2, KC])
    Bvec = sb("Bvec", [128, 2, KC])
    xnT = sb("xnT_main", [128, KC, NOWN], BF16)
    xnT_ctx = sb("xnT_ctx", [128, KC, LCTX], BF16)
    xnT_halo = sb("xnT_halo", [128, KC, 128], BF16)
    SinitB = sb("SinitB", [128, 16, 128])
    c_sb = sb("c_sb", [128, KC * 2])
    sc_sb = sb("sc_sb", [128, KC * 2])

    DMA('sp', lambda e: e.dma_start(out=mreset[:], in_=mreset_d), [], ['mreset'], 'c5')
    DMA('sp', lambda e: e.dma_start(out=maskab[:], in_=maskab_d), [], ['maskab'], 'c6')
    DMA('sp', lambda e: e.dma_start(out=lbl[:], in_=lbl_d), [], ['lbl'], 'c7')
    DMA('sp', lambda e: e.dma_start(out=hgain[:], in_=hgain_d), [], ['hgain'], 'c8')
    DMA('sp', lambda e: e.dma_start(out=c_sb[:], in_=cvec), [], ['c_sb'], 'c2')
    DMA('sp', lambda e: e.dma_start(out=gfm[:], in_=gain_fm), [], ['gfm'], 'c4')
    DVE(lambda e: e.tensor_copy(out=identb[:], in_=ident[:]), ['ident'], ['identb'])
    DVE(lambda e: e.memset(onesb[:], 1.0), [], ['onesb'])
    lbl4 = lbl[:].rearrange("p (d l h) -> p d l h", d=2, l=2)
    lb3 = lb[:].rearrange("p (d h) -> p d h", d=2)
    DVE(lambda e: e.tensor_tensor(out=lb3, in0=lbl4[:, :, 0, :], in1=lbl4[:, :, 1, :], op=ALU.subtract), ['lbl'], ['lb'])
    ACT(lambda e: e.activation(out=lb[:], in_=lb[:], func=AF.Sigmoid), ['lb'], ['lb'])
    DVE(lambda e: e.tensor_scalar(out=negoml[:], in0=lb[:], scalar1=-1.0, scalar2=None, op0=ALU.add), ['lb'], ['negoml'])
    DVE(lambda e: e.tensor_scalar(out=oml[:], in0=lb[:], scalar1=-1.0, scalar2=1.0, op0=ALU.mult, op1=ALU.add), ['lb'], ['oml'])

    ACT(lambda e: e.activation(out=sc_sb[:], in_=c_sb[:], func=AF.Silu), ['c_sb'], ['sc_sb'])
    a_reset()
    wada = [a_f32(KC * 512).rearrange("p (k c) -> p k c", k=KC) for _ in range(3)]
    badab = [a_f32(512) for _ in range(2)]
    mod2b = [a_f32(512) for _ in range(2)]
    for blk in range(12):
        s = blk % 2
        ws = blk % 3
        DMA('sp' if blk % 2 == 0 else 'act',
            lambda e, ws=ws, blk=blk: e.dma_start(out=wada[ws], in_=w_ada_l[:, :, blk * 512:(blk + 1) * 512]),
            [], [('wada', ws)], 'wada%d' % ws)
        DMA('sp', lambda e, s=s, blk=blk: e.dma_start(out=badab[s][0:2, :], in_=b_ada2[:, blk * 512:(blk + 1) * 512]),
            [], [('badab', s)], 'badab%d' % s)
        for kc in range(KC):
            PE(lambda e, s=s, ws=ws, kc=kc: e.matmul(psb[s][0:2, :], lhsT=sc_sb[:, 2 * kc:2 * kc + 2],
                                              rhs=wada[ws][:, kc, :], start=(kc == 0), stop=(kc == KC - 1)),
               ['sc_sb', ('wada', ws)], [('ps', s)])
        DVE(lambda e, s=s: e.tensor_tensor(out=mod2b[s][0:2, :], in0=psb[s][0:2, :], in1=badab[s][0:2, :], op=ALU.add),
            [('ps', s), ('badab', s)], [('mod2b', s)])
        if blk < 8:
            for jj in range(4):
                j = blk * 4 + jj
                PE(lambda e, s=s, j=j, jj=jj: e.matmul(psb[2][:, 2 * j:2 * j + 2], lhsT=mod2b[s][0:2, jj * 128:(jj + 1) * 128],
                                                       rhs=ident[0:2, 0:2], start=True, stop=True),
                   [('mod2b', s), 'ident'], [('ps', 2)])
        else:
            DMA('sp', lambda e, s=s, blk=blk: e.dma_start(out=gate_dram[0:1, (blk - 8) * 512:(blk - 7) * 512], in_=mod2b[s][0:1, :]),
                [('mod2b', s)], ['gate_dram'], 'gd%d' % s)
    DVE(lambda e: e.tensor_copy(out=modcol[:], in_=psb[2][:, 0:64]), [('ps', 2)], ['modcol'])
    mc3 = modcol[:].rearrange("p (j t) -> p j t", t=2)
    for t in range(2):
        DVE(lambda e, t=t: e.scalar_tensor_tensor(out=Avec[:, t, :], in0=mc3[:, 16:32, t], scalar=1.0, in1=gfm[:],
                                                  op0=ALU.add, op1=ALU.mult), ['modcol', 'gfm'], ['Avec'])
        DVE(lambda e, t=t: e.tensor_copy(out=Bvec[:, t, :], in_=mc3[:, 0:16, t]), ['modcol'], ['Bvec'])
    P.barrier()

    def alloc_xt():
        a_reset()
        d = dict(xt=[a_f32(D) for _ in range(3)], junk=a_bf16(D),
                 ssq=[a_f32(1) for _ in range(3)], rstd=[a_f32(1) for _ in range(3)])
        return d

    def xn_stats(X, n, src_rows):
        s = n % 3
        xt, junk, ssq, rstd = X['xt'], X['junk'], X['ssq'], X['rstd']
        DMA('sp', lambda e: e.dma_start(out=xt[s], in_=src_rows), [], [('xt', s)], 'xt%d' % s)
        DVE(lambda e: e.memset(ssq[s], 0.0), [], [('ssq', s)])
        ACT(lambda e: e.activation(out=junk, in_=xt[s], func=AF.Square, accum_out=ssq[s]),
            [('xt', s), ('ssq', s)], [('ssq', s), 'junk'])
        DVE(lambda e: e.tensor_scalar(out=rstd[s], in0=ssq[s], scalar1=1.0 / D, scalar2=EPS,
                                      op0=ALU.mult, op1=ALU.add), [('ssq', s)], [('rstd', s)])
        ACT(lambda e: e.sqrt(out=rstd[s], in_=rstd[s]), [('rstd', s)], [('rstd', s)])
        DVE(lambda e: e.reciprocal(out=rstd[s], in_=rstd[s]), [('rstd', s)], [('rstd', s)])
        DVE(lambda e: e.tensor_scalar(out=xt[s], in0=xt[s], scalar1=rstd[s][:, 0:1], scalar2=None, op0=ALU.mult),
            [('xt', s), ('rstd', s)], [('xt', s)])

    def xn_transpose(X, n, dst, dst_res, which):
        s = n % 3
        pb = n % 2
        xt = X['xt']
        for j in range(KC):
            bnk = j // 4 + 4 * pb
            PE(lambda e, j=j, bnk=bnk: e.transpose(out=psb[bnk][:, (j % 4) * 128:(j % 4 + 1) * 128],
                                                   in_=xt[s][:, j * 128:(j + 1) * 128], identity=ident[:]),
               [('xt', s), 'ident'], [('ps', bnk)])
        for j in range(KC):
            bnk = j // 4 + 4 * pb
            ACT(lambda e, j=j, bnk=bnk: e.activation(out=dst[:, j, :], in_=psb[bnk][:, (j % 4) * 128:(j % 4 + 1) * 128],
                                                     func=AF.Identity, scale=Avec[:, which, j:j + 1],
                                                     bias=Bvec[:, which, j:j + 1]),
                [('ps', bnk), 'Avec', 'Bvec'], [dst_res])

    def build_xnT_seq(X, tiles):
        for n, tl in enumerate(tiles):
            if n == 0:
                xn_stats(X, 0, tl[0])
            if n + 1 < len(tiles):
                xn_stats(X, n + 1, tiles[n + 1][0])
            xn_transpose(X, n, tl[1], tl[2], tl[3])

    X = alloc_xt()
    tiles = [(ctx_loc[t * 128:(t + 1) * 128, :], xnT_ctx[:, :, t * 128:(t + 1) * 128], 'xnT_ctx', 1) for t in range(2)]
    tiles += [(x_loc[NOWN + t * 128:NOWN + (t + 1) * 128, :], xnT[:, :, t * 128:(t + 1) * 128], 'xnT', 0)
              for t in range(NOTH // 128)]
    build_xnT_seq(X, tiles)
    POOL(lambda e: e.tensor_copy(out=xnT_halo[:], in_=xnT[:, :, 0:128]), ['xnT'], ['xnT_halo'])
    P.barrier()
    NWS = 8
    wctr = [0]
    onecol = sb("onecol", [128, 1])
    DVE(lambda e: e.memset(onecol[:], 1.0), [], ['onecol'])

    def alloc_hgrn():
        a_reset()
        H = {}
        H['wh'] = [a_bf16(KC * 128).rearrange("p (k c) -> p k c", k=KC) for _ in range(NWS)]
        H['T'] = [{nm: a_f32(512) for nm in ('ks', 'f', 'G', 'E', 'X', 'Eq', 'Ek')} for _ in range(2)]
        H['qf'] = a_f32(512)
        H['rs'] = H['T'][1]['ks']
        H['tt'] = H['T'][1]['f']
        H['qp'] = [a_bf16(NOWN) for _ in range(2)]
        H['kp'] = [a_bf16(NOWN) for _ in range(2)]
        H['gate'] = a_bf16(NOWN)
        H['vT'] = a_bf16(512)
        H['vtm'] = a_bf16(32 * 128).rearrange("p (c v) -> p c v", v=128)
        H['ktm'] = a_bf16(4 * 128).rearrange("p (c v) -> p c v", v=128)
        H['Sp'] = a_bf16(4 * 128).rearrange("p (c v) -> p c v", v=128)
        H['Am'] = a_bf16(256)
        H['tmpU'] = a_f32(4 * 128).rearrange("p (c v) -> p c v", v=128)
        H['oacc'] = a_f32(NOWN)
        H['SA'] = a_f32(128)
        H['abc'] = a_f32(3 * 2 * 32).rearrange("p (k d c) -> p k d c", k=3, d=2)
        H['sq'] = a_bf16(512)
        H['ktm2'] = [H['ktm'], a_bf16(4 * 128).rearrange("p (c v) -> p c v", v=128)]
        H['Sp2'] = [H['Sp'], a_bf16(4 * 128).rearrange("p (c v) -> p c v", v=128)]
        H['Am2'] = [H['Am'], a_bf16(256)]
        H['tmpU2'] = [H['tmpU'], a_f32(4 * 128).rearrange("p (c v) -> p c v", v=128)]
        return H

    def load_unit(H, h, u):
        slot = wctr[0] % NWS
        wctr[0] += 1
        DMA('pool', lambda e: e.dma_start(out=H['wh'][slot].rearrange("p k c -> p (k c)"), in_=w_h[h, u]),
            [], [('wh', slot)], 'wh%d' % slot)
        return slot

    pctr = [0]

    def proj(H, slot, xsrc, n):
        bnk = pctr[0] % 4
        pctr[0] += 1
        for kc in range(KC):
            PE(lambda e, kc=kc: e.matmul(psb[bnk][:, 0:n], lhsT=H['wh'][slot][:, kc, :], rhs=xsrc[:, kc, :],
                                         start=(kc == 0), stop=(kc == KC - 1)),
               [('wh', slot), 'xnT', 'xnT_ctx'], [('ps', bnk)])
        return bnk

    def c3(ap, n):
        return ap[:, 0:n].rearrange("p (c t) -> p c t", t=64)

    def hgrn_elem(H, h, d, bnk, col0, n):
        T = H['T'][d]
        nch = n // 64
        cb = col0 // 64
        ks, f, G, E, X_, Eq, Ek = (T[k] for k in ('ks', 'f', 'G', 'E', 'X', 'Eq', 'Ek'))
        qf = H['qf']
        R = lambda nm: (nm, d)
        hd = d * 16 + h
        ACT(lambda e: e.activation(out=ks[:, 0:n], in_=psb[bnk][:, 0:n], func=AF.Sigmoid, scale=-1.0),
            [('ps', bnk)], [R('ks')])
        DVE(lambda e: e.tensor_scalar(out=f[:, 0:n], in0=ks[:, 0:n], scalar1=negoml[:, hd:hd + 1], scalar2=1.0,
                                      op0=ALU.mult, op1=ALU.add), [R('ks'), 'negoml'], [R('f')])
        ACT(lambda e: e.activation(out=f[:, 0:n], in_=f[:, 0:n], func=AF.Ln), [R('f')], [R('f')])
        DVE(lambda e: e.tensor_tensor_scan(out=G[:, 0:n], data0=mreset[:, 0:n], data1=f[:, 0:n], initial=0.0,
                                           op0=ALU.mult, op1=ALU.add), [R('f'), 'mreset'], [R('G')])
        G3, E3, X3, Eq3 = c3(G, n), c3(E, n), c3(X_, n), c3(Eq, n)
        if d == 0:
            DVE(lambda e: e.tensor_tensor(out=X3, in0=G3, in1=G3[:, :, 31:32].to_broadcast([128, nch, 64]),
                                          op=ALU.subtract), [R('G')], [R('X')])
        else:
            DVE(lambda e: e.tensor_tensor(out=E[:, 0:n], in0=G[:, 0:n], in1=f[:, 0:n], op=ALU.subtract),
                [R('G'), R('f')], [R('E')])
            DVE(lambda e: e.tensor_tensor(out=X3, in0=E3[:, :, 32:33].to_broadcast([128, nch, 64]), in1=E3,
                                          op=ALU.subtract), [R('E')], [R('X')])
        ACT(lambda e: e.activation(out=Eq[:, 0:n], in_=X_[:, 0:n], func=AF.Exp), [R('X')], [R('Eq')])
        ACT(lambda e: e.activation(out=Ek[:, 0:n], in_=X_[:, 0:n], func=AF.Exp, scale=-1.0), [R('X')], [R('Ek')])
        DVE(lambda e: e.scalar_tensor_tensor(out=H['kp'][d][:, col0:col0 + n], in0=ks[:, 0:n], scalar=oml[:, hd:hd + 1],
                                             in1=Ek[:, 0:n], op0=ALU.mult, op1=ALU.mult),
            [R('ks'), R('Ek'), 'oml'], [('kp', d)])
        DVE(lambda e: e.tensor_tensor(out=H['qp'][d][:, col0:col0 + n], in0=qf[:, 0:n], in1=Eq[:, 0:n], op=ALU.mult),
            ['qf', R('Eq')], [('qp', d)])
        abc = H['abc']
        ACT(lambda e: e.activation(out=abc[:, 0, d, cb:cb + nch], in_=G3[:, :, 63], func=AF.Exp), [R('G')], [('abc', d)])
        if d == 0:
            DVE(lambda e: e.tensor_copy(out=abc[:, 1, d, cb:cb + nch], in_=Eq3[:, :, 63]), [R('Eq')], [('abc', d)])
            ACT(lambda e: e.activation(out=abc[:, 2, d, cb:cb + nch], in_=G3[:, :, 31], func=AF.Exp), [R('G')], [('abc', d)])
        else:
            DVE(lambda e: e.tensor_copy(out=abc[:, 1, d, cb:cb + nch], in_=Eq3[:, :, 0]), [R('Eq')], [('abc', d)])
            DVE(lambda e: e.tensor_tensor(out=abc[:, 2, d, cb:cb + nch], in0=G3[:, :, 63], in1=E3[:, :, 32],
                                          op=ALU.subtract), [R('G'), R('E')], [('abc', d)])
            ACT(lambda e: e.activation(out=abc[:, 2, d, cb:cb + nch], in_=abc[:, 2, d, cb:cb + nch], func=AF.Exp),
                [('abc', d)], [('abc', d)])

    def build_vtm(H, bnk, col0, n):
        nch = n // 64
        cb = col0 // 64
        if bnk is not None:
            ACT(lambda e: e.activation(out=H['vT'][:, 0:n], in_=psb[bnk][:, 0:n], func=AF.Copy), [('ps', bnk)], ['vT'])
        for g0 in range(0, nch, 4):
            bk = 4 + (g0 // 4) % 2
            for j in range(4):
                PE(lambda e, j=j, g0=g0, bk=bk: e.matmul(psb[bk][0:64, j * 128:(j + 1) * 128],
                                                         lhsT=H['vT'][:, (g0 + j) * 64:(g0 + j + 1) * 64], rhs=identb[:],
                                                         start=True, stop=True), ['vT', 'identb'], [('ps', bk)])
        for g0 in range(0, nch, 4):
            bk = 4 + (g0 // 4) % 2
            ACT(lambda e, g0=g0, bk=bk: e.activation(out=H['vtm'][0:64, cb + g0:cb + g0 + 4, :],
                                                     in_=psb[bk][0:64, :].rearrange("p (c v) -> p c v", v=128), func=AF.Copy),
                [('ps', bk)], ['vtm'])

    def state_block(H, h, d, bz, bv, n, S, sres, par=0):
        T = H['T'][par]
        ks, f, G, E, Ek = (T[k] for k in ('ks', 'f', 'G', 'E', 'Ek'))
        R = lambda nm: (nm, par)
        hd = d * 16 + h
        nt = n // 128
        kpb = H['kp'][par]
        vT = H['vT'] if par == 0 else H['sq']
        ktm = H['ktm'] if par == 0 else H['Sp']
        vtm = H['vtm'][:, 4 * par:4 * par + 4, :]
        bT = 4 if par == 0 else 7
        ab = H['abc'][:, 0, par, 0:1]
        nV = 'vT' if par == 0 else ('vT', 1)
        nK = 'ktm' if par == 0 else ('ktm', 1)
        nVt = 'vtm' if par == 0 else ('vtm', 1)
        nP5 = ('ps', 5) if par == 0 else ('ps', 7)
        bV = 6 if par == 0 else 7
        bU_ = 5 if par == 0 else 7
        ACT(lambda e: e.activation(out=ks[:, 0:n], in_=psb[bz][:, 0:n], func=AF.Sigmoid, scale=-1.0), [('ps', bz)], [R('ks')])
        ACT(lambda e: e.activation(out=vT[:, 0:n], in_=psb[bv][:, 0:n], func=AF.Copy), [('ps', bv)], [nV])
        DVE(lambda e: e.tensor_scalar(out=f[:, 0:n], in0=ks[:, 0:n], scalar1=negoml[:, hd:hd + 1], scalar2=1.0,
                                      op0=ALU.mult, op1=ALU.add), [R('ks'), 'negoml'], [R('f')])
        ACT(lambda e: e.activation(out=f[:, 0:n], in_=f[:, 0:n], func=AF.Ln), [R('f')], [R('f')])
        DVE(lambda e: e.tensor_tensor_scan(out=G[:, 0:n], data0=onecol[:, 0:1].to_broadcast([128, n]), data1=f[:, 0:n],
                                           initial=0.0, op0=ALU.mult, op1=ALU.add), [R('f'), 'onecol'], [R('G')])
        if d == 0:
            ACT(lambda e: e.activation(out=Ek[:, 0:n], in_=G[:, 0:n], func=AF.Exp, scale=-1.0, bias=G[:, n - 1:n]),
                [R('G')], [R('Ek')])
        else:
            DVE(lambda e: e.tensor_tensor(out=E[:, 0:n], in0=G[:, 0:n], in1=f[:, 0:n], op=ALU.subtract),
                [R('G'), R('f')], [R('E')])
            ACT(lambda e: e.activation(out=Ek[:, 0:n], in_=E[:, 0:n], func=AF.Exp), [R('E')], [R('Ek')])
        ACT(lambda e: e.activation(out=ab, in_=G[:, n - 1:n], func=AF.Exp), [R('G')], [('abc', par)])
        DVE(lambda e: e.scalar_tensor_tensor(out=kpb[:, 0:n], in0=ks[:, 0:n], scalar=oml[:, hd:hd + 1],
                                             in1=Ek[:, 0:n], op0=ALU.mult, op1=ALU.mult),
            [R('ks'), R('Ek'), 'oml'], [('kp', par)])
        for t in range(nt):
            PE(lambda e, t=t: e.matmul(psb[bT][:, t * 128:(t + 1) * 128], lhsT=kpb[:, t * 128:(t + 1) * 128],
                                       rhs=identb[:], start=True, stop=True), [('kp', par), 'identb'], [('ps', bT)])
        ACT(lambda e: e.activation(out=ktm[:, 0:nt, :], in_=psb[bT][:, 0:nt * 128].rearrange("p (c v) -> p c v", v=128),
                                   func=AF.Copy), [('ps', bT)], [nK])
        for t in range(nt):
            PE(lambda e, t=t: e.matmul(psb[bV][:, t * 128:(t + 1) * 128], lhsT=vT[:, t * 128:(t + 1) * 128],
                                       rhs=identb[:], start=True, stop=True), [nV, 'identb'], [('ps', bV)])
        ACT(lambda e: e.activation(out=vtm[:, 0:nt, :], in_=psb[bV][:, 0:nt * 128].rearrange("p (c v) -> p c v", v=128),
                                   func=AF.Copy), [('ps', bV)], [nVt])
        for t in range(nt):
            PE(lambda e, t=t: e.matmul(psb[bU_][:, 0:128], lhsT=ktm[:, t, :], rhs=vtm[:, t, :],
                                       start=(t == 0), stop=(t == nt - 1)), [nK, nVt], [nP5])
        DVE(lambda e: e.scalar_tensor_tensor(out=S, in0=S, scalar=ab, in1=psb[bU_][:, 0:128],
                                             op0=ALU.mult, op1=ALU.add), [sres, nP5, ('abc', par)], [sres])

    def hgrn_chain(H, d, S, sres, groups, bset):
        abc = H['abc']
        ktm, Sp, Am, tmpU = H['ktm2'][bset], H['Sp2'][bset], H['Am2'][bset], H['tmpU2'][bset]
        bT, bU, bA, bO = (4, 5, 6, 7) if bset == 0 else (0, 1, 2, 3)
        nK = 'ktm' if bset == 0 else ('ktm', 'b')
        nA = 'Am' if bset == 0 else ('Am', 'b')
        nS = (lambda j: ('Sp', j)) if bset == 0 else (lambda j: ('Sp', 'b', j))
        nU = (lambda j: ('tmpU', j)) if bset == 0 else (lambda j: ('tmpU', 'b', j))
        for g0 in groups:
            order = range(4) if d == 0 else range(3, -1, -1)
            for j in range(4):
                c = g0 + j
                PE(lambda e, j=j, c=c: e.matmul(psb[bT][0:64, j * 128:(j + 1) * 128],
                                                lhsT=H['kp'][d][:, c * 64:(c + 1) * 64], rhs=identb[:],
                                                start=True, stop=True), [('kp', d), 'identb'], [('ps', bT)])
            ACT(lambda e: e.activation(out=ktm[0:64, :, :], in_=psb[bT][0:64, :].rearrange("p (c v) -> p c v", v=128),
                                       func=AF.Copy), [('ps', bT)], [nK])
            for j in range(4):
                c = g0 + j
                PE(lambda e, j=j, c=c: e.matmul(psb[bU][:, j * 128:(j + 1) * 128], lhsT=ktm[0:64, j, :],
                                                rhs=H['vtm'][0:64, c, :], start=True, stop=True),
                   [nK, 'vtm'], [('ps', bU)])
            for j in range(4):
                c = g0 + j
                PE(lambda e, j=j, c=c: e.matmul(psb[bA][0:64, j * 64:(j + 1) * 64], lhsT=H['kp'][d][:, c * 64:(c + 1) * 64],
                                                rhs=H['qp'][d][:, c * 64:(c + 1) * 64], start=True, stop=True),
                   [('kp', d), ('qp', d)], [('ps', bA)])
            DVE(lambda e: e.tensor_tensor(out=Am[0:64, 0:256].rearrange("p (c t) -> p c t", t=64),
                                          in0=psb[bA][0:64, 0:256].rearrange("p (c t) -> p c t", t=64),
                                          in1=maskab[:, d * 64:(d + 1) * 64].unsqueeze(1).to_broadcast([64, 4, 64]),
                                          op=ALU.mult), [('ps', bA), 'maskab'], [nA])
            for j in order:
                c = g0 + j
                ACT(lambda e, j=j, c=c: e.activation(out=tmpU[:, j, :], in_=psb[bU][:, j * 128:(j + 1) * 128],
                                                     func=AF.Copy, scale=abc[:, 1, d, c:c + 1]),
                    [('ps', bU), ('abc', d)], [nU(j)])
            for j in order:
                c = g0 + j
                DVE(lambda e, j=j, c=c: e.tensor_scalar(out=Sp[:, j, :], in0=S, scalar1=abc[:, 2, d, c:c + 1],
                                                        scalar2=None, op0=ALU.mult), [sres, ('abc', d)], [nS(j)])
                DVE(lambda e, j=j, c=c: e.scalar_tensor_tensor(out=S, in0=S, scalar=abc[:, 0, d, c:c + 1],
                                                               in1=tmpU[:, j, :], op0=ALU.mult, op1=ALU.add),
                    [sres, nU(j), ('abc', d)], [sres])
            for j in range(4):
                c = g0 + j
                PE(lambda e, j=j, c=c: e.matmul(psb[bO][:, j * 64:(j + 1) * 64], lhsT=Sp[:, j, :],
                                                rhs=H['qp'][d][:, c * 64:(c + 1) * 64], start=True, stop=False),
                   [nS(j), ('qp', d)], [('ps', bO)])
                PE(lambda e, j=j, c=c: e.matmul(psb[bO][:, j * 64:(j + 1) * 64], lhsT=H['vtm'][0:64, c, :],
                                                rhs=Am[0:64, j * 64:(j + 1) * 64], start=False, stop=True),
                   ['vtm', nA], [('ps', bO)])
            oc = H['oacc'][:, g0 * 64:g0 * 64 + 256]
            first = (g0 < 64) if d == 0 else (g0 >= 64)
            first = (g0 < 16) if d == 0 else (g0 >= 16)
            if first:
                DVE(lambda e, oc=oc: e.tensor_copy(out=oc, in_=psb[bO][:, 0:256]), [('ps', bO)], [('oacc', g0)])
            else:
                DVE(lambda e, oc=oc: e.tensor_tensor(out=oc, in0=oc, in1=psb[bO][:, 0:256], op=ALU.add),
                    [('ps', bO), ('oacc', g0)], [('oacc', g0)])

    H = alloc_hgrn()
    seq1 = []
    for h in range({99: 16, 1: DBG_HEADS, 2: 0}[stage]):
        seq1.append((h, xnT_ctx[:], LCTX))
        for tb in range(3, -1, -1):
            seq1.append((h, xnT[:, :, tb * 512:(tb + 1) * 512], 512))
    p1slots = {}

    def p1_units(h):
        if h not in p1slots:
            p1slots[h] = (load_unit(H, h, 2), load_unit(H, h, 3))
            S_ = SinitB[:, h, :]
            DVE(lambda e, S_=S_: e.memset(S_, 0.0), [], ['SB'])
        return p1slots[h]

    for i in range(0, len(seq1), 2):
        lists = []
        for par in range(2):
            if i + par >= len(seq1):
                lists.append([])
                continue
            h, xs_, n_ = seq1[i + par]
            sB, sV = p1_units(h)
            bz = proj(H, sB, xs_, n_)
            bv = proj(H, sV, xs_, n_)
            P.capture()
            state_block(H, h, 1, bz, bv, n_, SinitB[:, h, :], 'SB', par)
            lists.append(P.release())
        P.interleave(lists[0], lists[1])
    P.barrier()
    X = alloc_xt()
    build_xnT_seq(X, [(x_loc[t * 128:(t + 1) * 128, :], xnT[:, :, t * 128:(t + 1) * 128], 'xnT', 0)
                      for t in range(NOWN // 128)])
    P.barrier()

    H = alloc_hgrn()
    NH = {99: 16, 1: DBG_HEADS, 2: 0}[stage]
    for h in range(NH):
        su = [load_unit(H, h, u) for u in range(5)]
        SA = H['SA']
        DVE(lambda e: e.memset(SA, 0.0), [], ['SA'])
        bz = proj(H, su[1], xnT_ctx[:], LCTX)
        bv = proj(H, su[3], xnT_ctx[:], LCTX)
        state_block(H, h, 0, bz, bv, LCTX, SA, 'SA')
        for tb in range(4):
            if True:
                xs = xnT[:, :, tb * 512:(tb + 1) * 512]
                bza = proj(H, su[1], xs, 512)
                bzb = proj(H, su[2], xs, 512)
                bv = proj(H, su[3], xs, 512)
                bq = proj(H, su[0], xs, 512)
                P.capture()
                hgrn_elem(H, h, 0, bza, tb * 512, 512)
                la = P.release()
                P.capture()
                hgrn_elem(H, h, 1, bzb, tb * 512, 512)
                lb_ = P.release()
                P.capture()
                ACT(lambda e, bv=bv: e.activation(out=H['vT'][:, 0:512], in_=psb[bv][:, 0:512], func=AF.Copy), [('ps', bv)], ['vT'])
                ACT(lambda e, bq=bq: e.activation(out=H['qf'][:, :], in_=psb[bq][:, :], func=AF.Silu), [('ps', bq)], ['qf'])
                lc = P.release()
                P.interleave(la[:3], lb_[:3])
                P.interleave(lc, [])
                bg = proj(H, su[4], xs, 512)
                P.interleave(la[3:], lb_[3:])
                build_vtm(H, None, tb * 512, 512)
                ACT(lambda e, bg=bg, tb=tb: e.activation(out=H['gate'][:, tb * 512:(tb + 1) * 512], in_=psb[bg][:, :], func=AF.Silu),
                    [('ps', bg)], ['gate'])
        P.capture()
        hgrn_chain(H, 0, SA, 'SA', list(range(0, 32, 4)), 0)
        lca = P.release()
        P.capture()
        hgrn_chain(H, 1, SinitB[:, h, :], 'SB', list(range(28, -1, -4)), 1)
        lcb = P.release()
        P.interleave(lca, lcb)
        for tb in range(4):
            oc = H['oacc'][:, tb * 512:(tb + 1) * 512]
            ocr = [('oacc', tb * 8), ('oacc', tb * 8 + 4)]
            ACT(lambda e, oc=oc: e.activation(out=H['sq'][:, :], in_=oc, func=AF.Square), ocr, ['sq'])
            PE(lambda e: e.matmul(psb[6][:, :], lhsT=onesb[:], rhs=H['sq'][:, :], start=True, stop=True),
               ['sq', 'onesb'], [('ps', 6)])
            DVE(lambda e: e.tensor_scalar(out=H['rs'][:, :], in0=psb[6][:, :], scalar1=1.0 / 128, scalar2=EPS,
                                          op0=ALU.mult, op1=ALU.add), [('ps', 6)], [('ks', 1)])
            ACT(lambda e: e.activation(out=H['rs'][:, :], in_=H['rs'][:, :], func=AF.Ln), [('ks', 1)], [('ks', 1)])
            ACT(lambda e: e.activation(out=H['rs'][:, :], in_=H['rs'][:, :], func=AF.Exp, scale=-0.5), [('ks', 1)], [('ks', 1)])
            DVE(lambda e, oc=oc: e.tensor_tensor(out=H['tt'][:, :], in0=oc, in1=H['rs'][:, :], op=ALU.mult),
                ocr + [('ks', 1)], [('f', 1)])
            DVE(lambda e, tb=tb, h=h: e.scalar_tensor_tensor(out=H['gate'][:, tb * 512:(tb + 1) * 512], in0=H['tt'][:, :],
                                                             scalar=hgain[:, h:h + 1], in1=H['gate'][:, tb * 512:(tb + 1) * 512],
                                                             op0=ALU.mult, op1=ALU.mult), [('f', 1), 'gate', 'hgain'], ['gate'])
        DMA('sp', lambda e, h=h: e.dma_start(out=hg_spill[h * 128:(h + 1) * 128, :], in_=H['gate'][:, :]),
            ['gate'], ['hg_spill'], 'hgsp')
    P.barrier()

    if stage == 1:
        dbg_hg = dout("dbg_hg", [128 * NH, NOWN], BF16)
        dbg_sb = dout("dbg_sb", [128, 16 * 128])
        DMA('sp', lambda e: e.dma_start(out=dbg_hg, in_=hg_spill[0:128 * NH, :]), ['hg_spill'], ['o1'], 'o1')
        DMA('sp', lambda e: e.dma_start(out=dbg_sb, in_=SinitB[:].rearrange("p h v -> p (h v)")), ['SB'], ['o2'], 'o2')
        P.add('sp', lambda e: None, ['o1', 'o2'], [])
        P.emit(es)
        es.close()
        return nc
    w_att = din("w_att", [4, 9, 128, KC * 128])
    w_v = din("w_v", [128, KC * 512])
    cos_d = din("cosT", [128, NOWN + 128])
    sin_d = din("sinT", [128, NOWN + 128])
    rotT_d = din("rotT", [128, 128])
    sink_d = din("sink", [1, 16])
    bmask_d = din("bmask", [128, 2 * 512])
    att_spill = nc.dram_tensor("att_spill", [D, NOWN], BF16).ap()
    QSCALE = 128.0 ** -0.5
    NKT = NOWN // 128 + 1

    a_reset()
    A = {}
    A['wv_flat'] = a_bf16(KC * 512)
    A['wv'] = A['wv_flat'].rearrange("p (k c) -> p k c", k=KC)
    A['qT_all'] = A['wv_flat']
    A['wh'] = [a_bf16(KC * 128).rearrange("p (k c) -> p k c", k=KC) for _ in range(4)]
    A['V'] = a_bf16(NKT * 512).rearrange("p (t c) -> p t c", c=512)
    A['Vc'] = a_bf16(2 * 512).rearrange("p (t c) -> p t c", c=512)
    A['kT'] = a_bf16(NOWN + 128)
    A['kcT'] = a_bf16(LCTX)
    A['cos'] = a_f32(NOWN + 128)
    A['sin'] = a_f32(NOWN + 128)
    A['rotT'] = a_f32(128)
    A['qf'] = a_f32(512)
    A['t1'] = a_f32(512)
    A['t2'] = a_f32(512)
    A['gT_all'] = a_bf16(4 * 2048)
    A['ao'] = a_bf16(4 * 512)
    A['pT'] = [a_bf16(512) for _ in range(2)]
    A['den'] = a_f32(512)
    A['o1'] = a_f32(512)
    A['sinkrow'] = a_f32(512)
    A['sinkexp'] = a_f32(16)
    A['bmf'] = a_f32(1024)
    A['bm'] = a_bf16(1024)
    DMA('sp', lambda e: e.dma_start(out=A['cos'], in_=cos_d), [], ['cos'], 'p3a')
    DMA('sp', lambda e: e.dma_start(out=A['sin'], in_=sin_d), [], ['sin'], 'p3b')
    DMA('sp', lambda e: e.dma_start(out=A['rotT'], in_=rotT_d), [], ['rotT'], 'p3c')
    DMA('sp', lambda e: e.dma_start(out=A['bmf'], in_=bmask_d), [], ['bmf'], 'p3d')
    DMA('sp', lambda e: e.dma_start(out=A['sinkexp'], in_=sink_d.partition_broadcast(128)), [], ['sinkexp'], 'p3e')
    DMA('pool', lambda e: e.dma_start(out=A['wv'].rearrange("p k c -> p (k c)"), in_=w_v), [], ['wv'], 'p3f')
    ACT(lambda e: e.activation(out=A['sinkexp'], in_=A['sinkexp'], func=AF.Exp), ['sinkexp'], ['sinkexp'])
    DVE(lambda e: e.tensor_copy(out=A['bm'], in_=A['bmf']), ['bmf'], ['bm'])

    awctr = [0]

    def load_att(g, u):
        slot = awctr[0] % 4
        awctr[0] += 1
        DMA('pool', lambda e: e.dma_start(out=A['wh'][slot].rearrange("p k c -> p (k c)"), in_=w_att[g, u]),
            [], [('wh', slot)], 'wh%d' % slot)
        return slot

    apctr = [0]

    def aproj(slot, xsrc, n):
        bnk = apctr[0] % 3
        apctr[0] += 1
        for kc in range(KC):
            PE(lambda e, kc=kc: e.matmul(psb[bnk][:, 0:n], lhsT=A['wh'][slot][:, kc, :], rhs=xsrc[:, kc, :],
                                         start=(kc == 0), stop=(kc == KC - 1)),
               [('wh', slot), 'xnT', 'xnT_ctx', 'xnT_halo'], [('ps', bnk)])
        return bnk

    def vproj(xsrc, dst):
        bnk = apctr[0] % 3
        apctr[0] += 1
        for kc in range(KC):
            PE(lambda e, kc=kc: e.matmul(psb[bnk][:, :], lhsT=xsrc[:, kc, :], rhs=A['wv'][:, kc, :],
                                         start=(kc == 0), stop=(kc == KC - 1)),
               ['wv', 'xnT', 'xnT_ctx', 'xnT_halo'], [('ps', bnk)])
        ACT(lambda e: e.activation(out=dst, in_=psb[bnk][:, :], func=AF.Copy), [('ps', bnk)], ['V'])

    for t in range(NKT):
        src = xnT[:, :, t * 128:(t + 1) * 128] if t < NKT - 1 else xnT_halo[:]
        vproj(src, A['V'][:, t, :])
    for t in range(2):
        vproj(xnT_ctx[:, :, t * 128:(t + 1) * 128], A['Vc'][:, t, :])

    def rope(bnk, n, col0, dst):
        ACT(lambda e: e.activation(out=A['qf'][:, 0:n], in_=psb[bnk][:, 0:n], func=AF.Copy), [('ps', bnk)], ['qf'])
        PE(lambda e: e.matmul(psb[4][:, 0:n], lhsT=A['rotT'], rhs=A['qf'][:, 0:n], start=True, stop=True),
           ['qf', 'rotT'], [('ps', 4)])
        DVE(lambda e: e.tensor_tensor(out=A['t1'][:, 0:n], in0=A['qf'][:, 0:n], in1=A['cos'][:, col0:col0 + n], op=ALU.mult),
            ['qf', 'cos'], ['t1'])
        DVE(lambda e: e.tensor_tensor(out=A['t2'][:, 0:n], in0=psb[4][:, 0:n], in1=A['sin'][:, col0:col0 + n], op=ALU.mult),
            [('ps', 4), 'sin'], ['t2'])
        return ['t1', 't2'], dst

    for g in range(4 if stage != 2 else 1):
        DVE(lambda e, g=g: e.tensor_copy(out=A['sinkrow'].rearrange("p (i t) -> p i t", t=128),
                                         in_=A['sinkexp'][:, 4 * g:4 * g + 4].unsqueeze(2).to_broadcast([128, 4, 128])),
            ['sinkexp'], ['sinkrow'])
        sk = load_att(g, 4)
        for tb in range(5):
            n = 512 if tb < 4 else 128
            src = xnT[:, :, tb * 512:(tb + 1) * 512] if tb < 4 else xnT_halo[:]
            bk = aproj(sk, src, n)
            rope(bk, n, tb * 512, None)
            DVE(lambda e, tb=tb, n=n: e.tensor_tensor(out=A['kT'][:, tb * 512:tb * 512 + n], in0=A['t1'][:, 0:n],
                                                      in1=A['t2'][:, 0:n], op=ALU.add), ['t1', 't2'], ['kT'])
        bk = aproj(sk, xnT_ctx[:], LCTX)
        ACT(lambda e, bk=bk: e.activation(out=A['kcT'], in_=psb[bk][:, 0:LCTX], func=AF.Copy), [('ps', bk)], ['kcT'])
        for i in range(4):
            sq_ = load_att(g, i)
            for tb in range(4):
                xs = xnT[:, :, tb * 512:(tb + 1) * 512]
                qT4 = A['qT_all'][:, tb * 2048:(tb + 1) * 2048].rearrange("p (q i t) -> p q i t", q=4, i=4)
                bq = aproj(sq_, xs, 512)
                rope(bq, 512, tb * 512, None)
                DVE(lambda e, i=i, qT4=qT4: e.tensor_tensor(out=qT4[:, :, i, :], in0=A['t1'].rearrange("p (q t) -> p q t", t=128),
                                                            in1=A['t2'].rearrange("p (q t) -> p q t", t=128), op=ALU.add),
                    ['t1', 't2'], [('qT', tb), 'wv'])
        for i in range(4):
            sg = load_att(g, 5 + i)
            for tb in range(4):
                xs = xnT[:, :, tb * 512:(tb + 1) * 512]
                gT4 = A['gT_all'][:, tb * 2048:(tb + 1) * 2048].rearrange("p (q i t) -> p q i t", q=4, i=4)
                bg = aproj(sg, xs, 512)
                ACT(lambda e, i=i, bg=bg, gT4=gT4: e.activation(out=gT4[:, :, i, :], in_=psb[bg][:, :].rearrange("p (q t) -> p q t", t=128),
                                                                func=AF.Silu), [('ps', bg)], [('gT', tb)])

        def attn(tb):
            ao4 = A['ao'].rearrange("p (q i t) -> p q i t", q=4, i=4)
            for qb in range(4):
                Q = tb * 4 + qb
                kbs = []
                if Q >= 1:
                    kbs.append((A['kT'][:, (Q - 1) * 128:Q * 128], A['V'][:, Q - 1, g * 128:(g + 1) * 128], 0))
                kbs.append((A['kT'][:, Q * 128:(Q + 1) * 128], A['V'][:, Q, g * 128:(g + 1) * 128], None))
                kbs.append((A['kT'][:, (Q + 1) * 128:(Q + 2) * 128], A['V'][:, Q + 1, g * 128:(g + 1) * 128], 1))
                for t in range(2):
                    kbs.append((A['kcT'][:, t * 128:(t + 1) * 128], A['Vc'][:, t, g * 128:(g + 1) * 128], None))
                qrhs = A['qT_all'][:, tb * 2048 + qb * 512:tb * 2048 + (qb + 1) * 512]
                bO = 7 if Q % 2 == 0 else 0
                bD = 3 if Q % 2 == 0 else 1
                nk = len(kbs)

                def emit_S(ki):
                    kap, vap, mk = kbs[ki]
                    sl = (Q * 5 + ki) % 2
                    PE(lambda e, kap=kap, sl=sl, mk=mk, qrhs=qrhs: e.matmul(psb[5 + sl][:, :], lhsT=kap, rhs=qrhs, start=True, stop=(mk is None)),
                       ['kT', 'kcT', ('qT', tb)], [('ps', 5 + sl)])
                    if mk is not None:
                        PE(lambda e, sl=sl, mk=mk: e.matmul(psb[5 + sl][:, :], lhsT=identb[:], rhs=A['bm'][:, mk * 512:(mk + 1) * 512],
                                                            start=False, stop=True), ['identb', 'bm'], [('ps', 5 + sl)])

                emit_S(0)
                for ki in range(nk):
                    kap, vap, mk = kbs[ki]
                    sl = (Q * 5 + ki) % 2
                    if ki + 1 < nk:
                        emit_S(ki + 1)
                    ACT(lambda e, sl=sl: e.activation(out=A['pT'][sl], in_=psb[5 + sl][:, :], func=AF.Exp, scale=QSCALE),
                        [('ps', 5 + sl)], [('pT', sl)])
                    first, last = ki == 0, ki == nk - 1
                    PE(lambda e, vap=vap, sl=sl, first=first, last=last, bO=bO: e.matmul(psb[bO][:, :], lhsT=vap, rhs=A['pT'][sl],
                                                                                   start=first, stop=last),
                       ['V', ('pT', sl)], [('ps', bO)])
                    PE(lambda e, sl=sl, first=first, last=last, bD=bD: e.matmul(psb[bD][:, :], lhsT=onesb[:], rhs=A['pT'][sl],
                                                                          start=first, stop=last),
                       ['onesb', ('pT', sl)], [('ps', bD)])
                DVE(lambda e, bD=bD: e.tensor_tensor(out=A['den'], in0=psb[bD][:, :], in1=A['sinkrow'], op=ALU.add),
                    [('ps', bD), 'sinkrow'], ['den'])
                ACT(lambda e: e.activation(out=A['den'], in_=A['den'], func=AF.Ln), ['den'], ['den'])
                ACT(lambda e: e.activation(out=A['den'], in_=A['den'], func=AF.Exp, scale=-1.0), ['den'], ['den'])
                DVE(lambda e, bO=bO: e.tensor_tensor(out=A['o1'], in0=psb[bO][:, :], in1=A['den'], op=ALU.mult),
                    [('ps', bO), 'den'], ['o1'])
                DVE(lambda e, qb=qb: e.tensor_tensor(out=A['ao'][:, qb * 512:(qb + 1) * 512], in0=A['o1'],
                                                     in1=A['gT_all'][:, tb * 2048 + qb * 512:tb * 2048 + (qb + 1) * 512], op=ALU.mult),
                    ['o1', ('gT', tb)], ['ao'])
            for i in range(4):
                hh = 4 * g + i
                DMA('sp', lambda e, hh=hh, i=i, tb=tb: e.dma_start(
                    out=att_spill[hh * 128:(hh + 1) * 128, tb * 512:(tb + 1) * 512].rearrange("p (q t) -> p q t", t=128),
                    in_=ao4[:, :, i, :]), ['ao'], ['att_spill'], 'atsp')

        for tb in range(4):
            attn(tb)
    P.barrier()

    if stage == 2:
        dbg_at = dout("dbg_at", [512, NOWN], BF16)
        DMA('sp', lambda e: e.dma_start(out=dbg_at, in_=att_spill[0:512, :]), ['att_spill'], ['o1'], 'o1')
        P.add('sp', lambda e: None, ['o1'], [])
        P.emit(es)
        es.close()
        return nc

    w_fm = din("w_fm", [16, 4, 128, KC * 128])
    w_outd = din("w_out_l", [8, 128, KC * 256])
    bm_d = din("bmerge_fm", [128, 32])
    fg_d = din("fgain", [1, D])
    y_out = dout("y", [NOWN, D])
    TB4 = 512
    NB4 = NOWN // TB4
    NT4 = TB4 // 128

    a_reset()
    F = {}
    F['wh'] = [a_bf16(KC * 128).rearrange("p (k c) -> p k c", k=KC) for _ in range(3)]
    F['wh'].append(xnT_halo[:])
    F['wo'] = [a_bf16(KC * 256).rearrange("p (k c) -> p k c", k=KC) for _ in range(2)]
    F['hgT'] = a_bf16(KC * TB4).rearrange("p (k t) -> p k t", k=KC)
    F['atT'] = a_bf16(KC * TB4).rearrange("p (k t) -> p k t", k=KC)
    F['yT'] = a_bf16(KC * TB4).rearrange("p (k t) -> p k t", k=KC)
    F['hrow'] = [a_f32(D), a_f32(D), SinitB[:].rearrange("p h v -> p (h v)"),
                 xnT_ctx[:].rearrange("p k t -> p (k t)").bitcast(F32)]
    F['gate'] = a_f32(D)
    F['fg'] = a_f32(D)
    F['sa'] = a_f32(TB4)
    F['sb'] = a_f32(TB4)
    F['tmp'] = a_f32(512)
    F['junk'] = F['hgT'].rearrange("p k t -> p (k t)")[:, 0:D]
    F['ssq'] = [a_f32(1) for _ in range(4)]
    F['rstd'] = [a_f32(1) for _ in range(4)]
    F['bm'] = a_f32(32)
    DMA('sp', lambda e: e.dma_start(out=F['gate'], in_=gate_dram.partition_broadcast(128)), ['gate_dram'], ['gate_bc'], 'p4a')
    DMA('sp', lambda e: e.dma_start(out=F['fg'], in_=fg_d.partition_broadcast(128)), [], ['fg'], 'p4b')
    DMA('sp', lambda e: e.dma_start(out=F['bm'], in_=bm_d), [], ['bmf4'], 'p4c')

    fwctr = [0]
    fpctr = [0]
    owctr = [0]
    for tb in range(NB4):
        t0 = tb * TB4
        DMA('sp', lambda e, t0=t0: e.dma_start(out=F['hgT'], in_=hg_spill[:, t0:t0 + TB4].rearrange("(k p) t -> p k t", p=128)),
            ['hg_spill'], ['hgT'], 'p4h')
        DMA('act', lambda e, t0=t0: e.dma_start(out=F['atT'], in_=att_spill[:, t0:t0 + TB4].rearrange("(k p) t -> p k t", p=128)),
            ['att_spill'], ['atT'], 'p4t')
        xs = xnT[:, :, t0:t0 + TB4]
        for c in range(16):
            bnks = []
            for u, src, sres in ((0, F['hgT'], 'hgT'), (1, xs, 'xnT'), (2, F['atT'], 'atT'), (3, xs, 'xnT')):
                slot = fwctr[0] % 4
                fwctr[0] += 1
                DMA('pool', lambda e, slot=slot, c=c, u=u: e.dma_start(out=F['wh'][slot].rearrange("p k c -> p (k c)"),
                                                                       in_=w_fm[c, u]), [], [('wh', slot)], 'wh%d' % slot)
                bnk = fpctr[0] % 4
                fpctr[0] += 1
                for kc in range(KC):
                    PE(lambda e, kc=kc, slot=slot, bnk=bnk, src=src: e.matmul(psb[bnk][:, 0:TB4], lhsT=F['wh'][slot][:, kc, :],
                                                                              rhs=src[:, kc, :], start=(kc == 0), stop=(kc == KC - 1)),
                       [('wh', slot), sres], [('ps', bnk)])
                bnks.append(bnk)
            ACT(lambda e, b=bnks[1], c=c: e.activation(out=F['sa'], in_=psb[b][:, 0:TB4], func=AF.Sigmoid, bias=F['bm'][:, c:c + 1]),
                [('ps', bnks[1]), 'bmf4'], ['sa'])
            ACT(lambda e, b=bnks[3], c=c: e.activation(out=F['sb'], in_=psb[b][:, 0:TB4], func=AF.Sigmoid, bias=F['bm'][:, 16 + c:17 + c]),
                [('ps', bnks[3]), 'bmf4'], ['sb'])
            DVE(lambda e, b=bnks[0]: e.tensor_tensor(out=F['sa'], in0=psb[b][:, 0:TB4], in1=F['sa'], op=ALU.mult),
                [('ps', bnks[0]), 'sa'], ['sa'])
            DVE(lambda e, b=bnks[2]: e.tensor_tensor(out=F['sb'], in0=psb[b][:, 0:TB4], in1=F['sb'], op=ALU.mult),
                [('ps', bnks[2]), 'sb'], ['sb'])
            DVE(lambda e, c=c: e.tensor_tensor(out=F['yT'][:, c, :], in0=F['sa'], in1=F['sb'], op=ALU.add), ['sa', 'sb'], ['yT'])
        for tt in range(4):
            DMA('sp' if tt % 2 == 0 else 'act',
                lambda e, tt=tt, t0=t0: e.dma_start(out=F['hrow'][tt], in_=x_loc[t0 + tt * 128:t0 + (tt + 1) * 128, :]),
                [], [('hrow', tt)], 'p4x%d' % tt)
        for cb in range(8):
            os_ = owctr[0] % 2
            owctr[0] += 1
            DMA('pool', lambda e, os_=os_, cb=cb: e.dma_start(out=F['wo'][os_].rearrange("p k c -> p (k c)"), in_=w_outd[cb]),
                [], [('wo', os_)], 'wo%d' % os_)
            for tt in range(4):
                bnk = 4 + (cb * 4 + tt) % 4
                for kc in range(KC):
                    PE(lambda e, kc=kc, tt=tt, os_=os_, bnk=bnk: e.matmul(psb[bnk][:, 0:256], lhsT=F['yT'][:, kc, tt * 128:(tt + 1) * 128],
                                                                          rhs=F['wo'][os_][:, kc, :], start=(kc == 0), stop=(kc == KC - 1)),
                       ['yT', ('wo', os_)], [('ps', bnk)])
                DVE(lambda e, bnk=bnk, cb=cb: e.tensor_tensor(out=F['tmp'][:, 0:256], in0=psb[bnk][:, 0:256], in1=F['gate'][:, cb * 256:(cb + 1) * 256],
                                                              op=ALU.mult), [('ps', bnk), 'gate_bc'], ['tmp4'])
                DVE(lambda e, tt=tt, cb=cb: e.tensor_tensor(out=F['hrow'][tt][:, cb * 256:(cb + 1) * 256],
                                                            in0=F['hrow'][tt][:, cb * 256:(cb + 1) * 256], in1=F['tmp'][:, 0:256], op=ALU.add),
                    ['tmp4', ('hrow', tt)], [('hrow', tt)])
        for tt in range(4):
            hr = F['hrow'][tt]
            DVE(lambda e, tt=tt: e.memset(F['ssq'][tt], 0.0), [], [('ssq4', tt)])
            ACT(lambda e, tt=tt, hr=hr: e.activation(out=F['junk'], in_=hr, func=AF.Square, accum_out=F['ssq'][tt]),
                [('hrow', tt), ('ssq4', tt)], [('ssq4', tt), 'hgT'])
            DVE(lambda e, tt=tt: e.tensor_scalar(out=F['rstd'][tt], in0=F['ssq'][tt], scalar1=1.0 / D, scalar2=EPS,
                                                 op0=ALU.mult, op1=ALU.add), [('ssq4', tt)], [('rstd4', tt)])
            ACT(lambda e, tt=tt: e.sqrt(out=F['rstd'][tt], in_=F['rstd'][tt]), [('rstd4', tt)], [('rstd4', tt)])
            DVE(lambda e, tt=tt: e.reciprocal(out=F['rstd'][tt], in_=F['rstd'][tt]), [('rstd4', tt)], [('rstd4', tt)])
            DVE(lambda e, tt=tt, hr=hr: e.scalar_tensor_tensor(out=hr, in0=hr, scalar=F['rstd'][tt][:, 0:1], in1=F['fg'],
                                                               op0=ALU.mult, op1=ALU.mult),
                [('hrow', tt), ('rstd4', tt), 'fg'], [('hrow', tt)])
            DMA('sp', lambda e, tt=tt, hr=hr, t0=t0: e.dma_start(out=y_out[t0 + tt * 128:t0 + (tt + 1) * 128, :], in_=hr),
                [('hrow', tt)], [('y_out', tt)], 'p4y%d' % tt)
    P.add('sp', lambda e: None, [('y_out', tt) for tt in range(4)], [])

    P.emit(es)
    es.close()
    return nc


def prep_inputs(inp):
    f = np.float32
    x = np.asarray(inp['x'], f)
    ctx = np.asarray(inp['ctx'], f)
    c = np.asarray(inp['c'], f)
    c_ctx = np.asarray(inp['c_ctx'], f)
    w_ada = np.asarray(inp['w_ada'], f)[0]
    b_ada = np.asarray(inp['b_ada'], f)[0]
    w_ada_l = np.ascontiguousarray(w_ada.reshape(KC, 128, 3 * D).transpose(1, 0, 2))
    b_ada2 = np.ascontiguousarray(np.stack([b_ada, b_ada], 0))
    gain_fm = np.ascontiguousarray(np.asarray(inp['norm_gain'], f)[0].reshape(KC, 128).T)
    ident = np.eye(128, dtype=f)
    sel = np.zeros((2, 128), f)
    sel[0, :] = 1.0
    w_in = np.asarray(inp['w_in'], f)[0]
    OFF_HG_Q, OFF_HG_FF, OFF_HG_FB, OFF_HG_I, OFF_HG_G = 5120, 7168, 9216, 11264, 13312

    def unit(col0, ncol=128):
        return w_in[:, col0:col0 + ncol].reshape(KC, 128, ncol).transpose(1, 0, 2).reshape(128, KC * ncol)

    def fm(v):
        return np.ascontiguousarray(v.reshape(16, 128).T)

    w_h_s = []
    lbl_s = []
    lf = np.asarray(inp['lb_logits_fwd'], f)
    lbk = np.asarray(inp['lb_logits_bwd'], f)
    for s in range(2):
        offA, offB = (OFF_HG_FF, OFF_HG_FB) if s == 0 else (OFF_HG_FB, OFF_HG_FF)
        wh = np.empty((16, 5, 128, KC * 128), f)
        for h in range(16):
            for u, off in enumerate((OFF_HG_Q, offA, offB, OFF_HG_I, OFF_HG_G)):
                wh[h, u] = unit(off + h * 128)
        w_h_s.append(wh)
        la, lb_ = (lf, lbk) if s == 0 else (lbk, lf)
        arr = np.stack([np.stack([fm(la[0]), fm(la[1])], 1), np.stack([fm(lb_[0]), fm(lb_[1])], 1)], 1)
        lbl_s.append(np.ascontiguousarray(arr.reshape(128, 64)))
    hgain_fm = fm(np.asarray(inp['hgrn_norm_gain'], f)[0])
    mreset = np.ones((128, 512), f)
    mreset[:, ::64] = 0.0
    ss, tt = np.meshgrid(np.arange(64), np.arange(64), indexing='ij')
    maskab = np.concatenate([(ss <= tt).astype(f), (ss >= tt).astype(f)], 1)
    OFF_ATT_Q, OFF_ATT_K, OFF_ATT_V, OFF_ATT_G, OFF_MERGE = 0, 2048, 2560, 3072, 15360
    w_att = np.empty((4, 9, 128, KC * 128), f)
    for g in range(4):
        for u in range(4):
            w_att[g, u] = unit(OFF_ATT_Q + (4 * g + u) * 128)
            w_att[g, 5 + u] = unit(OFF_ATT_G + (4 * g + u) * 128)
        w_att[g, 4] = unit(OFF_ATT_K + g * 128)
    w_v = np.ascontiguousarray(unit(OFF_ATT_V, 512))
    w_o_hgrn = np.asarray(inp['w_o_hgrn'], f)[0]
    w_o_attn = np.asarray(inp['w_o_attn'], f)[0]
    w_out = np.asarray(inp['w_out'], f)[0]

    def unit_of(w, col0, ncol=128):
        return w[:, col0:col0 + ncol].reshape(KC, 128, ncol).transpose(1, 0, 2).reshape(128, KC * ncol)

    w_fm = np.empty((16, 4, 128, KC * 128), f)
    for cc in range(16):
        w_fm[cc, 0] = unit_of(w_o_hgrn, cc * 128)
        w_fm[cc, 1] = unit(OFF_MERGE + cc * 128)
        w_fm[cc, 2] = unit_of(w_o_attn, cc * 128)
        w_fm[cc, 3] = unit(OFF_MERGE + D + cc * 128)
    w_out_l = np.stack([unit_of(w_out, cb * 256, 256) for cb in range(8)], 0)
    bmg = np.asarray(inp['b_merge'], f)[0]
    bmerge_fm = np.ascontiguousarray(np.concatenate([fm(bmg[0]), fm(bmg[1])], 1))
    fgain = np.ascontiguousarray(np.asarray(inp['final_norm_gain'], f).reshape(1, D))
    sink = np.ascontiguousarray(np.asarray(inp['sink_logits'], f).reshape(1, 16))
    rotT = np.zeros((128, 128), f)
    for dq in range(128):
        if (dq // 32) % 2 == 0:
            rotT[dq + 32, dq] = -1.0
        else:
            rotT[dq - 32, dq] = 1.0
    aa, bb = np.meshgrid(np.arange(128), np.arange(128), indexing='ij')
    bm0 = np.tile(np.where(bb <= aa, 0.0, -30000.0).astype(f), (1, 4))
    bm1 = np.tile(np.where(aa <= bb, 0.0, -30000.0).astype(f), (1, 4))
    bmask = np.ascontiguousarray(np.concatenate([bm0, bm1], 1))
    inv_freq = (1.0 / (np.float32(10000.0) ** (np.arange(0, 64, 2, dtype=f) / np.float32(64)))).astype(f)
    rope_tabs = []
    for s in range(2):
        ii = np.arange(NOWN + 128)
        jj = ii if s == 0 else 4095 - ii
        row = (jj // 64).astype(f)
        colp = (jj % 64).astype(f)
        ang_r = row[:, None] * inv_freq[None, :]
        ang_c = colp[:, None] * inv_freq[None, :]
        ang = np.concatenate([ang_r, ang_r, ang_c, ang_c], -1).astype(f)
        rope_tabs.append((np.ascontiguousarray(np.cos(ang).T.astype(f)), np.ascontiguousarray(np.sin(ang).T.astype(f))))
    maps = []
    for core in range(8):
        b, s = core // 2, core % 2
        xb = x[b]
        cb = ctx[b]
        if s == 1:
            xb = xb[::-1]
            cb = cb[::-1]
        cv = np.stack([c[b].reshape(KC, 128).T, c_ctx.reshape(KC, 128).T], -1).reshape(128, KC * 2)
        maps.append(dict(
            x_loc=np.ascontiguousarray(xb), ctx_loc=np.ascontiguousarray(cb),
            cvec=np.ascontiguousarray(cv), w_ada_l=w_ada_l, b_ada2=b_ada2, gain_fm=gain_fm,
            ident=ident, sel=sel, w_h=w_h_s[s], lbl=lbl_s[s], hgain_fm=hgain_fm, mreset=mreset, maskab=maskab,
            w_att=w_att, w_v=w_v, cosT=rope_tabs[s][0], sinT=rope_tabs[s][1], rotT=rotT, sink=sink, bmask=bmask,
            w_fm=w_fm, w_out_l=w_out_l, bmerge_fm=bmerge_fm, fgain=fgain))
    return maps


def kernel(**inputs):
    maps = prep_inputs(inputs)
    nc = build()
    res = run_bass_kernel_spmd(nc, maps, core_ids=list(range(8)))
    out = np.zeros((4, 4096, D), np.float32)
    for core in range(8):
        b, s = core // 2, core % 2
        y = res.results[core]["y"]
        if s == 0:
            out[b, :NOWN] = y
        else:
            out[b, NOWN:] = y[::-1]
    return out
```

```python
import numpy as np
from contextlib import ExitStack
import concourse.bass as bass
import concourse.mybir as mybir
from concourse.bass_utils import run_bass_kernel_spmd

F32 = mybir.dt.float32
BF16 = mybir.dt.bfloat16
AF = mybir.ActivationFunctionType
ALU = mybir.AluOpType

D = 2048
KC = 16
NOWN = 2048
NOTH = 2048
LCTX = 256
EPS = 1e-6
DBG_HEADS = 2


class Prog:
    def __init__(self, nc):
        self.nc = nc
        self.ops = []

    def add(self, eng, fn, r=(), w=(), dma=None):
        self.ops.append(dict(eng=eng, fn=fn, r=tuple(r), w=tuple(w), dma=dma,
                             deps=[], signal=False, sig=None))

    def barrier(self):
        self.add('sp', lambda e: None, [], ['__bar'])
        self.ops[-1]['barrier'] = True
        for eng in ('pe', 'act', 'dve', 'pool'):
            self.add(eng, lambda e: None, ['__bar'], [])

    def capture(self):
        self._saved = self.ops
        self.ops = []

    def release(self):
        got = self.ops
        self.ops = self._saved
        return got

    def interleave(self, a, b):
        n = max(len(a), len(b))
        for i in range(n):
            if i < len(a):
                self.ops.append(a[i])
            if i < len(b):
                self.ops.append(b[i])

    def analyze(self):
        last_w = {}
        readers = {}
        ops = self.ops
        last_by_q = {}
        for i, op in enumerate(ops):
            deps = set()
            if op.get('barrier'):
                deps.update(last_by_q.values())
            last_by_q[('dma', op['dma']) if op['dma'] is not None else ('eng', op['eng'])] = i
            for res in op['r']:
                if res in last_w:
                    deps.add(last_w[res])
            for res in op['w']:
                if res in last_w:
                    deps.add(last_w[res])
                deps.update(readers.get(res, ()))
            final = []
            for d in deps:
                if d == i:
                    continue
                dop = ops[d]
                if dop['eng'] == op['eng'] and dop['dma'] is None and op['dma'] is None:
                    if op['eng'] == 'pe':
                        continue
                    if not (set(dop['w']) & set(op['r'])):
                        continue
                final.append(d)
            best = {}
            keep = []
            for d in final:
                if ops[d]['dma'] is None:
                    k = ops[d]['eng']
                    if k not in best or d > best[k]:
                        best[k] = d
                else:
                    keep.append(d)
            final = keep + list(best.values())
            op['deps'] = final
            for d in final:
                ops[d]['signal'] = True
            for res in op['w']:
                last_w[res] = i
                readers[res] = []
            for res in op['r']:
                if res not in op['w']:
                    readers.setdefault(res, []).append(i)
        cnt = {}
        for op in ops:
            if op['dma'] is not None:
                op['signal'] = True
            if op['signal']:
                key = ('dma', op['dma']) if op['dma'] is not None else ('eng', op['eng'])
                inc = 16 if op['dma'] is not None else 1
                cnt[key] = cnt.get(key, 0) + inc
                op['sig'] = (key, cnt[key])
        self.keys = list(cnt.keys())

    def emit(self, es):
        nc = self.nc
        self.analyze()
        sems = {}
        for n, k in enumerate(self.keys):
            sems[k] = es.enter_context(nc.semaphore("s%d" % n))
        block = es.enter_context(nc.Block())
        ops = self.ops

        def run(engname):
            def body(e):
                waited = {}
                for op in ops:
                    if op['eng'] != engname:
                        continue
                    need = {}
                    for d in op['deps']:
                        k, v = ops[d]['sig']
                        if v > need.get(k, 0):
                            need[k] = v
                    for k, v in need.items():
                        if v > waited.get(k, 0):
                            e.wait_ge(sems[k], v)
                            waited[k] = v
                    ins = op['fn'](e)
                    if op['signal']:
                        k, v = op['sig']
                        if ins is None:
                            ins = e.nop()
                        ins.then_inc(sems[k], 16 if op['dma'] is not None else 1)
            return body

        block.tensor(run('pe'))
        block.scalar(run('act'))
        block.vector(run('dve'))
        block.gpsimd(run('pool'))
        block.sync(run('sp'))


def build(stage=99):
    nc = bass.Bass("TRN2", target_bir_lowering=False)
    es = ExitStack()
    es.enter_context(nc.allow_low_precision("bf16 matmul operands, fp32 accumulation"))
    P = Prog(nc)

    def din(name, shape, dt=F32):
        return nc.dram_tensor(name, list(shape), dt, kind="ExternalInput").ap()

    def dout(name, shape, dt=F32):
        return nc.dram_tensor(name, list(shape), dt, kind="ExternalOutput").ap()

    def sb(name, shape, dt=F32):
        return es.enter_context(nc.sbuf_tensor(name, list(shape), dt))

    PE = lambda fn, r, w: P.add('pe', fn, r, w)
    ACT = lambda fn, r, w: P.add('act', fn, r, w)
    DVE = lambda fn, r, w: P.add('dve', fn, r, w)
    POOL = lambda fn, r, w: P.add('pool', fn, r, w)
    DMA = lambda q, fn, r, w, key: P.add(q, fn, r, w, dma=key)

    x_loc = din("x_loc", [NOWN + NOTH, D])
    ctx_loc = din("ctx_loc", [LCTX, D])
    cvec = din("cvec", [128, KC * 2])
    w_ada_l = din("w_ada_l", [128, KC, 3 * D])
    b_ada2 = din("b_ada2", [2, 3 * D])
    gain_fm = din("gain_fm", [128, KC])
    ident_d = din("ident", [128, 128])
    sel_d = din("sel", [2, 128])

    psb = [es.enter_context(nc.psum_tensor("ps%d" % i, [128, 512], F32)) for i in range(8)]

    ident = sb("ident_sb", [128, 128])
    sel = sb("sel_sb", [2, 128])
    DMA('sp', lambda e: e.dma_start(out=ident[:], in_=ident_d), [], ['ident'], 'c0')
    DMA('sp', lambda e: e.dma_start(out=sel[:], in_=sel_d), [], ['sel'], 'c1')

    w_h = din("w_h", [16, 5, 128, KC * 128])
    lbl_d = din("lbl", [128, 64])
    hgain_d = din("hgain_fm", [128, 16])
    mreset_d = din("mreset", [128, 512])
    maskab_d = din("maskab", [64, 128])
    gate_dram = nc.dram_tensor("gate_scr", [1, D], F32).ap()
    hg_spill = nc.dram_tensor("hg_spill", [D, NOWN], BF16).ap()

    ARENA_W = 29 * 1024
    arena = sb("arena", [128, ARENA_W])
    apos = [0]

    def a_reset():
        apos[0] = 0

    def a_f32(n):
        o = apos[0]
        apos[0] += n
        assert apos[0] <= ARENA_W, apos[0]
        return arena[:, o:o + n]

    def a_bf16(n):
        w = (n + 1) // 2
        return a_f32(w).bitcast(BF16)

    identb = sb("identb", [128, 128], BF16)
    onesb = sb("onesb", [128, 128], BF16)
    mreset = sb("mreset_sb", [128, 512])
    maskab = sb("maskab_sb", [64, 128])
    lbl = sb("lbl_sb", [128, 64])
    lb = sb("lb_sb", [128, 32])
    negoml = sb("negoml", [128, 32])
    oml = sb("oml", [128, 32])
    hgain = sb("hgain_sb", [128, 16])
    gfm = sb("gfm", [128, KC])
    modcol = sb("modcol", [128, 64])
    Avec = sb("Avec", [128, 2, KC])
    Bvec = sb("Bvec", [128, 2, KC])
    xnT = sb("xnT_main", [128, KC, NOWN], BF16)
    xnT_ctx = sb("xnT_ctx", [128, KC, LCTX], BF16)
    xnT_halo = sb("xnT_halo", [128, KC, 128], BF16)
    SinitB = sb("SinitB", [128, 16, 128])
    c_sb = sb("c_sb", [128, KC * 2])
    sc_sb = sb("sc_sb", [128, KC * 2])

    DMA('sp', lambda e: e.dma_start(out=mreset[:], in_=mreset_d), [], ['mreset'], 'c5')
    DMA('sp', lambda e: e.dma_start(out=maskab[:], in_=maskab_d), [], ['maskab'], 'c6')
    DMA('sp', lambda e: e.dma_start(out=lbl[:], in_=lbl_d), [], ['lbl'], 'c7')
    DMA('sp', lambda e: e.dma_start(out=hgain[:], in_=hgain_d), [], ['hgain'], 'c8')
    DMA('sp', lambda e: e.dma_start(out=c_sb[:], in_=cvec), [], ['c_sb'], 'c2')
    DMA('sp', lambda e: e.dma_start(out=gfm[:], in_=gain_fm), [], ['gfm'], 'c4')
    DVE(lambda e: e.tensor_copy(out=identb[:], in_=ident[:]), ['ident'], ['identb'])
    DVE(lambda e: e.memset(onesb[:], 1.0), [], ['onesb'])
    lbl4 = lbl[:].rearrange("p (d l h) -> p d l h", d=2, l=2)
    lb3 = lb[:].rearrange("p (d h) -> p d h", d=2)
    DVE(lambda e: e.tensor_tensor(out=lb3, in0=lbl4[:, :, 0, :], in1=lbl4[:, :, 1, :], op=ALU.subtract), ['lbl'], ['lb'])
    ACT(lambda e: e.activation(out=lb[:], in_=lb[:], func=AF.Sigmoid), ['lb'], ['lb'])
    DVE(lambda e: e.tensor_scalar(out=negoml[:], in0=lb[:], scalar1=-1.0, scalar2=None, op0=ALU.add), ['lb'], ['negoml'])
    DVE(lambda e: e.tensor_scalar(out=oml[:], in0=lb[:], scalar1=-1.0, scalar2=1.0, op0=ALU.mult, op1=ALU.add), ['lb'], ['oml'])

    ACT(lambda e: e.activation(out=sc_sb[:], in_=c_sb[:], func=AF.Silu), ['c_sb'], ['sc_sb'])
    a_reset()
    wada = [a_f32(KC * 512).rearrange("p (k c) -> p k c", k=KC) for _ in range(3)]
    badab = [a_f32(512) for _ in range(2)]
    mod2b = [a_f32(512) for _ in range(2)]
    for blk in range(12):
        s = blk % 2
        ws = blk % 3
        DMA('sp' if blk % 2 == 0 else 'act',
            lambda e, ws=ws, blk=blk: e.dma_start(out=wada[ws], in_=w_ada_l[:, :, blk * 512:(blk + 1) * 512]),
            [], [('wada', ws)], 'wada%d' % ws)
        DMA('sp', lambda e, s=s, blk=blk: e.dma_start(out=badab[s][0:2, :], in_=b_ada2[:, blk * 512:(blk + 1) * 512]),
            [], [('badab', s)], 'badab%d' % s)
        for kc in range(KC):
            PE(lambda e, s=s, ws=ws, kc=kc: e.matmul(psb[s][0:2, :], lhsT=sc_sb[:, 2 * kc:2 * kc + 2],
                                              rhs=wada[ws][:, kc, :], start=(kc == 0), stop=(kc == KC - 1)),
               ['sc_sb', ('wada', ws)], [('ps', s)])
        DVE(lambda e, s=s: e.tensor_tensor(out=mod2b[s][0:2, :], in0=psb[s][0:2, :], in1=badab[s][0:2, :], op=ALU.add),
            [('ps', s), ('badab', s)], [('mod2b', s)])
        if blk < 8:
            for jj in range(4):
                j = blk * 4 + jj
                PE(lambda e, s=s, j=j, jj=jj: e.matmul(psb[2][:, 2 * j:2 * j + 2], lhsT=mod2b[s][0:2, jj * 128:(jj + 1) * 128],
                                                       rhs=ident[0:2, 0:2], start=True, stop=True),
                   [('mod2b', s), 'ident'], [('ps', 2)])
        else:
            DMA('sp', lambda e, s=s, blk=blk: e.dma_start(out=gate_dram[0:1, (blk - 8) * 512:(blk - 7) * 512], in_=mod2b[s][0:1, :]),
                [('mod2b', s)], ['gate_dram'], 'gd%d' % s)
    DVE(lambda e: e.tensor_copy(out=modcol[:], in_=psb[2][:, 0:64]), [('ps', 2)], ['modcol'])
    mc3 = modcol[:].rearrange("p (j t) -> p j t", t=2)
    for t in range(2):
        DVE(lambda e, t=t: e.scalar_tensor_tensor(out=Avec[:, t, :], in0=mc3[:, 16:32, t], scalar=1.0, in1=gfm[:],
                                                  op0=ALU.add, op1=ALU.mult), ['modcol', 'gfm'], ['Avec'])
        DVE(lambda e, t=t: e.tensor_copy(out=Bvec[:, t, :], in_=mc3[:, 0:16, t]), ['modcol'], ['Bvec'])
    P.barrier()

    def alloc_xt():
        a_reset()
        d = dict(xt=[a_f32(D) for _ in range(3)], junk=a_bf16(D),
                 ssq=[a_f32(1) for _ in range(3)], rstd=[a_f32(1) for _ in range(3)])
        return d

    def xn_stats(X, n, src_rows):
        s = n % 3
        xt, junk, ssq, rstd = X['xt'], X['junk'], X['ssq'], X['rstd']
        DMA('sp', lambda e: e.dma_start(out=xt[s], in_=src_rows), [], [('xt', s)], 'xt%d' % s)
        DVE(lambda e: e.memset(ssq[s], 0.0), [], [('ssq', s)])
        ACT(lambda e: e.activation(out=junk, in_=xt[s], func=AF.Square, accum_out=ssq[s]),
            [('xt', s), ('ssq', s)], [('ssq', s), 'junk'])
        DVE(lambda e: e.tensor_scalar(out=rstd[s], in0=ssq[s], scalar1=1.0 / D, scalar2=EPS,
                                      op0=ALU.mult, op1=ALU.add), [('ssq', s)], [('rstd', s)])
        ACT(lambda e: e.sqrt(out=rstd[s], in_=rstd[s]), [('rstd', s)], [('rstd', s)])
        DVE(lambda e: e.reciprocal(out=rstd[s], in_=rstd[s]), [('rstd', s)], [('rstd', s)])
        DVE(lambda e: e.tensor_scalar(out=xt[s], in0=xt[s], scalar1=rstd[s][:, 0:1], scalar2=None, op0=ALU.mult),
            [('xt', s), ('rstd', s)], [('xt', s)])

    def xn_transpose(X, n, dst, dst_res, which):
        s = n % 3
        pb = n % 2
        xt = X['xt']
        for j in range(KC):
            bnk = j // 4 + 4 * pb
            PE(lambda e, j=j, bnk=bnk: e.transpose(out=psb[bnk][:, (j % 4) * 128:(j % 4 + 1) * 128],
                                                   in_=xt[s][:, j * 128:(j + 1) * 128], identity=ident[:]),
               [('xt', s), 'ident'], [('ps', bnk)])
        for j in range(KC):
            bnk = j // 4 + 4 * pb
            ACT(lambda e, j=j, bnk=bnk: e.activation(out=dst[:, j, :], in_=psb[bnk][:, (j % 4) * 128:(j % 4 + 1) * 128],
                                                     func=AF.Identity, scale=Avec[:, which, j:j + 1],
                                                     bias=Bvec[:, which, j:j + 1]),
                [('ps', bnk), 'Avec', 'Bvec'], [dst_res])

    def build_xnT_seq(X, tiles):
        for n, tl in enumerate(tiles):
            if n == 0:
                xn_stats(X, 0, tl[0])
            if n + 1 < len(tiles):
                xn_stats(X, n + 1, tiles[n + 1][0])
            xn_transpose(X, n, tl[1], tl[2], tl[3])

    X = alloc_xt()
    tiles = [(ctx_loc[t * 128:(t + 1) * 128, :], xnT_ctx[:, :, t * 128:(t + 1) * 128], 'xnT_ctx', 1) for t in range(2)]
    tiles += [(x_loc[NOWN + t * 128:NOWN + (t + 1) * 128, :], xnT[:, :, t * 128:(t + 1) * 128], 'xnT', 0)
              for t in range(NOTH // 128)]
    build_xnT_seq(X, tiles)
    POOL(lambda e: e.tensor_copy(out=xnT_halo[:], in_=xnT[:, :, 0:128]), ['xnT'], ['xnT_halo'])
    P.barrier()
    NWS = 8
    wctr = [0]
    onecol = sb("onecol", [128, 1])
    DVE(lambda e: e.memset(onecol[:], 1.0), [], ['onecol'])

    def alloc_hgrn():
        a_reset()
        H = {}
        H['wh'] = [a_bf16(KC * 128).rearrange("p (k c) -> p k c", k=KC) for _ in range(NWS)]
        H['T'] = [{nm: a_f32(512) for nm in ('ks', 'f', 'G', 'E', 'X', 'Eq', 'Ek')} for _ in range(2)]
        H['qf'] = a_f32(512)
        H['rs'] = H['T'][1]['ks']
        H['tt'] = H['T'][1]['f']
        H['qp'] = [a_bf16(NOWN) for _ in range(2)]
        H['kp'] = [a_bf16(NOWN) for _ in range(2)]
        H['gate'] = a_bf16(NOWN)
        H['vT'] = a_bf16(512)
        H['vtm'] = a_bf16(32 * 128).rearrange("p (c v) -> p c v", v=128)
        H['ktm'] = a_bf16(4 * 128).rearrange("p (c v) -> p c v", v=128)
        H['Sp'] = a_bf16(4 * 128).rearrange("p (c v) -> p c v", v=128)
        H['Am'] = a_bf16(256)
        H['tmpU'] = a_f32(4 * 128).rearrange("p (c v) -> p c v", v=128)
        H['oacc'] = a_f32(NOWN)
        H['SA'] = a_f32(128)
        H['abc'] = a_f32(3 * 2 * 32).rearrange("p (k d c) -> p k d c", k=3, d=2)
        H['sq'] = a_bf16(512)
        H['ktm2'] = [H['ktm'], a_bf16(4 * 128).rearrange("p (c v) -> p c v", v=128)]
        H['Sp2'] = [H['Sp'], a_bf16(4 * 128).rearrange("p (c v) -> p c v", v=128)]
        H['Am2'] = [H['Am'], a_bf16(256)]
        H['tmpU2'] = [H['tmpU'], a_f32(4 * 128).rearrange("p (c v) -> p c v", v=128)]
        return H

    def load_unit(H, h, u):
        slot = wctr[0] % NWS
        wctr[0] += 1
        DMA('pool', lambda e: e.dma_start(out=H['wh'][slot].rearrange("p k c -> p (k c)"), in_=w_h[h, u]),
            [], [('wh', slot)], 'wh%d' % slot)
        return slot

    pctr = [0]

    def proj(H, slot, xsrc, n, bank=None):
        if bank is None:
            bnk = pctr[0] % 4
            pctr[0] += 1
        else:
            bnk = bank
        for kc in range(KC):
            PE(lambda e, kc=kc: e.matmul(psb[bnk][:, 0:n], lhsT=H['wh'][slot][:, kc, :], rhs=xsrc[:, kc, :],
                                         start=(kc == 0), stop=(kc == KC - 1)),
               [('wh', slot), 'xnT', 'xnT_ctx'], [('ps', bnk)])
        return bnk

    def c3(ap, n):
        return ap[:, 0:n].rearrange("p (c t) -> p c t", t=64)

    def hgrn_elem(H, h, d, bnk, col0, n):
        T = H['T'][d]
        nch = n // 64
        cb = col0 // 64
        ks, f, G, E, X_, Eq, Ek = (T[k] for k in ('ks', 'f', 'G', 'E', 'X', 'Eq', 'Ek'))
        qf = H['qf']
        R = lambda nm: (nm, d)
        hd = d * 16 + h
        ACT(lambda e: e.activation(out=ks[:, 0:n], in_=psb[bnk][:, 0:n], func=AF.Sigmoid, scale=-1.0),
            [('ps', bnk)], [R('ks')])
        DVE(lambda e: e.tensor_scalar(out=f[:, 0:n], in0=ks[:, 0:n], scalar1=negoml[:, hd:hd + 1], scalar2=1.0,
                                      op0=ALU.mult, op1=ALU.add), [R('ks'), 'negoml'], [R('f')])
        ACT(lambda e: e.activation(out=f[:, 0:n], in_=f[:, 0:n], func=AF.Ln), [R('f')], [R('f')])
        DVE(lambda e: e.tensor_tensor_scan(out=G[:, 0:n], data0=mreset[:, 0:n], data1=f[:, 0:n], initial=0.0,
                                           op0=ALU.mult, op1=ALU.add), [R('f'), 'mreset'], [R('G')])
        G3, E3, X3, Eq3 = c3(G, n), c3(E, n), c3(X_, n), c3(Eq, n)
        if d == 0:
            DVE(lambda e: e.tensor_tensor(out=X3, in0=G3, in1=G3[:, :, 31:32].to_broadcast([128, nch, 64]),
                                          op=ALU.subtract), [R('G')], [R('X')])
        else:
            DVE(lambda e: e.tensor_tensor(out=E[:, 0:n], in0=G[:, 0:n], in1=f[:, 0:n], op=ALU.subtract),
                [R('G'), R('f')], [R('E')])
            DVE(lambda e: e.tensor_tensor(out=X3, in0=E3[:, :, 32:33].to_broadcast([128, nch, 64]), in1=E3,
                                          op=ALU.subtract), [R('E')], [R('X')])
        ACT(lambda e: e.activation(out=Eq[:, 0:n], in_=X_[:, 0:n], func=AF.Exp), [R('X')], [R('Eq')])
        ACT(lambda e: e.activation(out=Ek[:, 0:n], in_=X_[:, 0:n], func=AF.Exp, scale=-1.0), [R('X')], [R('Ek')])
        DVE(lambda e: e.scalar_tensor_tensor(out=H['kp'][d][:, col0:col0 + n], in0=ks[:, 0:n], scalar=oml[:, hd:hd + 1],
                                             in1=Ek[:, 0:n], op0=ALU.mult, op1=ALU.mult),
            [R('ks'), R('Ek'), 'oml'], [('kp', d)])
        DVE(lambda e: e.tensor_tensor(out=H['qp'][d][:, col0:col0 + n], in0=qf[:, 0:n], in1=Eq[:, 0:n], op=ALU.mult),
            ['qf', R('Eq')], [('qp', d)])
        abc = H['abc']
        ACT(lambda e: e.activation(out=abc[:, 0, d, cb:cb + nch], in_=G3[:, :, 63], func=AF.Exp), [R('G')], [('abc', d)])
        if d == 0:
            DVE(lambda e: e.tensor_copy(out=abc[:, 1, d, cb:cb + nch], in_=Eq3[:, :, 63]), [R('Eq')], [('abc', d)])
            ACT(lambda e: e.activation(out=abc[:, 2, d, cb:cb + nch], in_=G3[:, :, 31], func=AF.Exp), [R('G')], [('abc', d)])
        else:
            DVE(lambda e: e.tensor_copy(out=abc[:, 1, d, cb:cb + nch], in_=Eq3[:, :, 0]), [R('Eq')], [('abc', d)])
            DVE(lambda e: e.tensor_tensor(out=abc[:, 2, d, cb:cb + nch], in0=G3[:, :, 63], in1=E3[:, :, 32],
                                          op=ALU.subtract), [R('G'), R('E')], [('abc', d)])
            ACT(lambda e: e.activation(out=abc[:, 2, d, cb:cb + nch], in_=abc[:, 2, d, cb:cb + nch], func=AF.Exp),
                [('abc', d)], [('abc', d)])

    def build_vtm(H, bnk, col0, n):
        nch = n // 64
        cb = col0 // 64
        if bnk is not None:
            ACT(lambda e: e.activation(out=H['vT'][:, 0:n], in_=psb[bnk][:, 0:n], func=AF.Copy), [('ps', bnk)], ['vT'])
        for g0 in range(0, nch, 4):
            bk = 4 + (g0 // 4) % 2
            for j in range(4):
                PE(lambda e, j=j, g0=g0, bk=bk: e.matmul(psb[bk][0:64, j * 128:(j + 1) * 128],
                                                         lhsT=H['vT'][:, (g0 + j) * 64:(g0 + j + 1) * 64], rhs=identb[:],
                                                         start=True, stop=True), ['vT', 'identb'], [('ps', bk)])
        for g0 in range(0, nch, 4):
            bk = 4 + (g0 // 4) % 2
            ACT(lambda e, g0=g0, bk=bk: e.activation(out=H['vtm'][0:64, cb + g0:cb + g0 + 4, :],
                                                     in_=psb[bk][0:64, :].rearrange("p (c v) -> p c v", v=128), func=AF.Copy),
                [('ps', bk)], ['vtm'])

    def state_block(H, h, d, bz, bv, n, S, sres, par=0):
        T = H['T'][par]
        ks, f, G, E, Ek = (T[k] for k in ('ks', 'f', 'G', 'E', 'Ek'))
        R = lambda nm: (nm, par)
        hd = d * 16 + h
        nt = n // 128
        kpb = H['kp'][par]
        vT = H['vT'] if par == 0 else H['sq']
        ktm = H['ktm'] if par == 0 else H['Sp']
        vtm = H['vtm'][:, 4 * par:4 * par + 4, :]
        bT = 4 if par == 0 else 7
        ab = H['abc'][:, 0, par, 0:1]
        nV = 'vT' if par == 0 else ('vT', 1)
        nK = 'ktm' if par == 0 else ('ktm', 1)
        nVt = 'vtm' if par == 0 else ('vtm', 1)
        nP5 = ('ps', 5) if par == 0 else ('ps', 7)
        bV = 6 if par == 0 else 7
        bU_ = 5 if par == 0 else 7
        ACT(lambda e: e.activation(out=ks[:, 0:n], in_=psb[bz][:, 0:n], func=AF.Sigmoid, scale=-1.0), [('ps', bz)], [R('ks')])
        ACT(lambda e: e.activation(out=vT[:, 0:n], in_=psb[bv][:, 0:n], func=AF.Copy), [('ps', bv)], [nV])
        DVE(lambda e: e.tensor_scalar(out=f[:, 0:n], in0=ks[:, 0:n], scalar1=negoml[:, hd:hd + 1], scalar2=1.0,
                                      op0=ALU.mult, op1=ALU.add), [R('ks'), 'negoml'], [R('f')])
        ACT(lambda e: e.activation(out=f[:, 0:n], in_=f[:, 0:n], func=AF.Ln), [R('f')], [R('f')])
        DVE(lambda e: e.tensor_tensor_scan(out=G[:, 0:n], data0=onecol[:, 0:1].to_broadcast([128, n]), data1=f[:, 0:n],
                                           initial=0.0, op0=ALU.mult, op1=ALU.add), [R('f'), 'onecol'], [R('G')])
        if d == 0:
            ACT(lambda e: e.activation(out=Ek[:, 0:n], in_=G[:, 0:n], func=AF.Exp, scale=-1.0, bias=G[:, n - 1:n]),
                [R('G')], [R('Ek')])
        else:
            DVE(lambda e: e.tensor_tensor(out=E[:, 0:n], in0=G[:, 0:n], in1=f[:, 0:n], op=ALU.subtract),
                [R('G'), R('f')], [R('E')])
            ACT(lambda e: e.activation(out=Ek[:, 0:n], in_=E[:, 0:n], func=AF.Exp), [R('E')], [R('Ek')])
        ACT(lambda e: e.activation(out=ab, in_=G[:, n - 1:n], func=AF.Exp), [R('G')], [('abc', par)])
        DVE(lambda e: e.scalar_tensor_tensor(out=kpb[:, 0:n], in0=ks[:, 0:n], scalar=oml[:, hd:hd + 1],
                                             in1=Ek[:, 0:n], op0=ALU.mult, op1=ALU.mult),
            [R('ks'), R('Ek'), 'oml'], [('kp', par)])
        for t in range(nt):
            PE(lambda e, t=t: e.matmul(psb[bT][:, t * 128:(t + 1) * 128], lhsT=kpb[:, t * 128:(t + 1) * 128],
                                       rhs=identb[:], start=True, stop=True), [('kp', par), 'identb'], [('ps', bT)])
        ACT(lambda e: e.activation(out=ktm[:, 0:nt, :], in_=psb[bT][:, 0:nt * 128].rearrange("p (c v) -> p c v", v=128),
                                   func=AF.Copy), [('ps', bT)], [nK])
        for t in range(nt):
            PE(lambda e, t=t: e.matmul(psb[bV][:, t * 128:(t + 1) * 128], lhsT=vT[:, t * 128:(t + 1) * 128],
                                       rhs=identb[:], start=True, stop=True), [nV, 'identb'], [('ps', bV)])
        ACT(lambda e: e.activation(out=vtm[:, 0:nt, :], in_=psb[bV][:, 0:nt * 128].rearrange("p (c v) -> p c v", v=128),
                                   func=AF.Copy), [('ps', bV)], [nVt])
        for t in range(nt):
            PE(lambda e, t=t: e.matmul(psb[bU_][:, 0:128], lhsT=ktm[:, t, :], rhs=vtm[:, t, :],
                                       start=(t == 0), stop=(t == nt - 1)), [nK, nVt], [nP5])
        DVE(lambda e: e.scalar_tensor_tensor(out=S, in0=S, scalar=ab, in1=psb[bU_][:, 0:128],
                                             op0=ALU.mult, op1=ALU.add), [sres, nP5, ('abc', par)], [sres])

    def hgrn_chain(H, d, S, sres, groups, bset):
        abc = H['abc']
        ktm, Sp, Am, tmpU = H['ktm2'][bset], H['Sp2'][bset], H['Am2'][bset], H['tmpU2'][bset]
        bT, bU, bA, bO = (4, 5, 6, 7) if bset == 0 else (0, 1, 2, 3)
        nK = 'ktm' if bset == 0 else ('ktm', 'b')
        nA = 'Am' if bset == 0 else ('Am', 'b')
        nS = (lambda j: ('Sp', j)) if bset == 0 else (lambda j: ('Sp', 'b', j))
        nU = (lambda j: ('tmpU', j)) if bset == 0 else (lambda j: ('tmpU', 'b', j))
        for g0 in groups:
            order = range(4) if d == 0 else range(3, -1, -1)
            for j in range(4):
                c = g0 + j
                PE(lambda e, j=j, c=c: e.matmul(psb[bT][0:64, j * 128:(j + 1) * 128],
                                                lhsT=H['kp'][d][:, c * 64:(c + 1) * 64], rhs=identb[:],
                                                start=True, stop=True), [('kp', d), 'identb'], [('ps', bT)])
            ACT(lambda e: e.activation(out=ktm[0:64, :, :], in_=psb[bT][0:64, :].rearrange("p (c v) -> p c v", v=128),
                                       func=AF.Copy), [('ps', bT)], [nK])
            for j in range(4):
                c = g0 + j
                PE(lambda e, j=j, c=c: e.matmul(psb[bU][:, j * 128:(j + 1) * 128], lhsT=ktm[0:64, j, :],
                                                rhs=H['vtm'][0:64, c, :], start=True, stop=True),
                   [nK, 'vtm'], [('ps', bU)])
            for j in range(4):
                c = g0 + j
                PE(lambda e, j=j, c=c: e.matmul(psb[bA][0:64, j * 64:(j + 1) * 64], lhsT=H['kp'][d][:, c * 64:(c + 1) * 64],
                                                rhs=H['qp'][d][:, c * 64:(c + 1) * 64], start=True, stop=True),
                   [('kp', d), ('qp', d)], [('ps', bA)])
            DVE(lambda e: e.tensor_tensor(out=Am[0:64, 0:256].rearrange("p (c t) -> p c t", t=64),
                                          in0=psb[bA][0:64, 0:256].rearrange("p (c t) -> p c t", t=64),
                                          in1=maskab[:, d * 64:(d + 1) * 64].unsqueeze(1).to_broadcast([64, 4, 64]),
                                          op=ALU.mult), [('ps', bA), 'maskab'], [nA])
            for j in order:
                c = g0 + j
                ACT(lambda e, j=j, c=c: e.activation(out=tmpU[:, j, :], in_=psb[bU][:, j * 128:(j + 1) * 128],
                                                     func=AF.Copy, scale=abc[:, 1, d, c:c + 1]),
                    [('ps', bU), ('abc', d)], [nU(j)])
            for j in order:
                c = g0 + j
                DVE(lambda e, j=j, c=c: e.tensor_scalar(out=Sp[:, j, :], in0=S, scalar1=abc[:, 2, d, c:c + 1],
                                                        scalar2=None, op0=ALU.mult), [sres, ('abc', d)], [nS(j)])
                DVE(lambda e, j=j, c=c: e.scalar_tensor_tensor(out=S, in0=S, scalar=abc[:, 0, d, c:c + 1],
                                                               in1=tmpU[:, j, :], op0=ALU.mult, op1=ALU.add),
                    [sres, nU(j), ('abc', d)], [sres])
            for j in range(4):
                c = g0 + j
                PE(lambda e, j=j, c=c: e.matmul(psb[bO][:, j * 64:(j + 1) * 64], lhsT=Sp[:, j, :],
                                                rhs=H['qp'][d][:, c * 64:(c + 1) * 64], start=True, stop=False),
                   [nS(j), ('qp', d)], [('ps', bO)])
                PE(lambda e, j=j, c=c: e.matmul(psb[bO][:, j * 64:(j + 1) * 64], lhsT=H['vtm'][0:64, c, :],
                                                rhs=Am[0:64, j * 64:(j + 1) * 64], start=False, stop=True),
                   ['vtm', nA], [('ps', bO)])
            oc = H['oacc'][:, g0 * 64:g0 * 64 + 256]
            first = (g0 < 64) if d == 0 else (g0 >= 64)
            first = (g0 < 16) if d == 0 else (g0 >= 16)
            if first:
                DVE(lambda e, oc=oc: e.tensor_copy(out=oc, in_=psb[bO][:, 0:256]), [('ps', bO)], [('oacc', g0)])
            else:
                DVE(lambda e, oc=oc: e.tensor_tensor(out=oc, in0=oc, in1=psb[bO][:, 0:256], op=ALU.add),
                    [('ps', bO), ('oacc', g0)], [('oacc', g0)])

    H = alloc_hgrn()
    seq1 = []
    for h in range({99: 16, 1: DBG_HEADS, 2: 0}[stage]):
        seq1.append((h, xnT_ctx[:], LCTX))
        for tb in range(3, -1, -1):
            seq1.append((h, xnT[:, :, tb * 512:(tb + 1) * 512], 512))
    p1slots = {}

    def p1_units(h):
        if h not in p1slots:
            p1slots[h] = (load_unit(H, h, 2), load_unit(H, h, 3))
            S_ = SinitB[:, h, :]
            DVE(lambda e, S_=S_: e.memset(S_, 0.0), [], ['SB'])
        return p1slots[h]

    for i in range(0, len(seq1), 2):
        lists = []
        for par in range(2):
            if i + par >= len(seq1):
                lists.append([])
                continue
            h, xs_, n_ = seq1[i + par]
            sB, sV = p1_units(h)
            bz = proj(H, sB, xs_, n_)
            bv = proj(H, sV, xs_, n_)
            P.capture()
            state_block(H, h, 1, bz, bv, n_, SinitB[:, h, :], 'SB', par)
            lists.append(P.release())
        P.interleave(lists[0], lists[1])
    P.barrier()
    X = alloc_xt()
    build_xnT_seq(X, [(x_loc[t * 128:(t + 1) * 128, :], xnT[:, :, t * 128:(t + 1) * 128], 'xnT', 0)
                      for t in range(NOWN // 128)])
    P.barrier()

    H = alloc_hgrn()
    NH = {99: 16, 1: DBG_HEADS, 2: 0}[stage]
    def ctx_front(h):
        su_ = [load_unit(H, h, u) for u in range(5)]
        DVE(lambda e: e.memset(H['SA'], 0.0), [], ['SA'])
        bz_ = proj(H, su_[1], xnT_ctx[:], LCTX, bank=0)
        bv_ = proj(H, su_[3], xnT_ctx[:], LCTX, bank=1)
        state_block(H, h, 0, bz_, bv_, LCTX, H['SA'], 'SA')
        return su_

    def readout_blk(h, tb, sq_, rs_, tt_, bk, nsq, nrs, ntt):
        oc = H['oacc'][:, tb * 512:(tb + 1) * 512]
        ocr = [('oacc', tb * 8), ('oacc', tb * 8 + 4)]
        ACT(lambda e: e.activation(out=sq_, in_=oc, func=AF.Square), ocr, [nsq])
        PE(lambda e: e.matmul(psb[bk][:, :], lhsT=onesb[:], rhs=sq_, start=True, stop=True), [nsq, 'onesb'], [('ps', bk)])
        DVE(lambda e: e.tensor_scalar(out=rs_, in0=psb[bk][:, :], scalar1=1.0 / 128, scalar2=EPS,
                                      op0=ALU.mult, op1=ALU.add), [('ps', bk)], [nrs])
        ACT(lambda e: e.activation(out=rs_, in_=rs_, func=AF.Ln), [nrs], [nrs])
        ACT(lambda e: e.activation(out=rs_, in_=rs_, func=AF.Exp, scale=-0.5), [nrs], [nrs])
        DVE(lambda e: e.tensor_tensor(out=tt_, in0=oc, in1=rs_, op=ALU.mult), ocr + [nrs], [ntt])
        DVE(lambda e: e.scalar_tensor_tensor(out=H['gate'][:, tb * 512:(tb + 1) * 512], in0=tt_,
                                             scalar=hgain[:, h:h + 1], in1=H['gate'][:, tb * 512:(tb + 1) * 512],
                                             op0=ALU.mult, op1=ALU.mult), [ntt, 'gate', 'hgain'], ['gate'])

    sus = {}
    for h in range(NH):
        if h == 0:
            sus[0] = ctx_front(0)
        su = sus[h]
        SA = H['SA']
        for tb in range(4):
            if True:
                xs = xnT[:, :, tb * 512:(tb + 1) * 512]
                bza = proj(H, su[1], xs, 512)
                bzb = proj(H, su[2], xs, 512)
                bv = proj(H, su[3], xs, 512)
                bq = proj(H, su[0], xs, 512)
                P.capture()
                hgrn_elem(H, h, 0, bza, tb * 512, 512)
                la = P.release()
                P.capture()
                hgrn_elem(H, h, 1, bzb, tb * 512, 512)
                lb_ = P.release()
                P.capture()
                ACT(lambda e, bv=bv: e.activation(out=H['vT'][:, 0:512], in_=psb[bv][:, 0:512], func=AF.Copy), [('ps', bv)], ['vT'])
                ACT(lambda e, bq=bq: e.activation(out=H['qf'][:, :], in_=psb[bq][:, :], func=AF.Silu), [('ps', bq)], ['qf'])
                lc = P.release()
                P.interleave(la[:3], lb_[:3])
                P.interleave(lc, [])
                bg = proj(H, su[4], xs, 512)
                P.interleave(la[3:], lb_[3:])
                build_vtm(H, None, tb * 512, 512)
                ACT(lambda e, bg=bg, tb=tb: e.activation(out=H['gate'][:, tb * 512:(tb + 1) * 512], in_=psb[bg][:, :], func=AF.Silu),
                    [('ps', bg)], ['gate'])
        P.capture()
        hgrn_chain(H, 0, SA, 'SA', list(range(0, 32, 4)), 0)
        lca = P.release()
        P.capture()
        hgrn_chain(H, 1, SinitB[:, h, :], 'SB', list(range(28, -1, -4)), 1)
        lcb = P.release()
        P.interleave(lca, lcb)
        T1 = H['T'][1]
        setA = (H['sq'][:, :], T1['ks'][:, :], T1['f'][:, :], 7, 'sq', ('ks', 1), ('f', 1))
        setB = (T1['X'].bitcast(BF16)[:, 0:512], T1['G'][:, :], T1['E'][:, :], 3, ('X', 1), ('G', 1), ('E', 1))
        P.capture()
        readout_blk(h, 0, *setA)
        readout_blk(h, 2, *setA)
        lra = P.release()
        P.capture()
        readout_blk(h, 1, *setB)
        readout_blk(h, 3, *setB)
        lrb = P.release()
        lrc = []
        if h + 1 < NH:
            P.capture()
            sus[h + 1] = ctx_front(h + 1)
            lrc = P.release()
        for i_ in range(max(len(lra), len(lrb), len(lrc))):
            for L_ in (lra, lrb, lrc):
                if i_ < len(L_):
                    P.ops.append(L_[i_])
        DMA('sp', lambda e, h=h: e.dma_start(out=hg_spill[h * 128:(h + 1) * 128, :], in_=H['gate'][:, :]),
            ['gate'], ['hg_spill'], 'hgsp')
    P.barrier()

    if stage == 1:
        dbg_hg = dout("dbg_hg", [128 * NH, NOWN], BF16)
        dbg_sb = dout("dbg_sb", [128, 16 * 128])
        DMA('sp', lambda e: e.dma_start(out=dbg_hg, in_=hg_spill[0:128 * NH, :]), ['hg_spill'], ['o1'], 'o1')
        DMA('sp', lambda e: e.dma_start(out=dbg_sb, in_=SinitB[:].rearrange("p h v -> p (h v)")), ['SB'], ['o2'], 'o2')
        P.add('sp', lambda e: None, ['o1', 'o2'], [])
        P.emit(es)
        es.close()
        return nc
    w_att = din("w_att", [4, 9, 128, KC * 128])
    w_v = din("w_v", [128, KC * 512])
    cos_d = din("cosT", [128, NOWN + 128])
    sin_d = din("sinT", [128, NOWN + 128])
    rotT_d = din("rotT", [128, 128])
    sink_d = din("sink", [1, 16])
    bmask_d = din("bmask", [128, 2 * 512])
    att_spill = nc.dram_tensor("att_spill", [D, NOWN], BF16).ap()
    QSCALE = 128.0 ** -0.5
    NKT = NOWN // 128 + 1

    a_reset()
    A = {}
    A['wv_flat'] = a_bf16(KC * 512)
    A['wv'] = A['wv_flat'].rearrange("p (k c) -> p k c", k=KC)
    A['qT_all'] = A['wv_flat']
    A['wh'] = [a_bf16(KC * 128).rearrange("p (k c) -> p k c", k=KC) for _ in range(4)]
    A['V'] = a_bf16(NKT * 512).rearrange("p (t c) -> p t c", c=512)
    A['Vc'] = a_bf16(2 * 512).rearrange("p (t c) -> p t c", c=512)
    A['kT'] = a_bf16(NOWN + 128)
    A['kcT'] = a_bf16(LCTX)
    A['cos'] = a_f32(NOWN + 128)
    A['sin'] = a_f32(NOWN + 128)
    A['rotT'] = a_f32(128)
    A['qf'] = a_f32(512)
    A['t1'] = a_f32(512)
    A['t2'] = a_f32(512)
    A['gT_all'] = a_bf16(4 * 2048)
    A['ao'] = a_bf16(4 * 512)
    A['pT'] = [a_bf16(512) for _ in range(2)]
    A['den'] = a_f32(512)
    A['o1'] = a_f32(512)
    A['sinkrow'] = a_f32(512)
    A['sinkexp'] = a_f32(16)
    A['bmf'] = a_f32(1024)
    A['bm'] = a_bf16(1024)
    DMA('sp', lambda e: e.dma_start(out=A['cos'], in_=cos_d), [], ['cos'], 'p3a')
    DMA('sp', lambda e: e.dma_start(out=A['sin'], in_=sin_d), [], ['sin'], 'p3b')
    DMA('sp', lambda e: e.dma_start(out=A['rotT'], in_=rotT_d), [], ['rotT'], 'p3c')
    DMA('sp', lambda e: e.dma_start(out=A['bmf'], in_=bmask_d), [], ['bmf'], 'p3d')
    DMA('sp', lambda e: e.dma_start(out=A['sinkexp'], in_=sink_d.partition_broadcast(128)), [], ['sinkexp'], 'p3e')
    DMA('pool', lambda e: e.dma_start(out=A['wv'].rearrange("p k c -> p (k c)"), in_=w_v), [], ['wv'], 'p3f')
    ACT(lambda e: e.activation(out=A['sinkexp'], in_=A['sinkexp'], func=AF.Exp), ['sinkexp'], ['sinkexp'])
    DVE(lambda e: e.tensor_copy(out=A['bm'], in_=A['bmf']), ['bmf'], ['bm'])

    awctr = [0]

    def load_att(g, u):
        slot = awctr[0] % 4
        awctr[0] += 1
        DMA('pool', lambda e: e.dma_start(out=A['wh'][slot].rearrange("p k c -> p (k c)"), in_=w_att[g, u]),
            [], [('wh', slot)], 'wh%d' % slot)
        return slot

    apctr = [0]

    def aproj(slot, xsrc, n):
        bnk = apctr[0] % 3
        apctr[0] += 1
        for kc in range(KC):
            PE(lambda e, kc=kc: e.matmul(psb[bnk][:, 0:n], lhsT=A['wh'][slot][:, kc, :], rhs=xsrc[:, kc, :],
                                         start=(kc == 0), stop=(kc == KC - 1)),
               [('wh', slot), 'xnT', 'xnT_ctx', 'xnT_halo'], [('ps', bnk)])
        return bnk

    def vproj(xsrc, dst):
        bnk = apctr[0] % 3
        apctr[0] += 1
        for kc in range(KC):
            PE(lambda e, kc=kc: e.matmul(psb[bnk][:, :], lhsT=xsrc[:, kc, :], rhs=A['wv'][:, kc, :],
                                         start=(kc == 0), stop=(kc == KC - 1)),
               ['wv', 'xnT', 'xnT_ctx', 'xnT_halo'], [('ps', bnk)])
        ACT(lambda e: e.activation(out=dst, in_=psb[bnk][:, :], func=AF.Copy), [('ps', bnk)], ['V'])

    for t in range(NKT):
        src = xnT[:, :, t * 128:(t + 1) * 128] if t < NKT - 1 else xnT_halo[:]
        vproj(src, A['V'][:, t, :])
    for t in range(2):
        vproj(xnT_ctx[:, :, t * 128:(t + 1) * 128], A['Vc'][:, t, :])

    def rope(bnk, n, col0, dst):
        ACT(lambda e: e.activation(out=A['qf'][:, 0:n], in_=psb[bnk][:, 0:n], func=AF.Copy), [('ps', bnk)], ['qf'])
        PE(lambda e: e.matmul(psb[4][:, 0:n], lhsT=A['rotT'], rhs=A['qf'][:, 0:n], start=True, stop=True),
           ['qf', 'rotT'], [('ps', 4)])
        DVE(lambda e: e.tensor_tensor(out=A['t1'][:, 0:n], in0=A['qf'][:, 0:n], in1=A['cos'][:, col0:col0 + n], op=ALU.mult),
            ['qf', 'cos'], ['t1'])
        DVE(lambda e: e.tensor_tensor(out=A['t2'][:, 0:n], in0=psb[4][:, 0:n], in1=A['sin'][:, col0:col0 + n], op=ALU.mult),
            [('ps', 4), 'sin'], ['t2'])
        return ['t1', 't2'], dst

    for g in range(4 if stage != 2 else 1):
        DVE(lambda e, g=g: e.tensor_copy(out=A['sinkrow'].rearrange("p (i t) -> p i t", t=128),
                                         in_=A['sinkexp'][:, 4 * g:4 * g + 4].unsqueeze(2).to_broadcast([128, 4, 128])),
            ['sinkexp'], ['sinkrow'])
        sk = load_att(g, 4)
        for tb in range(5):
            n = 512 if tb < 4 else 128
            src = xnT[:, :, tb * 512:(tb + 1) * 512] if tb < 4 else xnT_halo[:]
            bk = aproj(sk, src, n)
            rope(bk, n, tb * 512, None)
            DVE(lambda e, tb=tb, n=n: e.tensor_tensor(out=A['kT'][:, tb * 512:tb * 512 + n], in0=A['t1'][:, 0:n],
                                                      in1=A['t2'][:, 0:n], op=ALU.add), ['t1', 't2'], ['kT'])
        bk = aproj(sk, xnT_ctx[:], LCTX)
        ACT(lambda e, bk=bk: e.activation(out=A['kcT'], in_=psb[bk][:, 0:LCTX], func=AF.Copy), [('ps', bk)], ['kcT'])
        for i in range(4):
            sq_ = load_att(g, i)
            for tb in range(4):
                xs = xnT[:, :, tb * 512:(tb + 1) * 512]
                qT4 = A['qT_all'][:, tb * 2048:(tb + 1) * 2048].rearrange("p (q i t) -> p q i t", q=4, i=4)
                bq = aproj(sq_, xs, 512)
                rope(bq, 512, tb * 512, None)
                DVE(lambda e, i=i, qT4=qT4: e.tensor_tensor(out=qT4[:, :, i, :], in0=A['t1'].rearrange("p (q t) -> p q t", t=128),
                                                            in1=A['t2'].rearrange("p (q t) -> p q t", t=128), op=ALU.add),
                    ['t1', 't2'], [('qT', tb), 'wv'])
        for i in range(4):
            sg = load_att(g, 5 + i)
            for tb in range(4):
                xs = xnT[:, :, tb * 512:(tb + 1) * 512]
                gT4 = A['gT_all'][:, tb * 2048:(tb + 1) * 2048].rearrange("p (q i t) -> p q i t", q=4, i=4)
                bg = aproj(sg, xs, 512)
                ACT(lambda e, i=i, bg=bg, gT4=gT4: e.activation(out=gT4[:, :, i, :], in_=psb[bg][:, :].rearrange("p (q t) -> p q t", t=128),
                                                                func=AF.Silu), [('ps', bg)], [('gT', tb)])

        def attn(tb):
            ao4 = A['ao'].rearrange("p (q i t) -> p q i t", q=4, i=4)
            for qb in range(4):
                Q = tb * 4 + qb
                kbs = []
                if Q >= 1:
                    kbs.append((A['kT'][:, (Q - 1) * 128:Q * 128], A['V'][:, Q - 1, g * 128:(g + 1) * 128], 0))
                kbs.append((A['kT'][:, Q * 128:(Q + 1) * 128], A['V'][:, Q, g * 128:(g + 1) * 128], None))
                kbs.append((A['kT'][:, (Q + 1) * 128:(Q + 2) * 128], A['V'][:, Q + 1, g * 128:(g + 1) * 128], 1))
                for t in range(2):
                    kbs.append((A['kcT'][:, t * 128:(t + 1) * 128], A['Vc'][:, t, g * 128:(g + 1) * 128], None))
                qrhs = A['qT_all'][:, tb * 2048 + qb * 512:tb * 2048 + (qb + 1) * 512]
                bO = 7 if Q % 2 == 0 else 0
                bD = 3 if Q % 2 == 0 else 1
                nk = len(kbs)

                def emit_S(ki):
                    kap, vap, mk = kbs[ki]
                    sl = (Q * 5 + ki) % 2
                    PE(lambda e, kap=kap, sl=sl, mk=mk, qrhs=qrhs: e.matmul(psb[5 + sl][:, :], lhsT=kap, rhs=qrhs, start=True, stop=(mk is None)),
                       ['kT', 'kcT', ('qT', tb)], [('ps', 5 + sl)])
                    if mk is not None:
                        PE(lambda e, sl=sl, mk=mk: e.matmul(psb[5 + sl][:, :], lhsT=identb[:], rhs=A['bm'][:, mk * 512:(mk + 1) * 512],
                                                            start=False, stop=True), ['identb', 'bm'], [('ps', 5 + sl)])

                emit_S(0)
                for ki in range(nk):
                    kap, vap, mk = kbs[ki]
                    sl = (Q * 5 + ki) % 2
                    if ki + 1 < nk:
                        emit_S(ki + 1)
                    ACT(lambda e, sl=sl: e.activation(out=A['pT'][sl], in_=psb[5 + sl][:, :], func=AF.Exp, scale=QSCALE),
                        [('ps', 5 + sl)], [('pT', sl)])
                    first, last = ki == 0, ki == nk - 1
                    PE(lambda e, vap=vap, sl=sl, first=first, last=last, bO=bO: e.matmul(psb[bO][:, :], lhsT=vap, rhs=A['pT'][sl],
                                                                                   start=first, stop=last),
                       ['V', ('pT', sl)], [('ps', bO)])
                    PE(lambda e, sl=sl, first=first, last=last, bD=bD: e.matmul(psb[bD][:, :], lhsT=onesb[:], rhs=A['pT'][sl],
                                                                          start=first, stop=last),
                       ['onesb', ('pT', sl)], [('ps', bD)])
                DVE(lambda e, bD=bD: e.tensor_tensor(out=A['den'], in0=psb[bD][:, :], in1=A['sinkrow'], op=ALU.add),
                    [('ps', bD), 'sinkrow'], ['den'])
                ACT(lambda e: e.activation(out=A['den'], in_=A['den'], func=AF.Ln), ['den'], ['den'])
                ACT(lambda e: e.activation(out=A['den'], in_=A['den'], func=AF.Exp, scale=-1.0), ['den'], ['den'])
                DVE(lambda e, bO=bO: e.tensor_tensor(out=A['o1'], in0=psb[bO][:, :], in1=A['den'], op=ALU.mult),
                    [('ps', bO), 'den'], ['o1'])
                DVE(lambda e, qb=qb: e.tensor_tensor(out=A['ao'][:, qb * 512:(qb + 1) * 512], in0=A['o1'],
                                                     in1=A['gT_all'][:, tb * 2048 + qb * 512:tb * 2048 + (qb + 1) * 512], op=ALU.mult),
                    ['o1', ('gT', tb)], ['ao'])
            for i in range(4):
                hh = 4 * g + i
                DMA('sp', lambda e, hh=hh, i=i, tb=tb: e.dma_start(
                    out=att_spill[hh * 128:(hh + 1) * 128, tb * 512:(tb + 1) * 512].rearrange("p (q t) -> p q t", t=128),
                    in_=ao4[:, :, i, :]), ['ao'], ['att_spill'], 'atsp')

        for tb in range(4):
            attn(tb)
    P.barrier()

    if stage == 2:
        dbg_at = dout("dbg_at", [512, NOWN], BF16)
        DMA('sp', lambda e: e.dma_start(out=dbg_at, in_=att_spill[0:512, :]), ['att_spill'], ['o1'], 'o1')
        P.add('sp', lambda e: None, ['o1'], [])
        P.emit(es)
        es.close()
        return nc

    w_fm = din("w_fm", [16, 4, 128, KC * 128])
    w_outd = din("w_out_l", [8, 128, KC * 256])
    bm_d = din("bmerge_fm", [128, 32])
    fg_d = din("fgain", [1, D])
    y_out = dout("y", [NOWN, D])
    TB4 = 512
    NB4 = NOWN // TB4
    NT4 = TB4 // 128

    a_reset()
    F = {}
    F['wh'] = [a_bf16(KC * 128).rearrange("p (k c) -> p k c", k=KC) for _ in range(3)]
    F['wh'].append(xnT_halo[:])
    F['wo'] = [a_bf16(KC * 256).rearrange("p (k c) -> p k c", k=KC) for _ in range(2)]
    F['hgT'] = a_bf16(KC * TB4).rearrange("p (k t) -> p k t", k=KC)
    F['atT'] = a_bf16(KC * TB4).rearrange("p (k t) -> p k t", k=KC)
    F['yT'] = a_bf16(KC * TB4).rearrange("p (k t) -> p k t", k=KC)
    F['hrow'] = [a_f32(D), a_f32(D), SinitB[:].rearrange("p h v -> p (h v)"),
                 xnT_ctx[:].rearrange("p k t -> p (k t)").bitcast(F32)]
    F['gate'] = a_f32(D)
    F['fg'] = a_f32(D)
    F['sa'] = a_f32(TB4)
    F['sb'] = a_f32(TB4)
    F['tmp'] = a_f32(512)
    F['junk'] = F['hgT'].rearrange("p k t -> p (k t)")[:, 0:D]
    F['ssq'] = [a_f32(1) for _ in range(4)]
    F['rstd'] = [a_f32(1) for _ in range(4)]
    F['bm'] = a_f32(32)
    DMA('sp', lambda e: e.dma_start(out=F['gate'], in_=gate_dram.partition_broadcast(128)), ['gate_dram'], ['gate_bc'], 'p4a')
    DMA('sp', lambda e: e.dma_start(out=F['fg'], in_=fg_d.partition_broadcast(128)), [], ['fg'], 'p4b')
    DMA('sp', lambda e: e.dma_start(out=F['bm'], in_=bm_d), [], ['bmf4'], 'p4c')

    fwctr = [0]
    fpctr = [0]
    owctr = [0]
    for tb in range(NB4):
        t0 = tb * TB4
        DMA('sp', lambda e, t0=t0: e.dma_start(out=F['hgT'], in_=hg_spill[:, t0:t0 + TB4].rearrange("(k p) t -> p k t", p=128)),
            ['hg_spill'], ['hgT'], 'p4h')
        DMA('act', lambda e, t0=t0: e.dma_start(out=F['atT'], in_=att_spill[:, t0:t0 + TB4].rearrange("(k p) t -> p k t", p=128)),
            ['att_spill'], ['atT'], 'p4t')
        xs = xnT[:, :, t0:t0 + TB4]
        for c in range(16):
            bnks = []
            for u, src, sres in ((0, F['hgT'], 'hgT'), (1, xs, 'xnT'), (2, F['atT'], 'atT'), (3, xs, 'xnT')):
                slot = fwctr[0] % 4
                fwctr[0] += 1
                DMA('pool', lambda e, slot=slot, c=c, u=u: e.dma_start(out=F['wh'][slot].rearrange("p k c -> p (k c)"),
                                                                       in_=w_fm[c, u]), [], [('wh', slot)], 'wh%d' % slot)
                bnk = fpctr[0] % 4
                fpctr[0] += 1
                for kc in range(KC):
                    PE(lambda e, kc=kc, slot=slot, bnk=bnk, src=src: e.matmul(psb[bnk][:, 0:TB4], lhsT=F['wh'][slot][:, kc, :],
                                                                              rhs=src[:, kc, :], start=(kc == 0), stop=(kc == KC - 1)),
                       [('wh', slot), sres], [('ps', bnk)])
                bnks.append(bnk)
            ACT(lambda e, b=bnks[1], c=c: e.activation(out=F['sa'], in_=psb[b][:, 0:TB4], func=AF.Sigmoid, bias=F['bm'][:, c:c + 1]),
                [('ps', bnks[1]), 'bmf4'], ['sa'])
            ACT(lambda e, b=bnks[3], c=c: e.activation(out=F['sb'], in_=psb[b][:, 0:TB4], func=AF.Sigmoid, bias=F['bm'][:, 16 + c:17 + c]),
                [('ps', bnks[3]), 'bmf4'], ['sb'])
            DVE(lambda e, b=bnks[0]: e.tensor_tensor(out=F['sa'], in0=psb[b][:, 0:TB4], in1=F['sa'], op=ALU.mult),
                [('ps', bnks[0]), 'sa'], ['sa'])
            DVE(lambda e, b=bnks[2]: e.tensor_tensor(out=F['sb'], in0=psb[b][:, 0:TB4], in1=F['sb'], op=ALU.mult),
                [('ps', bnks[2]), 'sb'], ['sb'])
            DVE(lambda e, c=c: e.tensor_tensor(out=F['yT'][:, c, :], in0=F['sa'], in1=F['sb'], op=ALU.add), ['sa', 'sb'], ['yT'])
        for tt in range(4):
            DMA('sp' if tt % 2 == 0 else 'act',
                lambda e, tt=tt, t0=t0: e.dma_start(out=F['hrow'][tt], in_=x_loc[t0 + tt * 128:t0 + (tt + 1) * 128, :]),
                [], [('hrow', tt)], 'p4x%d' % tt)
        for cb in range(8):
            os_ = owctr[0] % 2
            owctr[0] += 1
            DMA('pool', lambda e, os_=os_, cb=cb: e.dma_start(out=F['wo'][os_].rearrange("p k c -> p (k c)"), in_=w_outd[cb]),
                [], [('wo', os_)], 'wo%d' % os_)
            for tt in range(4):
                bnk = 4 + (cb * 4 + tt) % 4
                for kc in range(KC):
                    PE(lambda e, kc=kc, tt=tt, os_=os_, bnk=bnk: e.matmul(psb[bnk][:, 0:256], lhsT=F['yT'][:, kc, tt * 128:(tt + 1) * 128],
                                                                          rhs=F['wo'][os_][:, kc, :], start=(kc == 0), stop=(kc == KC - 1)),
                       ['yT', ('wo', os_)], [('ps', bnk)])
                DVE(lambda e, bnk=bnk, cb=cb: e.tensor_tensor(out=F['tmp'][:, 0:256], in0=psb[bnk][:, 0:256], in1=F['gate'][:, cb * 256:(cb + 1) * 256],
                                                              op=ALU.mult), [('ps', bnk), 'gate_bc'], ['tmp4'])
                DVE(lambda e, tt=tt, cb=cb: e.tensor_tensor(out=F['hrow'][tt][:, cb * 256:(cb + 1) * 256],
                                                            in0=F['hrow'][tt][:, cb * 256:(cb + 1) * 256], in1=F['tmp'][:, 0:256], op=ALU.add),
                    ['tmp4', ('hrow', tt)], [('hrow', tt)])
        for tt in range(4):
            hr = F['hrow'][tt]
            DVE(lambda e, tt=tt: e.memset(F['ssq'][tt], 0.0), [], [('ssq4', tt)])
            ACT(lambda e, tt=tt, hr=hr: e.activation(out=F['junk'], in_=hr, func=AF.Square, accum_out=F['ssq'][tt]),
                [('hrow', tt), ('ssq4', tt)], [('ssq4', tt), 'hgT'])
            DVE(lambda e, tt=tt: e.tensor_scalar(out=F['rstd'][tt], in0=F['ssq'][tt], scalar1=1.0 / D, scalar2=EPS,
                                                 op0=ALU.mult, op1=ALU.add), [('ssq4', tt)], [('rstd4', tt)])
            ACT(lambda e, tt=tt: e.sqrt(out=F['rstd'][tt], in_=F['rstd'][tt]), [('rstd4', tt)], [('rstd4', tt)])
            DVE(lambda e, tt=tt: e.reciprocal(out=F['rstd'][tt], in_=F['rstd'][tt]), [('rstd4', tt)], [('rstd4', tt)])
            DVE(lambda e, tt=tt, hr=hr: e.scalar_tensor_tensor(out=hr, in0=hr, scalar=F['rstd'][tt][:, 0:1], in1=F['fg'],
                                                               op0=ALU.mult, op1=ALU.mult),
                [('hrow', tt), ('rstd4', tt), 'fg'], [('hrow', tt)])
            DMA('sp', lambda e, tt=tt, hr=hr, t0=t0: e.dma_start(out=y_out[t0 + tt * 128:t0 + (tt + 1) * 128, :], in_=hr),
                [('hrow', tt)], [('y_out', tt)], 'p4y%d' % tt)
    P.add('sp', lambda e: None, [('y_out', tt) for tt in range(4)], [])

    P.emit(es)
    es.close()
    return nc


def prep_inputs(inp):
    f = np.float32
    x = np.asarray(inp['x'], f)
    ctx = np.asarray(inp['ctx'], f)
    c = np.asarray(inp['c'], f)
    c_ctx = np.asarray(inp['c_ctx'], f)
    w_ada = np.asarray(inp['w_ada'], f)[0]
    b_ada = np.asarray(inp['b_ada'], f)[0]
    w_ada_l = np.ascontiguousarray(w_ada.reshape(KC, 128, 3 * D).transpose(1, 0, 2))
    b_ada2 = np.ascontiguousarray(np.stack([b_ada, b_ada], 0))
    gain_fm = np.ascontiguousarray(np.asarray(inp['norm_gain'], f)[0].reshape(KC, 128).T)
    ident = np.eye(128, dtype=f)
    sel = np.zeros((2, 128), f)
    sel[0, :] = 1.0
    w_in = np.asarray(inp['w_in'], f)[0]
    OFF_HG_Q, OFF_HG_FF, OFF_HG_FB, OFF_HG_I, OFF_HG_G = 5120, 7168, 9216, 11264, 13312

    def unit(col0, ncol=128):
        return w_in[:, col0:col0 + ncol].reshape(KC, 128, ncol).transpose(1, 0, 2).reshape(128, KC * ncol)

    def fm(v):
        return np.ascontiguousarray(v.reshape(16, 128).T)

    w_h_s = []
    lbl_s = []
    lf = np.asarray(inp['lb_logits_fwd'], f)
    lbk = np.asarray(inp['lb_logits_bwd'], f)
    for s in range(2):
        offA, offB = (OFF_HG_FF, OFF_HG_FB) if s == 0 else (OFF_HG_FB, OFF_HG_FF)
        wh = np.empty((16, 5, 128, KC * 128), f)
        for h in range(16):
            for u, off in enumerate((OFF_HG_Q, offA, offB, OFF_HG_I, OFF_HG_G)):
                wh[h, u] = unit(off + h * 128)
        w_h_s.append(wh)
        la, lb_ = (lf, lbk) if s == 0 else (lbk, lf)
        arr = np.stack([np.stack([fm(la[0]), fm(la[1])], 1), np.stack([fm(lb_[0]), fm(lb_[1])], 1)], 1)
        lbl_s.append(np.ascontiguousarray(arr.reshape(128, 64)))
    hgain_fm = fm(np.asarray(inp['hgrn_norm_gain'], f)[0])
    mreset = np.ones((128, 512), f)
    mreset[:, ::64] = 0.0
    ss, tt = np.meshgrid(np.arange(64), np.arange(64), indexing='ij')
    maskab = np.concatenate([(ss <= tt).astype(f), (ss >= tt).astype(f)], 1)
    OFF_ATT_Q, OFF_ATT_K, OFF_ATT_V, OFF_ATT_G, OFF_MERGE = 0, 2048, 2560, 3072, 15360
    w_att = np.empty((4, 9, 128, KC * 128), f)
    for g in range(4):
        for u in range(4):
            w_att[g, u] = unit(OFF_ATT_Q + (4 * g + u) * 128)
            w_att[g, 5 + u] = unit(OFF_ATT_G + (4 * g + u) * 128)
        w_att[g, 4] = unit(OFF_ATT_K + g * 128)
    w_v = np.ascontiguousarray(unit(OFF_ATT_V, 512))
    w_o_hgrn = np.asarray(inp['w_o_hgrn'], f)[0]
    w_o_attn = np.asarray(inp['w_o_attn'], f)[0]
    w_out = np.asarray(inp['w_out'], f)[0]

    def unit_of(w, col0, ncol=128):
        return w[:, col0:col0 + ncol].reshape(KC, 128, ncol).transpose(1, 0, 2).reshape(128, KC * ncol)

    w_fm = np.empty((16, 4, 128, KC * 128), f)
    for cc in range(16):
        w_fm[cc, 0] = unit_of(w_o_hgrn, cc * 128)
        w_fm[cc, 1] = unit(OFF_MERGE + cc * 128)
        w_fm[cc, 2] = unit_of(w_o_attn, cc * 128)
        w_fm[cc, 3] = unit(OFF_MERGE + D + cc * 128)
    w_out_l = np.stack([unit_of(w_out, cb * 256, 256) for cb in range(8)], 0)
    bmg = np.asarray(inp['b_merge'], f)[0]
    bmerge_fm = np.ascontiguousarray(np.concatenate([fm(bmg[0]), fm(bmg[1])], 1))
    fgain = np.ascontiguousarray(np.asarray(inp['final_norm_gain'], f).reshape(1, D))
    sink = np.ascontiguousarray(np.asarray(inp['sink_logits'], f).reshape(1, 16))
    rotT = np.zeros((128, 128), f)
    for dq in range(128):
        if (dq // 32) % 2 == 0:
            rotT[dq + 32, dq] = -1.0
        else:
            rotT[dq - 32, dq] = 1.0
    aa, bb = np.meshgrid(np.arange(128), np.arange(128), indexing='ij')
    bm0 = np.tile(np.where(bb <= aa, 0.0, -30000.0).astype(f), (1, 4))
    bm1 = np.tile(np.where(aa <= bb, 0.0, -30000.0).astype(f), (1, 4))
    bmask = np.ascontiguousarray(np.concatenate([bm0, bm1], 1))
    inv_freq = (1.0 / (np.float32(10000.0) ** (np.arange(0, 64, 2, dtype=f) / np.float32(64)))).astype(f)
    rope_tabs = []
    for s in range(2):
        ii = np.arange(NOWN + 128)
        jj = ii if s == 0 else 4095 - ii
        row = (jj // 64).astype(f)
        colp = (jj % 64).astype(f)
        ang_r = row[:, None] * inv_freq[None, :]
        ang_c = colp[:, None] * inv_freq[None, :]
        ang = np.concatenate([ang_r, ang_r, ang_c, ang_c], -1).astype(f)
        rope_tabs.append((np.ascontiguousarray(np.cos(ang).T.astype(f)), np.ascontiguousarray(np.sin(ang).T.astype(f))))
    maps = []
    for core in range(8):
        b, s = core // 2, core % 2
        xb = x[b]
        cb = ctx[b]
        if s == 1:
            xb = xb[::-1]
            cb = cb[::-1]
        cv = np.stack([c[b].reshape(KC, 128).T, c_ctx.reshape(KC, 128).T], -1).reshape(128, KC * 2)
        maps.append(dict(
            x_loc=np.ascontiguousarray(xb), ctx_loc=np.ascontiguousarray(cb),
            cvec=np.ascontiguousarray(cv), w_ada_l=w_ada_l, b_ada2=b_ada2, gain_fm=gain_fm,
            ident=ident, sel=sel, w_h=w_h_s[s], lbl=lbl_s[s], hgain_fm=hgain_fm, mreset=mreset, maskab=maskab,
            w_att=w_att, w_v=w_v, cosT=rope_tabs[s][0], sinT=rope_tabs[s][1], rotT=rotT, sink=sink, bmask=bmask,
            w_fm=w_fm, w_out_l=w_out_l, bmerge_fm=bmerge_fm, fgain=fgain))
    return maps


def kernel(**inputs):
    maps = prep_inputs(inputs)
    nc = build()
    res = run_bass_kernel_spmd(nc, maps, core_ids=list(range(8)))
    out = np.zeros((4, 4096, D), np.float32)
    for core in range(8):
        b, s = core // 2, core % 2
        y = res.results[core]["y"]
        if s == 0:
            out[b, :NOWN] = y
        else:
            out[b, NOWN:] = y[::-1]
    return out
```
